# Optimizing a Trainium2 kernel written in Bass

```python
import jax, jax.numpy as jnp
from jax import lax
import numpy as np

D_MODEL = 1024
BATCH = 2
SEQ = 8192
DEPTH = 1
DEC_BATCH = 16
DEC_SEQ = 64
PAST_LEN = 1024

CHUNK = 64
Q_BLOCK = 128
HEAD_DIM = 64
N_SB_HEADS = 8
N_FOX_HEADS = 8
SB_WIDTH = N_SB_HEADS * HEAD_DIM
FOX_WIDTH = N_FOX_HEADS * HEAD_DIM
MIX_WIDTH = SB_WIDTH + FOX_WIDTH
IN_COLS = 3 * SB_WIDTH + 3 * FOX_WIDTH + N_FOX_HEADS
D_FF = -(-8 * D_MODEL // (3 * 256)) * 256
EPS = 1e-6
SPLITS = [SB_WIDTH, 2 * SB_WIDTH, 3 * SB_WIDTH,
          3 * SB_WIDTH + FOX_WIDTH, 3 * SB_WIDTH + 2 * FOX_WIDTH, 3 * SB_WIDTH + 3 * FOX_WIDTH]

kernel_name = "stickbreak_forgetting_hybrid_stream_step"


def rmsnorm(x, g):
    xf = x.astype(jnp.float32)
    y = xf * lax.rsqrt(jnp.mean(xf * xf, axis=-1, keepdims=True) + EPS)
    return (y * g.astype(jnp.float32)).astype(x.dtype)


def ada_terms(c, w_ada, b_ada):
    a = jax.nn.silu(c) @ w_ada + b_ada
    return [t[:, None, :] for t in jnp.split(a, 6, axis=-1)]


def sb_attend(q, k, v, q_pos, k_pos):
    z = jnp.einsum('bqhd,bkhd->bhqk', q.astype(jnp.float32), k.astype(jnp.float32)) * (HEAD_DIM ** -0.5)
    mask = k_pos[None, :] < q_pos[:, None]
    log_keep = jnp.where(mask, jax.nn.log_sigmoid(-z), 0.0)
    between = lax.cumsum(log_keep, axis=3, reverse=True) - log_keep
    a = jnp.where(mask, jnp.exp(jax.nn.log_sigmoid(z) + between), 0.0)
    return jnp.einsum('bhqk,bkhd->bqhd', a, v.astype(jnp.float32)).astype(v.dtype)


def fox_attend(q, k, v, fq, fk, q_pos, k_pos):
    s = jnp.einsum('bqhd,bkhd->bhqk', q.astype(jnp.float32), k.astype(jnp.float32)) * (HEAD_DIM ** -0.5)
    s = s + jnp.transpose(fq, (0, 2, 1))[:, :, :, None] - jnp.transpose(fk, (0, 2, 1))[:, :, None, :]
    mask = k_pos[None, :] <= q_pos[:, None]
    p = jax.nn.softmax(jnp.where(mask, s, -jnp.inf), axis=-1)
    return jnp.einsum('bhqk,bkhd->bqhd', p, v.astype(jnp.float32)).astype(v.dtype)


def to_blocks(a):
    b, t = a.shape[0], a.shape[1]
    return jnp.swapaxes(a.reshape((b, t // Q_BLOCK, Q_BLOCK) + a.shape[2:]), 0, 1)


def from_blocks(a):
    nb, b = a.shape[0], a.shape[1]
    return jnp.swapaxes(a, 0, 1).reshape((b, nb * Q_BLOCK) + a.shape[3:])


def sb_prompt(q, k, v):
    pos = jnp.arange(q.shape[1], dtype=jnp.int32)
    out = lax.map(lambda a: sb_attend(a[0], k, v, a[1], pos), (to_blocks(q), pos.reshape(-1, Q_BLOCK)))
    return from_blocks(out)


def fox_prompt(q, k, v, f):
    pos = jnp.arange(q.shape[1], dtype=jnp.int32)
    out = lax.map(lambda a: fox_attend(a[0], k, v, a[1], f, a[2], pos),
                  (to_blocks(q), to_blocks(f), pos.reshape(-1, Q_BLOCK)))
    return from_blocks(out)


def pre_mixer(x, shift, scale, g_mix, w_in, b_f):
    b, t, _ = x.shape
    h = rmsnorm(x, g_mix) * (1.0 + scale) + shift
    parts = jnp.split(h @ w_in, SPLITS, axis=-1)
    q_sb, k_sb, v_sb = [p.reshape(b, t, N_SB_HEADS, HEAD_DIM) for p in parts[0:3]]
    q_fx, k_fx, v_fx = [p.reshape(b, t, N_FOX_HEADS, HEAD_DIM) for p in parts[3:6]]
    logf = jax.nn.log_sigmoid((parts[6] + b_f).astype(jnp.float32))
    return q_sb, k_sb, v_sb, q_fx, k_fx, v_fx, logf


def post_mixer(x, o_sb, o_fx, ada, g_sb_out, g_fox_out, w_o, g_ffn, w_gate, w_up, w_down):
    b, t, _ = x.shape
    o = jnp.concatenate([rmsnorm(o_sb.reshape(b, t, SB_WIDTH), g_sb_out),
                         rmsnorm(o_fx.reshape(b, t, FOX_WIDTH), g_fox_out)], axis=-1)
    x = x + (1.0 + ada[2]) * (o @ w_o)
    h = rmsnorm(x, g_ffn) * (1.0 + ada[4]) + ada[3]
    f = (jax.nn.silu(h @ w_gate) * (h @ w_up)) @ w_down
    return x + (1.0 + ada[5]) * f


def setup_inputs(seed: int = 0) -> dict:
    key = jax.random.key(seed)
    ks = jax.random.split(key, 24)
    f32 = jnp.float32
    nrm = lambda k, shape, s=1.0: (jax.random.normal(k, shape, f32) * s)
    return {
        "x_prompt": nrm(ks[0], (BATCH, SEQ, D_MODEL)),
        "x_sample": nrm(ks[1], (DEC_BATCH, DEC_SEQ, D_MODEL)),
        "c_prompt": nrm(ks[2], (BATCH, D_MODEL)),
        "c_sample": nrm(ks[3], (DEC_BATCH, D_MODEL)),
        "cache_sb_k": nrm(ks[4], (DEPTH, DEC_BATCH, PAST_LEN, N_SB_HEADS, HEAD_DIM)),
        "cache_sb_v": nrm(ks[5], (DEPTH, DEC_BATCH, PAST_LEN, N_SB_HEADS, HEAD_DIM)),
        "cache_fox_k": nrm(ks[6], (DEPTH, DEC_BATCH, PAST_LEN, N_FOX_HEADS, HEAD_DIM)),
        "cache_fox_v": nrm(ks[7], (DEPTH, DEC_BATCH, PAST_LEN, N_FOX_HEADS, HEAD_DIM)),
        "cache_fox_logf": jax.nn.log_sigmoid(nrm(ks[8], (DEPTH, DEC_BATCH, PAST_LEN, N_FOX_HEADS)) + 3.0),
        "w_ada": nrm(ks[9], (DEPTH, D_MODEL, 6 * D_MODEL), 0.1 * D_MODEL ** -0.5),
        "b_ada": nrm(ks[10], (DEPTH, 6 * D_MODEL), 0.01),
        "g_mix": 1.0 + nrm(ks[11], (DEPTH, D_MODEL), 0.05),
        "w_in": nrm(ks[12], (DEPTH, D_MODEL, IN_COLS), D_MODEL ** -0.5),
        "b_f": 1.0 + 3.0 * jax.random.uniform(ks[13], (DEPTH, N_FOX_HEADS), f32),
        "g_sb_out": 1.0 + nrm(ks[14], (DEPTH, SB_WIDTH), 0.05),
        "g_fox_out": 1.0 + nrm(ks[15], (DEPTH, FOX_WIDTH), 0.05),
        "w_o": nrm(ks[16], (DEPTH, MIX_WIDTH, D_MODEL), MIX_WIDTH ** -0.5),
        "g_ffn": 1.0 + nrm(ks[17], (DEPTH, D_MODEL), 0.05),
        "w_gate": nrm(ks[18], (DEPTH, D_MODEL, D_FF), D_MODEL ** -0.5),
        "w_up": nrm(ks[19], (DEPTH, D_MODEL, D_FF), D_MODEL ** -0.5),
        "w_down": nrm(ks[20], (DEPTH, D_FF, D_MODEL), D_FF ** -0.5),
        "g_final": 1.0 + nrm(ks[21], (D_MODEL,), 0.05),
    }


def reference(x_prompt, x_sample, c_prompt, c_sample, cache_sb_k, cache_sb_v, cache_fox_k,
              cache_fox_v, cache_fox_logf, w_ada, b_ada, g_mix, w_in, b_f, g_sb_out, g_fox_out,
              w_o, g_ffn, w_gate, w_up, w_down, g_final):
    xp, xs = x_prompt, x_sample
    n_new = xs.shape[1]
    past = cache_sb_k.shape[2]
    q_pos_s = past + jnp.arange(n_new, dtype=jnp.int32)
    k_pos_s = jnp.arange(past + n_new, dtype=jnp.int32)
    sbk_p, sbv_p, fxk_p, fxv_p, lf_p = [], [], [], [], []
    sbk_s, sbv_s, fxk_s, fxv_s, lf_s = [], [], [], [], []
    for l in range(DEPTH):
        ada_p = ada_terms(c_prompt, w_ada[l], b_ada[l])
        q_sb, k_sb, v_sb, q_fx, k_fx, v_fx, logf = pre_mixer(xp, ada_p[0], ada_p[1], g_mix[l], w_in[l], b_f[l])
        o_sb = sb_prompt(q_sb, k_sb, v_sb)
        f_cum = jnp.cumsum(logf, axis=1)
        o_fx = fox_prompt(q_fx, k_fx, v_fx, f_cum)
        xp = post_mixer(xp, o_sb, o_fx, ada_p, g_sb_out[l], g_fox_out[l], w_o[l], g_ffn[l],
                        w_gate[l], w_up[l], w_down[l])
        sbk_p.append(k_sb); sbv_p.append(v_sb); fxk_p.append(k_fx); fxv_p.append(v_fx)
        lf_p.append(logf.astype(x_prompt.dtype))
        ada_s = ada_terms(c_sample, w_ada[l], b_ada[l])
        q_sb, k_sb, v_sb, q_fx, k_fx, v_fx, logf = pre_mixer(xs, ada_s[0], ada_s[1], g_mix[l], w_in[l], b_f[l])
        k_all = jnp.concatenate([cache_sb_k[l], k_sb], axis=1)
        v_all = jnp.concatenate([cache_sb_v[l], v_sb], axis=1)
        o_sb = sb_attend(q_sb, k_all, v_all, q_pos_s, k_pos_s)
        f_all = jnp.cumsum(jnp.concatenate([cache_fox_logf[l].astype(jnp.float32), logf], axis=1), axis=1)
        kf_all = jnp.concatenate([cache_fox_k[l], k_fx], axis=1)
        vf_all = jnp.concatenate([cache_fox_v[l], v_fx], axis=1)
        o_fx = fox_attend(q_fx, kf_all, vf_all, f_all[:, past:], f_all, q_pos_s, k_pos_s)
        xs = post_mixer(xs, o_sb, o_fx, ada_s, g_sb_out[l], g_fox_out[l], w_o[l], g_ffn[l],
                        w_gate[l], w_up[l], w_down[l])
        sbk_s.append(k_sb); sbv_s.append(v_sb); fxk_s.append(k_fx); fxv_s.append(v_fx)
        lf_s.append(logf.astype(x_sample.dtype))
    y_prompt = rmsnorm(xp, g_final)
    y_sample = rmsnorm(xs, g_final)
    return (y_prompt, y_sample,
            jnp.stack(sbk_p), jnp.stack(sbv_p), jnp.stack(fxk_p), jnp.stack(fxv_p), jnp.stack(lf_p),
            jnp.stack(sbk_s), jnp.stack(sbv_s), jnp.stack(fxk_s), jnp.stack(fxv_s), jnp.stack(lf_s))
```

```python
import numpy as np
import ml_dtypes
from contextlib import ExitStack
import concourse.bass as bass
import concourse.mybir as mybir
from concourse.bass_utils import run_bass_kernel_spmd

F32 = mybir.dt.float32
BF16 = mybir.dt.bfloat16
U8 = mybir.dt.uint8
AF = mybir.ActivationFunctionType
ALU = mybir.AluOpType
EPS = 1e-6
NEG = -30000.0
ENGS = ("pe", "act", "dve", "pool", "sp")
MAXV = 30000

CFG_FULL = dict(D=1024, NH=8, T=8192, DFF=2816, PAST=1024, DT=64)


class Res:
    __slots__ = ("name", "w", "rs", "cw", "stg")

    def __init__(self, name, cw=False, stg=False):
        self.name = name
        self.w = {}
        self.rs = {}
        self.cw = cw
        self.stg = stg


DMA_SLOTS = {"sp": 4, "pool": 2}


class Prog:
    def __init__(self, nc, es):
        self.nc, self.es = nc, es
        self.q = {e: [] for e in ENGS}
        self.sem = {}
        self.cnt = {}
        self.seen = {e: {} for e in ENGS}
        self.nsem = 0
        for e in ENGS:
            self._rot(e)
        self.slots = {e: [[self._newsem(f"d{e}{k}"), 0] for k in range(n)] for e, n in DMA_SLOTS.items()}
        self.dn = {e: 0 for e in DMA_SLOTS}

    def _newsem(self, nm):
        self.nsem += 1
        return self.es.enter_context(self.nc.semaphore(f"{nm}_{self.nsem}"))

    def _rot(self, e):
        self.sem[e] = self._newsem("s" + e)
        self.cnt[e] = 0

    def _deps(self, eng, reads, writes, cdma=False):
        evs = {}

        def add(ev):
            s, v = ev[0], ev[1]
            k = id(s)
            if k not in evs or evs[k][1] < v:
                evs[k] = (s, v)

        for r in reads:
            for ev in r.w.values():
                add(ev)
        for w in writes:
            for ev in w.w.values():
                if cdma and w.cw and ev[2]:
                    continue
                add(ev)
            for ev in w.rs.values():
                add(ev)
        waits = []
        seen = self.seen[eng]
        for k, (s, v) in evs.items():
            if eng == "pe" and s is self.sem["pe"]:
                continue
            if seen.get(k, 0) >= v:
                continue
            seen[k] = v
            waits.append((s, v))
        return waits

    def _mark(self, ev, reads, writes, cdma=False):
        k = id(ev[0])
        for r in reads:
            r.rs[k] = (ev[0], ev[1])
        for w in writes:
            if cdma and w.cw:
                w.w[k] = (ev[0], ev[1], True)
            else:
                w.w = {k: (ev[0], ev[1], False)}
            w.rs = {}

    def op(self, eng, fn, reads=(), writes=()):
        waits = self._deps(eng, reads, writes)
        if self.cnt[eng] >= MAXV:
            self._rot(eng)
        self.cnt[eng] += 1
        ev = (self.sem[eng], self.cnt[eng])
        self.q[eng].append((waits, fn, (ev[0], 1)))
        self._mark(ev, reads, writes)

    def dma(self, eng, out, in_, reads=(), writes=()):
        nsl = len(self.slots[eng])
        slot = self.slots[eng][self.dn[eng] % nsl]
        self.dn[eng] += 1
        if slot[1] >= MAXV:
            slot[0] = self._newsem(f"d{eng}r")
            slot[1] = 0
        sem = slot[0]
        waits = self._deps(eng, reads, writes, cdma=True)
        if slot[1] > 0 and self.seen[eng].get(id(sem), 0) < slot[1]:
            self.seen[eng][id(sem)] = slot[1]
            waits.append((sem, slot[1]))
        slot[1] += 16
        ev = (sem, slot[1])
        self.q[eng].append((waits, lambda e: e.dma_start(out=out, in_=in_, allow_slow_non_contiguous=True), (sem, 16)))
        self._mark(ev, reads, writes, cdma=True)

    def barrier(self):
        evs = [(self.sem[e], self.cnt[e]) for e in ENGS if self.cnt[e] > 0]
        for e in self.slots:
            for (sm, c) in self.slots[e]:
                if c > 0:
                    evs.append((sm, c))
        for e in ENGS:
            waits = []
            for s, v in evs:
                if s is self.sem[e]:
                    continue
                if self.seen[e].get(id(s), 0) >= v:
                    continue
                self.seen[e][id(s)] = v
                waits.append((s, v))
            if waits:
                self.q[e].append((waits, None, None))

    def emit(self, blk):
        def run(eng_name):
            def body(e):
                for waits, fn, inc in self.q[eng_name]:
                    for s, v in waits:
                        e.wait_ge(s, v)
                    if fn is not None:
                        ins = fn(e)
                        ins.then_inc(inc[0], inc[1])
            return body

        blk.tensor(run("pe"))
        blk.scalar(run("act"))
        blk.vector(run("dve"))
        blk.gpsimd(run("pool"))
        blk.sync(run("sp"))

    def mm(self, out, lhsT, rhs, start, stop, reads, writes):
        self.op("pe", lambda e: e.matmul(out, lhsT=lhsT, rhs=rhs, start=start, stop=stop,
                                         skip_group_check=True), reads, writes)

    def tr(self, out, in_, ident, reads, writes):
        self.op("pe", lambda e: e.transpose(out, in_, ident), reads, writes)

    def act(self, out, in_, func, reads, writes, bias=0.0, scale=1.0, accum_out=None):
        if accum_out is None:
            self.op("act", lambda e: e.activation(out=out, in_=in_, func=func, bias=bias, scale=scale),
                    reads, writes)
        else:
            self.op("act", lambda e: e.activation(out=out, in_=in_, func=func, bias=bias, scale=scale,
                                                  accum_out=accum_out), reads, writes)

    def ts(self, eng, out, in0, s1, s2, op0, op1, reads, writes):
        if s2 is None:
            self.op(eng, lambda e: e.tensor_scalar(out=out, in0=in0, scalar1=s1, scalar2=None, op0=op0),
                    reads, writes)
        else:
            self.op(eng, lambda e: e.tensor_scalar(out=out, in0=in0, scalar1=s1, scalar2=s2, op0=op0, op1=op1),
                    reads, writes)

    def tt(self, eng, out, in0, in1, op, reads, writes):
        self.op(eng, lambda e: e.tensor_tensor(out=out, in0=in0, in1=in1, op=op), reads, writes)

    def stt(self, eng, out, in0, scalar, in1, op0, op1, reads, writes):
        self.op(eng, lambda e: e.scalar_tensor_tensor(out=out, in0=in0, scalar=scalar, in1=in1, op0=op0, op1=op1),
                reads, writes)

    def cp(self, eng, out, in_, reads, writes):
        self.op(eng, lambda e: e.tensor_copy(out=out, in_=in_), reads, writes)

    def memset(self, eng, ap, val, writes):
        self.op(eng, lambda e: e.memset(ap, val), (), writes)


class Arena:
    def __init__(self, t, nbytes):
        self.t, self.n, self.off = t, nbytes, 0

    def mark(self):
        return self.off

    def release(self, m):
        self.off = m

    def alloc(self, shape, dt, name=""):
        esz = 4 if dt == F32 else 2
        n = 1
        for s in shape[1:]:
            n *= s
        nb = (n * esz + 63) // 64 * 64
        assert self.off + nb <= self.n, f"arena overflow {name}: {self.off}+{nb}>{self.n}"
        self.hw = max(getattr(self, "hw", 0), self.off + nb)
        v = self.t[:, self.off:self.off + n * esz].bitcast(dt)
        self.off += nb
        if len(shape) == 3:
            v = v.rearrange("p (a b) -> p a b", a=shape[1])
        elif len(shape) == 4:
            v = v.rearrange("p (a b c) -> p a b c", a=shape[1], b=shape[2])
        if shape[0] < 128:
            v = v[0:shape[0]]
        return v


def build(cfg):
    D, NH, T, DFF, PAST, DT = cfg["D"], cfg["NH"], cfg["T"], cfg["DFF"], cfg["PAST"], cfg["DT"]
    KD = D // 128
    W = NH * 64
    WC = W // 128
    MIX = 2 * W
    MC = MIX // 128
    IN = 6 * W + NH
    FFC = DFF // 128
    NZ = T // 2048
    TO = T // 4
    NSUBO = TO // 128
    PB = PAST // 128
    SKB = PB + 1
    SK = SKB * 128
    NQS = DT
    assert DT == 64 and T % 2048 == 0 and D % 512 == 0 and W % 128 == 0
    CQ, CK, CV = 0, W, 2 * W
    FQ, FK, FV, LF = 3 * W, 4 * W, 5 * W, 6 * W

    nc = bass.Bass("TRN2", target_bir_lowering=False)

    def din(name, shape):
        return nc.dram_tensor(name, list(shape), F32, kind="ExternalInput").ap()

    def dout(name, shape):
        return nc.dram_tensor(name, list(shape), F32, kind="ExternalOutput").ap()

    def dscr(name, shape, dt):
        return nc.dram_tensor(name, list(shape), dt, kind="Internal").ap()

    xfull = din("xfull", (T, D)); xown = din("xown", (TO, D)); xsam = din("xsam", (128, D))
    cT = din("cT", (128, KD * 3))
    csk = din("csk", (2, PAST, W)); csv = din("csv", (2, PAST, W))
    cfk = din("cfk", (2, PAST, W)); cfv = din("cfv", (2, PAST, W)); clfT = din("clfT", (NH, 2 * PAST))
    w_ada = din("w_ada", (D, 6 * D)); b_ada = din("b_ada", (6 * D,))
    w_in = din("w_in", (D, IN)); w_o = din("w_o", (MIX, D))
    w_gate = din("w_gate", (D, DFF)); w_up = din("w_up", (D, DFF)); w_down = din("w_down", (DFF, D))
    g_mix = din("g_mix", (D,)); g_ffn = din("g_ffn", (D,)); g_final = din("g_final", (D,))
    goT = din("goT", (128, MC)); b_f = din("b_f", (NH,))
    cst = din("cst", (128, 640)); masks = din("masks", (128, 2 * 16 * 512)); smask = din("smask", (128, 2 * 64))
    sel = din("sel", (128, 4))

    y_own = dout("y_own", (TO, D)); y_sam = dout("y_sam", (128, D))
    o_sbk = dout("o_sbk", (TO, W)); o_sbv = dout("o_sbv", (TO, W)); o_fxk = dout("o_fxk", (TO, W))
    o_fxv = dout("o_fxv", (TO, W)); o_lf = dout("o_lf", (TO, NH))
    s_sbk = dout("s_sbk", (128, W)); s_sbv = dout("s_sbv", (128, W)); s_fxk = dout("s_fxk", (128, W))
    s_fxv = dout("s_fxv", (128, W)); s_lf = dout("s_lf", (128, NH))

    KTs = dscr("KTs", (NH // 2, 2, 128, T), BF16)
    KAs = dscr("KAs", (NH, 6, T), BF16)
    QAs = dscr("QAs", (NH, 6, TO), BF16)
    Vs = dscr("Vs", (2, T, W), BF16)
    QTs = dscr("QTs", (NH // 2, 2, 128, TO), BF16)
    Os = dscr("Os", (TO, MIX), F32)
    KTss = dscr("KTss", (2, NH, 2, 70, SK), BF16)
    Vss = dscr("Vss", (2, 2, SK, W), BF16)
    QTss = dscr("QTss", (2, NH, 2, 70, NQS), BF16)
    Oss = dscr("Oss", (128, MIX), F32)
    adas = dscr("adas", (3, 6, D), F32)

    es = ExitStack()
    ARENA_BYTES = cfg.get("ARENA", 207 * 1024)
    arena_t = es.enter_context(nc.sbuf_tensor("arena", [128, ARENA_BYTES], U8))
    ps = es.enter_context(nc.psum_tensor("ps", [128, 4096], F32))
    P = Prog(nc, es)
    AR = Arena(arena_t, ARENA_BYTES)

    def bank(i):
        return ps[:, i * 512:(i + 1) * 512]

    def bank_bf(i):
        return ps[:, i * 512:(i + 1) * 512].bitcast(BF16)

    RB = [Res(f"bank{i}") for i in range(8)]
    R_scr = {k: Res(k) for k in ("KTs", "Vs", "QTs", "Os", "KTss", "Vss", "QTss", "Oss", "adas", "KAs", "QAs")}
    R_in = Res("inputs")
    R_out = Res("outputs")

    identf = AR.alloc([128, 128], F32, "identf")
    identb = AR.alloc([128, 128], BF16, "identb")
    triIb = AR.alloc([128, 128], BF16, "triI")
    compb = AR.alloc([128, 128], BF16, "comp")
    zrow = AR.alloc([128, 128], BF16, "zrow")
    onesb = AR.alloc([128, 512], BF16, "onesb")
    selt = AR.alloc([128, 4], F32, "sel")
    goTt = AR.alloc([128, MC], F32, "goT")
    bft = AR.alloc([128, 1], F32, "bft")
    nbft = AR.alloc([128, 1], F32, "nbft")
    bfrow = AR.alloc([128, NH], F32, "bfrow")
    scT = AR.alloc([128, KD, 3], BF16, "scT")
    fmix = AR.alloc([128, 2, 3, KD], F32, "fmix")
    fffn = AR.alloc([128, 2, 3, KD], F32, "fffn")
    R_c = Res("consts")
    R_f = Res("fvecs")

    P.dma("sp", identf, cst[:, 0:128], (R_in,), (R_c,))
    P.dma("pool", identb, cst[:, 0:128], (R_in,), (R_c,))
    P.dma("pool", triIb, cst[:, 128:256], (R_in,), (R_c,))
    P.dma("pool", compb, cst[:, 256:384], (R_in,), (R_c,))
    P.dma("pool", zrow, cst[:, 384:512], (R_in,), (R_c,))
    P.dma("sp", selt, sel, (R_in,), (R_c,))
    P.dma("sp", goTt, goT, (R_in,), (R_c,))
    P.dma("sp", bft[0:NH, :], b_f.rearrange("(h o) -> h o", o=1), (R_in,), (R_c,))
    P.dma("sp", bfrow, b_f.partition_broadcast(128), (R_in,), (R_c,))
    P.memset("dve", onesb, 1.0, (R_c,))
    P.ts("dve", nbft[0:NH, :], bft[0:NH, :], -1.0, None, ALU.mult, None, (R_c,), (R_c,))

    m0 = AR.mark()
    ctile = AR.alloc([128, KD * 3], F32, "ctile")
    ctmp = AR.alloc([128, KD * 3], F32, "ctmp")
    arow = AR.alloc([128, 6 * D], F32, "arow")[0:3]
    brow = AR.alloc([128, 6 * D], F32, "brow")[0:3]
    grow = AR.alloc([128, 2 * D], F32, "grow")[0:3]
    wadab = [AR.alloc([128, KD, 512], BF16, f"wada{i}") for i in range(2)]
    R_ct, R_arow, R_wada = Res("ct"), Res("arow", stg=True), [Res("wada0"), Res("wada1")]
    P.dma("sp", ctile, cT, (R_in,), (R_ct,))
    P.dma("sp", brow, b_ada.partition_broadcast(3), (R_in,), (R_arow,))
    P.dma("sp", grow[:, 0:D], g_mix.partition_broadcast(3), (R_in,), (R_arow,))
    P.dma("sp", grow[:, D:2 * D], g_ffn.partition_broadcast(3), (R_in,), (R_arow,))
    P.act(ctmp, ctile, AF.Exp, (R_ct,), (R_ct,), scale=-1.0)
    P.ts("dve", ctmp, ctmp, 1.0, None, ALU.add, None, (R_ct,), (R_ct,))
    P.op("dve", lambda e: e.reciprocal(out=ctmp, in_=ctmp), (R_ct,), (R_ct,))
    P.tt("dve", scT.rearrange("p k c -> p (k c)"), ctile, ctmp, ALU.mult, (R_ct,), (R_c,))
    wadav = w_ada.rearrange("(k p) c -> p k c", p=128)
    NAC = 6 * D // 512
    for j in range(NAC):
        wb, rw = wadab[j % 2], R_wada[j % 2]
        P.dma("pool", wb, wadav[:, :, j * 512:(j + 1) * 512], (R_in,), (rw,))
        bk = bank(j % 2)
        for k in range(KD):
            P.mm(bk[0:3, :], scT[:, k, :], wb[:, k, :], k == 0, k == KD - 1, (rw, R_c), (RB[j % 2],))
        P.tt("dve", arow[:, j * 512:(j + 1) * 512], bk[0:3, :], brow[:, j * 512:(j + 1) * 512], ALU.add,
             (RB[j % 2], R_arow), (R_arow,))
    for idx, gsrc in ((1, 0), (4, 1)):
        P.stt("dve", arow[:, idx * D:(idx + 1) * D], arow[:, idx * D:(idx + 1) * D], 1.0,
              grow[:, gsrc * D:(gsrc + 1) * D], ALU.add, ALU.mult, (R_arow,), (R_arow,))
    for idx in (2, 5):
        P.ts("dve", arow[:, idx * D:(idx + 1) * D], arow[:, idx * D:(idx + 1) * D], 1.0, None, ALU.add, None,
             (R_arow,), (R_arow,))
    P.dma("sp", adas.rearrange("c s d -> c (s d)"), arow, (R_arow,), (R_scr["adas"],))
    with nc.allow_non_contiguous_dma(reason="tiny feature-major vector loads"):
        for cnd in range(3):
            for (dst, a_sc, a_sh) in ((fmix, 1, 0), (fffn, 4, 3)):
                P.dma("sp", dst[:, 0, cnd, :], adas[cnd, a_sc, :].rearrange("(k p) -> p k", p=128),
                      (R_scr["adas"],), (R_f,))
                P.dma("sp", dst[:, 1, cnd, :], adas[cnd, a_sh, :].rearrange("(k p) -> p k", p=128),
                      (R_scr["adas"],), (R_f,))
    P.barrier()
    AR.release(m0)

    def norm_to_fm(xt, G, width, tmpsq, ss, xs_b, hT, conds, fvec, Rx, Rtmp, RhT, nchunk, tbank, part="all"):
        if part in ("all", "pre"):
            for g in range(G):
                P.memset("dve", ss[:, g:g + 1], 0.0, (Rtmp,))
                P.act(tmpsq[:, 0:width], xt[:, g, :], AF.Square, (Rx, Rtmp), (Rtmp,), accum_out=ss[:, g:g + 1])
            P.ts("dve", ss[:, 0:G], ss[:, 0:G], 1.0 / width, EPS, ALU.mult, ALU.add, (Rtmp,), (Rtmp,))
            P.act(ss[:, 0:G], ss[:, 0:G], AF.Ln, (Rtmp,), (Rtmp,))
            P.act(ss[:, 0:G], ss[:, 0:G], AF.Exp, (Rtmp,), (Rtmp,), scale=-0.5)
            for g in range(G):
                P.ts("dve", xs_b[:, g, :], xt[:, g, :], ss[:, g:g + 1], None, ALU.mult, None, (Rx, Rtmp), (Rtmp,))
        if part == "pre":
            return
        for k in range(nchunk):
            bi = tbank[k % len(tbank)]
            pb = bank_bf(bi)
            for g in range(G):
                P.tr(pb[:, g * 128:(g + 1) * 128], xs_b[:, g, k * 128:(k + 1) * 128], identb, (Rtmp, R_c), (RB[bi],))
            for (lo, hi, cnd) in conds:
                sc_ap, sh_ap = fvec(cnd, k)
                src = pb[:, 0:G * 128].rearrange("p (g t) -> p g t", g=G)[:, :, lo:hi]
                dst = hT[:, k, 0:G * 128].rearrange("p (g t) -> p g t", g=G)[:, :, lo:hi]
                if sh_ap is None:
                    P.ts("dve", dst, src, sc_ap, None, ALU.mult, None, (RB[bi], R_f, R_c), (RhT,))
                else:
                    P.ts("dve", dst, src, sc_ap, sh_ap, ALU.mult, ALU.add, (RB[bi], R_f, R_c), (RhT,))

    zbig = AR.alloc([128, 512], BF16, "zbig")
    mL = AR.mark()
    LGT = AR.alloc([128, T], F32, "LGT")[0:NH]
    LGS = AR.alloc([128, 2 * (PAST + DT)], F32, "LGS")[0:NH]
    LGSv = LGS.rearrange("h (s n) -> h s n", s=2)
    lgtok = AR.alloc([128, NSUBO + 1, NH], F32, "lgtok")
    R_LGT, R_LGS, R_lgtok = Res("LGT"), Res("LGS"), Res("lgtok", stg=True)
    P.memset("dve", zbig, 0.0, (R_c,))
    mA = AR.mark()
    winb = AR.alloc([128, KD, IN], BF16, "winb")
    R_win = Res("win")
    winv = w_in.rearrange("(k p) c -> p k c", p=128)
    for k in range(KD):
        P.dma("pool", winb[:, k, :], winv[:, k, :], (R_in,), (R_win,))
    GA = 4
    xt2 = [AR.alloc([128, GA, D], F32, f"xt{i}") for i in range(2)]
    R_xt = [Res("xt0"), Res("xt1")]
    tmpsq = AR.alloc([128, D], F32, "tmpsq")
    ssA = AR.alloc([128, 8], F32, "ssA")
    xs_b2 = [AR.alloc([128, GA, D], BF16, f"xs_b{i}") for i in range(2)]
    hTA2 = [AR.alloc([128, KD, GA * 128], BF16, f"hTA{i}") for i in range(2)]
    R_tmpA2, R_hTA2 = [Res("tmpA0"), Res("tmpA1")], [Res("hTA0"), Res("hTA1")]
    cur = {"i": 0}

    class _H:
        def __getitem__(self, key):
            return hTA2[cur["i"]][key]
    hTA = _H()

    class _R:
        pass

    NST = 2
    fmo = [AR.alloc([128, 512], BF16, f"fmo{i}") for i in range(NST)]
    R_fmo = [Res(f"fmo{i}", stg=True) for i in range(NST)]
    tmo = [AR.alloc([128, 512], BF16, f"tmo{i}") for i in range(NST)]
    R_tmo = [Res(f"tmo{i}", stg=True) for i in range(NST)]
    tmf = [AR.alloc([128, 512], F32, f"tmf{i}") for i in range(NST)]
    R_tmf = [Res(f"tmf{i}", stg=True) for i in range(NST)]
    cnt = {"fm": 0, "tm": 0, "tf": 0, "x": 0}

    def fvec_mix(cnd, k):
        return fmix[:, 0, cnd, k:k + 1], fmix[:, 1, cnd, k:k + 1]

    def fm_chunk(N, col0, scale):
        bi = 2 + (cnt["fm"] % 2)
        bk = bank(bi)
        for k in range(KD):
            P.mm(bk[:, 0:N], winb[:, k, col0:col0 + 128], hTA[:, k, 0:N], k == 0, k == KD - 1,
                 (R_win, R_hTA2[cur["i"]]), (RB[bi],))
        i = cnt["fm"] % NST
        cnt["fm"] += 1
        P.act(fmo[i][:, 0:N], bk[:, 0:N], AF.Copy, (RB[bi],), (R_fmo[i],), scale=scale)
        return fmo[i], R_fmo[i]

    def tm_block(g, col0, ncols):
        bi = 4 + (cnt["tm"] % 2)
        cnt["tm"] += 1
        bk = bank(bi)
        for k in range(KD):
            P.mm(bk[:, 0:ncols], hTA[:, k, g * 128:(g + 1) * 128], winb[:, k, col0:col0 + ncols], k == 0,
                 k == KD - 1, (R_win, R_hTA2[cur["i"]]), (RB[bi],))
        return bk, bi

    def to_bf(bk, bi, ncols):
        i = cnt["tf"] % NST
        cnt["tf"] += 1
        P.cp("dve", tmo[i][:, 0:ncols], bk[:, 0:ncols], (RB[bi],), (R_tmo[i],))
        return tmo[i], R_tmo[i]

    def to_f32(bk, bi, ncols):
        i = cnt["tf"] % NST
        cnt["tf"] += 1
        P.act(tmf[i][:, 0:ncols], bk[:, 0:ncols], AF.Copy, (RB[bi],), (R_tmf[i],))
        return tmf[i], R_tmf[i]

    ssA2 = [ssA, AR.alloc([128, 8], F32, "ssA1")]
    tmpsq2 = [tmpsq, AR.alloc([128, D], BF16, "tmpsq1")]

    def front(xsrc, G, conds, i, part):
        if part == "pre":
            P.dma("pool", xt2[i][:, 0:G, :], xsrc.rearrange("(g p) d -> p g d", p=128), (R_in,), (R_xt[i],))
        norm_to_fm(xt2[i], G, D, tmpsq2[i], ssA2[i], xs_b2[i], hTA2[i], conds, fvec_mix, R_xt[i], R_tmpA2[i],
                   R_hTA2[i], KD, (0, 1), part=part)

    def logits_fm(N):
        bk = bank(6)
        for k in range(KD):
            P.mm(bk[0:NH, 0:N], winb[:, k, LF:LF + NH], hTA[:, k, 0:N], k == 0, k == KD - 1,
                 (R_win, R_hTA2[cur["i"]]), (RB[6],))
        return bk

    def logits_tm(g, slot):
        bk = bank(7)
        for k in range(KD):
            P.mm(bk[:, 0:NH], hTA[:, k, g * 128:(g + 1) * 128], winb[:, k, LF:LF + NH], k == 0, k == KD - 1,
                 (R_win, R_hTA2[cur["i"]]), (RB[7],))
        P.tt("dve", lgtok[:, slot, :], bk[:, 0:NH], bfrow, ALU.add, (RB[7], R_c), (R_lgtok,))

    CBLK = [(c0, min(512, W - c0)) for c0 in range(0, W, 512)]

    tiles = []

    def a1_p1(tt_):
        for s_, col0 in ((0, CK), (1, FK)):
            for c in range(WC):
                t_, r_ = fm_chunk(512, col0 + c * 128, 1.0)
                P.dma("sp", KTs[c, s_, :, tt_ * 512:(tt_ + 1) * 512], t_[:, 0:512], (r_,), (R_scr["KTs"],))

    def a1_p2(tt_):
        for g in range(GA):
            tok0 = tt_ * 512 + g * 128
            for s_, col0 in ((0, CV), (1, FV)):
                for (c0, nn) in CBLK:
                    bk, bi = tm_block(g, col0 + c0, nn)
                    t_, r_ = to_bf(bk, bi, nn)
                    P.dma("sp", Vs[s_, tok0:tok0 + 128, c0:c0 + nn], t_[:, 0:nn], (r_,), (R_scr["Vs"],))
        bk = logits_fm(512)
        P.cp("dve", LGT[:, tt_ * 512:(tt_ + 1) * 512], bk[0:NH, 0:512], (RB[6],), (R_LGT,))

    for tt_ in range(T // 512):
        tiles.append((xfull[tt_ * 512:(tt_ + 1) * 512, :], GA, [(0, 128, 0)],
                      (lambda tt_=tt_: a1_p1(tt_)), (lambda tt_=tt_: a1_p2(tt_))))

    def a2_p1(z):
        for s_, col0 in ((0, CQ), (1, FQ)):
            for c in range(WC):
                t_, r_ = fm_chunk(512, col0 + c * 128, 0.125)
                P.dma("sp", QTs[c, s_, :, z * 512:(z + 1) * 512], t_[:, 0:512], (r_,), (R_scr["QTs"],))

    def a2_p2(z):
        for g in range(GA):
            r0 = z * 512 + g * 128
            for (col0, od) in ((CK, o_sbk), (CV, o_sbv), (FK, o_fxk), (FV, o_fxv)):
                for (c0, nn) in CBLK:
                    bk, bi = tm_block(g, col0 + c0, nn)
                    t_, r_ = to_f32(bk, bi, nn)
                    P.dma("sp", od[r0:r0 + 128, c0:c0 + nn], t_[:, 0:nn], (r_,), (R_out,))
            logits_tm(g, z * 4 + g)

    for z in range(NZ):
        tiles.append((xown[z * 512:(z + 1) * 512, :], GA, [(0, 128, 0)],
                      (lambda z=z: a2_p1(z)), (lambda z=z: a2_p2(z))))

    def as_p1():
        for (s_, colq, colk) in ((0, CQ, CK), (1, FQ, FK)):
            for kind, col0, scale in (("q", colq, 0.125), ("k", colk, 1.0)):
                for c in range(WC):
                    t_, r_ = fm_chunk(128, col0 + c * 128, scale)
                    for half in range(2):
                        h = 2 * c + half
                        for sbi in range(2):
                            src = t_[half * 64:(half + 1) * 64, sbi * 64:(sbi + 1) * 64]
                            if kind == "q":
                                P.dma("sp", QTss[sbi, h, s_, 0:64, :], src, (r_,), (R_scr["QTss"],))
                            else:
                                P.dma("sp", KTss[sbi, h, s_, 0:64, 0:64], src, (r_,), (R_scr["KTss"],))

    def as_p2():
        for (col0, od) in ((CK, s_sbk), (CV, s_sbv), (FK, s_fxk), (FV, s_fxv)):
            for (c0, nn) in CBLK:
                bk, bi = tm_block(0, col0 + c0, nn)
                t_, r_ = to_f32(bk, bi, nn)
                P.dma("sp", od[0:128, c0:c0 + nn], t_[:, 0:nn], (r_,), (R_out,))
                if col0 in (CV, FV):
                    sidx = 0 if col0 == CV else 1
                    t2, r2 = to_bf(bk, bi, nn)
                    for sbi in range(2):
                        P.dma("sp", Vss[sbi, sidx, 0:64, c0:c0 + nn], t2[sbi * 64:(sbi + 1) * 64, 0:nn],
                              (r2,), (R_scr["Vss"],))
        logits_tm(0, NSUBO)
        bk = logits_fm(128)
        for sbi in range(2):
            P.cp("dve", LGSv[:, sbi, PAST:PAST + DT], bk[0:NH, sbi * 64:(sbi + 1) * 64], (RB[6],), (R_LGS,))

    tiles.append((xsam, 1, [(0, 64, 1), (64, 128, 2)], as_p1, as_p2))

    front(tiles[0][0], tiles[0][1], tiles[0][2], 0, "pre")
    front(tiles[0][0], tiles[0][1], tiles[0][2], 0, "tr")
    for ti, (xsrc_, G_, conds_, p1_, p2_) in enumerate(tiles):
        nx = tiles[ti + 1] if ti + 1 < len(tiles) else None
        if nx is not None:
            front(nx[0], nx[1], nx[2], (ti + 1) % 2, "pre")
        cur["i"] = ti % 2
        p1_()
        if nx is not None:
            front(nx[0], nx[1], nx[2], (ti + 1) % 2, "tr")
        cur["i"] = ti % 2
        p2_()
    lgflat = lgtok.rearrange("p a h -> p (a h)")
    P.act(lgflat, lgflat, AF.Exp, (R_lgtok,), (R_lgtok,), scale=-1.0)
    P.act(lgflat, lgflat, AF.Ln, (R_lgtok,), (R_lgtok,), bias=1.0)
    P.ts("dve", lgflat, lgflat, -1.0, None, ALU.mult, None, (R_lgtok,), (R_lgtok,))
    with nc.allow_non_contiguous_dma(reason="small logf rows"):
        P.dma("sp", o_lf.rearrange("(a p) h -> p a h", p=128), lgtok[:, 0:NSUBO, :], (R_lgtok,), (R_out,))
        P.dma("sp", s_lf, lgtok[:, NSUBO, :], (R_lgtok,), (R_out,))

    ktin = AR.alloc([128, PB, W], F32, "ktin")
    ktb = AR.alloc([128, PAST], BF16, "ktb")
    R_ktin, R_ktb = Res("ktin"), Res("ktb", stg=True)
    for sbi in range(2):
        for (s, ck, cv) in ((0, csk, csv), (1, cfk, cfv)):
            P.dma("pool", Vss[sbi, s, 128:128 + PAST, :], cv[sbi], (R_in,), (R_scr["Vss"],))
            P.dma("sp", Vss[sbi, s, 64:128, :], zbig[0:64, 0:W], (R_c,), (R_scr["Vss"],))
            P.dma("sp", ktin, ck[sbi].rearrange("(kb p) w -> p kb w", p=128), (R_in,), (R_ktin,))
            for c in range(WC):
                for kb0 in range(0, PB, 4):
                    nk = min(4, PB - kb0)
                    bi = 2 + (cnt["fm"] % 2)
                    cnt["fm"] += 1
                    bk = bank(bi)
                    for j in range(nk):
                        P.tr(bk[:, j * 128:(j + 1) * 128], ktin[:, kb0 + j, c * 128:(c + 1) * 128], identf,
                             (R_ktin, R_c), (RB[bi],))
                    P.act(ktb[:, kb0 * 128:(kb0 + nk) * 128], bk[:, 0:nk * 128], AF.Copy, (RB[bi],), (R_ktb,))
                for half in range(2):
                    P.dma("sp", KTss[sbi, 2 * c + half, s, 0:64, 128:128 + PAST], ktb[half * 64:(half + 1) * 64, :],
                          (R_ktb,), (R_scr["KTss"],))
            for h in range(NH):
                P.dma("sp", KTss[sbi, h, s, :, 64:128], zbig[0:70, 0:64], (R_c,), (R_scr["KTss"],))
    P.barrier()
    AR.release(mA)

    mF = AR.mark()
    Gp = AR.alloc([128, T], F32, "Gp")[0:NH]
    Go = AR.alloc([128, TO], F32, "Go")[0:NH]
    Gs = AR.alloc([128, 2 * (PAST + DT)], F32, "Gs")[0:NH]
    Gsv = Gs.rearrange("h (s n) -> h s n", s=2)
    Gq = AR.alloc([128, 2 * DT], F32, "Gq")[0:NH]
    Gqv = Gq.rearrange("h (s n) -> h s n", s=2)
    pbuf = [AR.alloc([128, max(T, 2 * (PAST + DT))], BF16, f"pbuf{i}")[0:NH] for i in range(2)]
    R_G, R_Go, R_Gs, R_Gq = Res("G"), Res("Go"), Res("Gs"), Res("Gq")
    R_pb = [Res("pb0", stg=True), Res("pb1", stg=True)]
    pcount = [0]

    def cumsum(dst, src, n, rsrc, rdst):
        for c0 in range(0, n, 512):
            nn = min(512, n - c0)
            init = 0.0 if c0 == 0 else dst[:, c0 - 1:c0]
            P.op("dve", lambda e, c0=c0, nn=nn, init=init: e.tensor_tensor_scan(
                out=dst[:, c0:c0 + nn], data0=onesb[0:NH, 0:nn], data1=src[:, c0:c0 + nn], initial=init,
                op0=ALU.mult, op1=ALU.add), (rsrc, rdst, R_c), (rdst,))

    def split_rows(src, n, rsrc, dst_fn):
        for j in range(3):
            i = pcount[0] % 2
            pcount[0] += 1
            pb = pbuf[i]
            P.cp("dve", pb[:, 0:n], src, (rsrc,), (R_pb[i],))
            if j < 2:
                P.tt("dve", src, src, pb[:, 0:n], ALU.subtract, (rsrc, R_pb[i]), (rsrc,))
            for (d_, s_, rd_) in dst_fn(j, pb):
                P.dma("sp", d_, s_, (R_pb[i],), (rd_,))

    def ones_rows(n, dst_list):
        i = pcount[0] % 2
        pcount[0] += 1
        P.memset("dve", pbuf[i][:, 0:n], 1.0, (R_pb[i],))
        for (d_, rd_) in dst_list:
            P.dma("sp", d_, pbuf[i][:, 0:d_.shape[-1]], (R_pb[i],), (rd_,))

    P.act(LGT, LGT, AF.Exp, (R_LGT, R_c), (R_LGT,), scale=-1.0, bias=nbft[0:NH, :])
    P.act(LGT, LGT, AF.Ln, (R_LGT,), (R_LGT,), bias=1.0)
    cumsum(Gp, LGT, T, R_LGT, R_G)
    Gv = Gp.rearrange("h (m r t) -> h m r t", r=4, t=128)
    Gov = Go.rearrange("h (m t) -> h m t", t=128)
    P.ts("dve", Gov, Gv[:, :, 0, :], selt[0:NH, 0:1], None, ALU.mult, None, (R_G, R_c), (R_Go,))
    for rr in range(1, 4):
        P.stt("dve", Gov, Gv[:, :, rr, :], selt[0:NH, rr:rr + 1], Gov, ALU.mult, ALU.add, (R_G, R_c, R_Go), (R_Go,))
    P.ts("dve", Go, Go, -1.0, None, ALU.mult, None, (R_Go,), (R_Go,))
    split_rows(Gp, T, R_G, lambda j, pb: [(KAs[:, 3 + j, :], pb[:, 0:T], R_scr["KAs"])])
    split_rows(Go, TO, R_Go, lambda j, pb: [(QAs[:, j, :], pb[:, 0:TO], R_scr["QAs"])])
    ones_rows(T, [(KAs[:, j, :], R_scr["KAs"]) for j in range(3)] +
              [(QAs[:, 3 + j, :], R_scr["QAs"]) for j in range(3)])
    P.dma("sp", LGSv[:, :, 0:PAST], clfT.rearrange("h (s n) -> h s n", s=2), (R_in,), (R_LGS,))
    P.ts("dve", LGSv[:, :, 0:PAST], LGSv[:, :, 0:PAST], -1.0, None, ALU.mult, None, (R_LGS,), (R_LGS,))
    P.act(LGSv[:, :, PAST:PAST + DT], LGSv[:, :, PAST:PAST + DT], AF.Exp, (R_LGS, R_c), (R_LGS,), scale=-1.0,
          bias=nbft[0:NH, :])
    P.act(LGSv[:, :, PAST:PAST + DT], LGSv[:, :, PAST:PAST + DT], AF.Ln, (R_LGS,), (R_LGS,), bias=1.0)
    for sbi in range(2):
        cumsum(Gsv[:, sbi, :], LGSv[:, sbi, :], PAST + DT, R_LGS, R_Gs)
    P.ts("dve", Gqv, Gsv[:, :, PAST:PAST + DT], -1.0, None, ALU.mult, None, (R_Gs,), (R_Gq,))

    def kdst_s(j, pb):
        pv = pb[:, 0:2 * (PAST + DT)].rearrange("h (s n) -> h s n", s=2)
        out = []
        for sbi in range(2):
            out.append((KTss[sbi, :, 1, 67 + j, 0:DT], pv[:, sbi, PAST:PAST + DT], R_scr["KTss"]))
            out.append((KTss[sbi, :, 1, 67 + j, 128:128 + PAST], pv[:, sbi, 0:PAST], R_scr["KTss"]))
        return out

    def qdst_s(j, pb):
        pv = pb[:, 0:2 * DT].rearrange("h (s n) -> h s n", s=2)
        return [(QTss[sbi, :, 1, 64 + j, :], pv[:, sbi, :], R_scr["QTss"]) for sbi in range(2)]

    split_rows(Gs, 2 * (PAST + DT), R_Gs, kdst_s)
    split_rows(Gq, 2 * DT, R_Gq, qdst_s)
    ones_rows(SK, [(KTss[sbi, :, 1, 64 + j, :], R_scr["KTss"]) for sbi in range(2) for j in range(3)] +
              [(QTss[sbi, :, 1, 67 + j, :], R_scr["QTss"]) for sbi in range(2) for j in range(3)])
    P.barrier()
    AR.release(mL)

    mB = AR.mark()
    maskb = AR.alloc([128, 2, 16, 512], BF16, "maskb")
    smaskb = AR.alloc([128, 2, 64], BF16, "smaskb")
    R_mask = Res("mask")
    mv = masks.rearrange("p (s k q) -> p s k q", s=2, k=16)
    for s in range(2):
        for k4 in range(0, 16, 4):
            P.dma("pool", maskb[:, s, k4:k4 + 4, :], mv[:, s, k4:k4 + 4, :], (R_in,), (R_mask,))
    P.dma("pool", smaskb, smask.rearrange("p (s q) -> p s q", s=2), (R_in,), (R_mask,))
    NKB = T // 128
    KTt = [[AR.alloc([128, T], BF16, f"KTt{s}{i}") for i in range(2)] for s in range(2)]
    Vt = [[AR.alloc([128, NKB, 128], BF16, f"Vt{s}{i}") for i in range(2)] for s in range(2)]
    QTt = [[AR.alloc([128, TO], BF16, f"QTt{s}{i}") for i in range(2)] for s in range(2)]
    R_KT = [[Res(f"KT{s}{i}") for i in range(2)] for s in range(2)]
    R_V = [[Res(f"V{s}{i}") for i in range(2)] for s in range(2)]
    R_QT = [[Res(f"QT{s}{i}") for i in range(2)] for s in range(2)]
    for s in range(2):
        for i in range(2):
            P.memset("pool", Vt[s][i], 0.0, (R_V[s][i],))
            P.memset("pool", KTt[s][i], 0.0, (R_KT[s][i],))
            P.memset("pool", QTt[s][i], 0.0, (R_QT[s][i],))
        for i in range(2):
            if s == 1:
                P.memset("dve", Vt[s][i][:, :, 64:65], 1.0, (R_V[s][i],))
    SPt = [AR.alloc([128, 512], BF16, f"SPt{i}") for i in range(2)]
    Gt = [AR.alloc([128, 512], F32, f"Gt{i}") for i in range(2)]
    at = [AR.alloc([128, 512], BF16, f"at{i}") for i in range(2)]
    Pt = [AR.alloc([128, 512], BF16, f"Pt{i}") for i in range(2)]
    R_E = [Res("E0"), Res("E1")]; R_SP = [Res("SP0"), Res("SP1")]; R_G2 = [Res("G0"), Res("G1")]
    R_a = [Res("a0"), Res("a1")]; R_P = [Res("P0"), Res("P1")]
    osb_s = AR.alloc([128, 512], F32, "osb_s")
    ofx_s = AR.alloc([128, 512], F32, "ofx_s")
    otok = AR.alloc([128, 4, 2, 64], F32, "otok")
    rec = AR.alloc([128, 4], F32, "rec")
    R_os, R_otok = Res("os"), Res("otok", stg=True)
    BZ, BS, BA, BOS, BOF, BTP = (0, 1), (2, 3), 4, 5, 6, 7
    itc = [0]

    def sweep(KT, QT, Vtile, rK, rQ, rV, q0, Nq, blocks, odst_fn):
        A = bank(BA); Osb = bank(BOS); Ofx = bank(BOF)
        for (bk_, M, rb) in ((A, 128, RB[BA]), (Osb, 128, RB[BOS]), (Ofx, 128, RB[BOF])):
            P.mm(bk_[0:M, 0:Nq], zrow[0:1, 0:M], onesb[0:1, 0:Nq], True, True, (R_c,), (rb,))
        nb = len(blocks)

        def mm1(i):
            kcol, vkb, qlo, msb, mfx = blocks[i]
            b = (itc[0] + i) % 2
            Z = bank(BZ[b]); S = bank(BS[b])
            P.mm(Z[:, qlo:Nq], KT[0][:, kcol:kcol + 128], QT[0][:, q0 + qlo:q0 + Nq], True, msb is None,
                 (rK[0], rQ[0]), (RB[BZ[b]],))
            if msb is not None:
                P.mm(Z[:, qlo:Nq], identb, msb, False, True, (R_c, R_mask), (RB[BZ[b]],))
            P.mm(S[:, qlo:Nq], KT[1][:, kcol:kcol + 128], QT[1][:, q0 + qlo:q0 + Nq], True, mfx is None,
                 (rK[1], rQ[1]), (RB[BS[b]],))
            if mfx is not None:
                P.mm(S[:, qlo:Nq], identb, mfx, False, True, (R_c, R_mask), (RB[BS[b]],))

        def actE(i):
            kcol, vkb, qlo, msb, mfx = blocks[i]
            b = (itc[0] + i) % 2
            Z = bank(BZ[b])
            P.act(Z[:, qlo:Nq], Z[:, qlo:Nq], AF.Exp, (RB[BZ[b]],), (RB[BZ[b]],))

        mm1(0)
        actE(0)
        for i in range(nb):
            kcol, vkb, qlo, msb, mfx = blocks[i]
            b = (itc[0] + i) % 2
            Z = bank(BZ[b]); S = bank(BS[b])
            sl = slice(qlo, Nq)
            if i + 1 < nb:
                mm1(i + 1)
            P.act(SPt[b][:, sl], Z[:, sl], AF.Ln, (RB[BZ[b]],), (R_SP[b],), bias=1.0)
            P.mm(A[:, sl], triIb, SPt[b][:, sl], False, True, (R_c, R_SP[b]), (RB[BA],))
            P.act(Pt[b][:, sl], S[:, sl], AF.Exp, (RB[BS[b]],), (R_P[b],))
            P.mm(Ofx[:, sl], Vtile[1][:, vkb, :], Pt[b][:, sl], False, True, (rV[1], R_P[b]), (RB[BOF],))
            if i + 1 < nb:
                actE(i + 1)
            if i > 0:
                pk, pv, pq, _, _ = blocks[i - 1]
                pb_ = (itc[0] + i - 1) % 2
                P.mm(Osb[:, pq:Nq], Vtile[0][:, pv, :], at[pb_][:, pq:Nq], False, True,
                     (rV[0], R_a[pb_]), (RB[BOS],))
            P.act(Gt[b][:, sl], A[:, sl], AF.Exp, (RB[BA],), (R_G2[b],))
            P.mm(A[:, sl], compb, SPt[b][:, sl], False, True, (R_c, R_SP[b]), (RB[BA],))
            P.tt("dve", at[b][:, sl], Z[:, sl], Gt[b][:, sl], ALU.mult, (RB[BZ[b]], R_G2[b]), (R_a[b],))
        pk, pv, pq, _, _ = blocks[nb - 1]
        pb_ = (itc[0] + nb - 1) % 2
        P.mm(Osb[:, pq:Nq], Vtile[0][:, pv, :], at[pb_][:, pq:Nq], False, True, (rV[0], R_a[pb_]), (RB[BOS],))
        itc[0] += nb
        P.cp("dve", osb_s[0:64, 0:Nq], Osb[0:64, 0:Nq], (RB[BOS],), (R_os,))
        P.cp("dve", ofx_s[0:65, 0:Nq], Ofx[0:65, 0:Nq], (RB[BOF],), (R_os,))
        nt = min(128, Nq)
        ng = max(1, Nq // 128)
        TP = bank(BTP); TF = bank(BS[1])
        for i in range(ng):
            P.tr(TP[0:nt, i * 64:(i + 1) * 64], osb_s[0:64, i * 128:i * 128 + nt], identf[0:64, 0:64],
                 (R_os, R_c), (RB[BTP],))
            P.tr(TF[0:nt, i * 65:(i + 1) * 65], ofx_s[0:65, i * 128:i * 128 + nt], identf[0:65, 0:65],
                 (R_os, R_c), (RB[BS[1]],))
        TFv = TF[0:nt, 0:ng * 65].rearrange("p (g c) -> p g c", c=65)
        P.op("dve", lambda e: e.reciprocal(out=rec[0:nt, 0:ng], in_=TFv[:, :, 64]), (RB[BS[1]],), (R_otok,))
        P.cp("dve", otok[0:nt, 0:ng, 0, :], TP[0:nt, 0:ng * 64].rearrange("p (g c) -> p g c", c=64),
             (RB[BTP],), (R_otok,))
        for i in range(ng):
            P.ts("dve", otok[0:nt, i, 1, :], TFv[:, i, 0:64], rec[0:nt, i:i + 1], None, ALU.mult, None,
                 (RB[BS[1]], R_otok), (R_otok,))
        for (d_, s_, rd_) in odst_fn(otok, nt, ng):
            P.dma("sp", d_, s_, (R_otok,), (rd_,))

    hp_count = [0]

    def load_pair(ktsrc, vsrc, qsrc, nkeys, nkb, nq, rk_scr, rv_scr, rq_scr):
        i = hp_count[0] % 2
        hp_count[0] += 1
        for s in range(2):
            for (r0, r1, src) in ktsrc(s):
                P.dma("sp", KTt[s][i][r0:r1, 0:nkeys], src, rk_scr, (R_KT[s][i],))
            for (r0, r1, src) in qsrc(s):
                P.dma("sp", QTt[s][i][r0:r1, 0:nq], src, rq_scr, (R_QT[s][i],))
            P.dma("sp", Vt[s][i][:, 0:nkb, 0:64], vsrc(s).rearrange("(kb k) d -> k kb d", k=128),
                  (rv_scr,), (R_V[s][i],))
        return ([KTt[0][i], KTt[1][i]], [QTt[0][i], QTt[1][i]], [Vt[0][i], Vt[1][i]],
                [R_KT[0][i], R_KT[1][i]], [R_QT[0][i], R_QT[1][i]], [R_V[0][i], R_V[1][i]])

    with nc.allow_non_contiguous_dma(reason="head-sliced V rows / O columns (128-256B segments)"):
        jobs = []
        for p in range(NH):
            def ld(p=p):
                hs = slice((p % 2) * 64, (p % 2) * 64 + 64)
                return load_pair(
                    lambda s: [(0, 64, KTs[p // 2, s, hs, :])] + ([(64, 70, KAs[p])] if s == 1 else []),
                    lambda s: Vs[s, :, p * 64:(p + 1) * 64],
                    lambda s: [(0, 64, QTs[p // 2, s, hs, :])] + ([(64, 70, QAs[p])] if s == 1 else []),
                    T, NKB, TO, (R_scr["KTs"], R_scr["KAs"]), R_scr["Vs"], (R_scr["QTs"], R_scr["QAs"]))

            def sw(ld_, p=p):
                KT, QT, Vl, rK, rQ, rV = ld_
                for z in range(NZ):
                    blocks = []
                    for kbz in range(15, -1, -1):
                        qlo = (kbz // 4) * 128
                        kb = 16 * z + kbz
                        blocks.append((kb * 128, kb, qlo, maskb[:, 0, kbz, qlo:512], maskb[:, 1, kbz, qlo:512]))
                    for kb in range(16 * z - 1, -1, -1):
                        blocks.append((kb * 128, kb, 0, None, None))

                    def odst(ot, nt, ng, z=z, p=p):
                        rows = Os[z * 512:(z + 1) * 512, :].rearrange("(g t) c -> t g c", t=128)
                        return [(rows[:, :, p * 64:(p + 1) * 64], ot[:, :, 0, :], R_scr["Os"]),
                                (rows[:, :, W + p * 64:W + (p + 1) * 64], ot[:, :, 1, :], R_scr["Os"])]

                    sweep(KT, QT, Vl, rK, rQ, rV, z * 512, 512, blocks, odst)
            jobs.append((ld, sw))
        for sbi in range(2):
            for p in range(NH):
                def ld(sbi=sbi, p=p):
                    return load_pair(
                        lambda s: [(0, 64 if s == 0 else 70, KTss[sbi, p, s, 0:(64 if s == 0 else 70), :])],
                        lambda s: Vss[sbi, s, :, p * 64:(p + 1) * 64],
                        lambda s: [(0, 64 if s == 0 else 70, QTss[sbi, p, s, 0:(64 if s == 0 else 70), :])],
                        SK, SKB, NQS, (R_scr["KTss"],), R_scr["Vss"], (R_scr["QTss"],))

                def sw(ld_, sbi=sbi, p=p):
                    KT, QT, Vl, rK, rQ, rV = ld_
                    blocks = [(0, 0, 0, smaskb[:, 0, :], smaskb[:, 1, :])]
                    for kb in range(PB - 1, -1, -1):
                        blocks.append((128 + kb * 128, 1 + kb, 0, None, None))

                    def odst_s(ot, nt, ng, sbi=sbi, p=p):
                        return [(Oss[sbi * 64:(sbi + 1) * 64, p * 64:(p + 1) * 64], ot[0:64, 0, 0, :], R_scr["Oss"]),
                                (Oss[sbi * 64:(sbi + 1) * 64, W + p * 64:W + (p + 1) * 64], ot[0:64, 0, 1, :],
                                 R_scr["Oss"])]

                    sweep(KT, QT, Vl, rK, rQ, rV, 0, NQS, blocks, odst_s)
                jobs.append((ld, sw))
        nxt = jobs[0][0]()
        for ji, (ld, sw) in enumerate(jobs):
            cur_ld = nxt
            if ji + 1 < len(jobs):
                nxt = jobs[ji + 1][0]()
            sw(cur_ld)
    P.barrier()
    AR.release(mB)

    GC = 2
    wob = AR.alloc([128, MC, D], BF16, "wob")
    wgb = AR.alloc([128, KD, DFF], BF16, "wgb")
    wub = AR.alloc([128, KD, DFF], BF16, "wub")
    wdb = AR.alloc([128, FFC, D], BF16, "wdb")
    R_wo, R_wg, R_wu, R_wd = Res("wo"), Res("wg"), Res("wu"), Res("wd")
    for (dst, src, nk, rw_) in ((wob, w_o, MC, R_wo), (wgb, w_gate, KD, R_wg), (wub, w_up, KD, R_wu),
                                (wdb, w_down, FFC, R_wd)):
        sv = src.rearrange("(k p) c -> p k c", p=128)
        for k in range(nk):
            P.dma("pool", dst[:, k, :], sv[:, k, :], (R_in,), (rw_,))
    gp2 = AR.alloc([128, D], F32, "gp2"); gp5 = AR.alloc([128, D], F32, "gp5")
    gfin = AR.alloc([128, D], F32, "gfin")
    R_g = Res("gates")
    P.dma("sp", gp2, adas[0, 2, :].partition_broadcast(128), (R_scr["adas"],), (R_g,))
    P.dma("sp", gp5, adas[0, 5, :].partition_broadcast(128), (R_scr["adas"],), (R_g,))
    P.dma("sp", gfin, g_final.partition_broadcast(128), (R_in,), (R_g,))
    xc = AR.alloc([128, GC, D], F32, "xc")
    ocy = AR.alloc([128, GC, max(D, MIX)], F32, "ocy")
    oc = ocy[:, :, 0:MIX]
    yc = ocy[:, :, 0:D]
    nb_ = AR.alloc([128, GC, max(D, MIX)], BF16, "nb_")
    fT = AR.alloc([128, max(KD, MC), GC * 128], BF16, "fT")
    actT = AR.alloc([128, FFC, GC * 128], BF16, "actT")
    tsq = AR.alloc([128, max(D, MIX)], BF16, "tsq")
    ssC = AR.alloc([128, 8], F32, "ssC")
    tmpf = AR.alloc([128, 512], F32, "tmpf")
    sg = AR.alloc([128, GC * 128], F32, "sg")
    R_xc, R_oc, R_nb, R_fT, R_actT, R_tmpf, R_sg = (Res("xc"), Res("oc", stg=True), Res("nb"), Res("fT"),
                                                   Res("actT"), Res("tmpf"), Res("sg"))
    R_yc = R_oc

    def fvec_ffn(cnd, k):
        return fffn[:, 0, cnd, k:k + 1], fffn[:, 1, cnd, k:k + 1]

    def post_group(xsrc, osrc, ydst, G, conds, g2, g5):
        N = G * 128
        P.dma("sp", xc[:, 0:G, :], xsrc.rearrange("(g p) d -> p g d", p=128), (R_in,), (R_xc,))
        P.dma("sp", oc[:, 0:G, :], osrc.rearrange("(g p) d -> p g d", p=128), (R_scr["Os"], R_scr["Oss"]), (R_oc,))
        for s in range(2):
            norm_to_fm(oc[:, :, s * W:(s + 1) * W], G, W, tsq, ssC, nb_[:, :, 0:W],
                       fT[:, s * WC:(s + 1) * WC, :], [(0, 128, 0)],
                       lambda cnd, k, s=s: (goTt[:, s * WC + k:s * WC + k + 1], None),
                       R_oc, R_nb, R_fT, WC, (0, 1))
        for g in range(G):
            for nb2 in range(D // 512):
                cs = slice(nb2 * 512, (nb2 + 1) * 512)
                bi = 2 + ((g * (D // 512) + nb2) % 2)
                bk = bank(bi)
                for c in range(MC):
                    P.mm(bk, fT[:, c, g * 128:(g + 1) * 128], wob[:, c, cs], c == 0, c == MC - 1, (R_fT, R_wo), (RB[bi],))
                P.tt("dve", tmpf, bk, g2[:, cs], ALU.mult, (RB[bi], R_g), (R_tmpf,))
                P.tt("dve", xc[:, g, cs], xc[:, g, cs], tmpf, ALU.add, (R_xc, R_tmpf), (R_xc,))
        norm_to_fm(xc, G, D, tsq, ssC, nb_[:, :, 0:D], fT[:, 0:KD, :], conds, fvec_ffn, R_xc, R_nb, R_fT, KD, (0, 1))
        for fc in range(FFC):
            bg, bu = 4 + (fc % 2), 6 + (fc % 2)
            for k in range(KD):
                P.mm(bank(bg)[:, 0:N], wgb[:, k, fc * 128:(fc + 1) * 128], fT[:, k, 0:N], k == 0, k == KD - 1,
                     (R_wg, R_fT), (RB[bg],))
            for k in range(KD):
                P.mm(bank(bu)[:, 0:N], wub[:, k, fc * 128:(fc + 1) * 128], fT[:, k, 0:N], k == 0, k == KD - 1,
                     (R_wu, R_fT), (RB[bu],))
            P.act(sg[:, 0:N], bank(bg)[:, 0:N], AF.Silu, (RB[bg],), (R_sg,))
            P.tt("dve", actT[:, fc, 0:N], sg[:, 0:N], bank(bu)[:, 0:N], ALU.mult, (R_sg, RB[bu]), (R_actT,))
        for g in range(G):
            for nb2 in range(D // 512):
                cs = slice(nb2 * 512, (nb2 + 1) * 512)
                bi = 2 + ((g * (D // 512) + nb2) % 2)
                bk = bank(bi)
                for fc in range(FFC):
                    P.mm(bk, actT[:, fc, g * 128:(g + 1) * 128], wdb[:, fc, cs], fc == 0, fc == FFC - 1,
                         (R_actT, R_wd), (RB[bi],))
                P.tt("dve", tmpf, bk, g5[:, cs], ALU.mult, (RB[bi], R_g), (R_tmpf,))
                P.tt("dve", xc[:, g, cs], xc[:, g, cs], tmpf, ALU.add, (R_xc, R_tmpf), (R_xc,))
        for g in range(G):
            P.memset("dve", ssC[:, g:g + 1], 0.0, (R_nb,))
            P.act(tsq[:, 0:D], xc[:, g, :], AF.Square, (R_xc, R_nb), (R_nb,), accum_out=ssC[:, g:g + 1])
        P.ts("dve", ssC[:, 0:G], ssC[:, 0:G], 1.0 / D, EPS, ALU.mult, ALU.add, (R_nb,), (R_nb,))
        P.act(ssC[:, 0:G], ssC[:, 0:G], AF.Ln, (R_nb,), (R_nb,))
        P.act(ssC[:, 0:G], ssC[:, 0:G], AF.Exp, (R_nb,), (R_nb,), scale=-0.5)
        for g in range(G):
            P.stt("dve", yc[:, g, :], xc[:, g, :], ssC[:, g:g + 1], gfin, ALU.mult, ALU.mult, (R_xc, R_nb, R_g), (R_yc,))
        P.dma("sp", ydst.rearrange("(g p) d -> p g d", p=128), yc[:, 0:G, :], (R_yc,), (R_out,))

    for gi in range(TO // (GC * 128)):
        rs = slice(gi * GC * 128, (gi + 1) * GC * 128)
        post_group(xown[rs, :], Os[rs, :], y_own[rs, :], GC, [(0, 128, 0)], gp2, gp5)
    for sbi in range(2):
        P.dma("sp", gp2[sbi * 64:(sbi + 1) * 64, :], adas[1 + sbi, 2, :].partition_broadcast(64),
              (R_scr["adas"],), (R_g,))
        P.dma("sp", gp5[sbi * 64:(sbi + 1) * 64, :], adas[1 + sbi, 5, :].partition_broadcast(64),
              (R_scr["adas"],), (R_g,))
    post_group(xsam, Oss, y_sam, 1, [(0, 64, 1), (64, 128, 2)], gp2, gp5)
    P.barrier()

    print("arena high-water", AR.hw, "of", ARENA_BYTES, "ops", {e: len(P.q[e]) for e in ENGS}, "sems", P.nsem, flush=True)
    blk = es.enter_context(nc.Block())
    P.emit(blk)
    es.close()
    return nc


def make_consts():
    c = np.zeros((128, 640), np.float32)
    s = np.arange(128)[:, None]
    j = np.arange(128)[None, :]
    c[:, 0:128] = (s == j)
    c[:, 128:256] = -(s >= j).astype(np.float32)
    c[:, 256:384] = -(s < j).astype(np.float32)
    return c


def make_masks(r):
    m = np.zeros((128, 2, 16, 512), np.float32)
    j = np.arange(128)[:, None]
    for kbz in range(16):
        for i in range(4):
            key = kbz * 128 + j
            q = (4 * i + r) * 128 + np.arange(128)[None, :]
            m[:, 0, kbz, i * 128:(i + 1) * 128] = np.where(key < q, 0.0, NEG)
            m[:, 1, kbz, i * 128:(i + 1) * 128] = np.where(key <= q, 0.0, NEG)
    sm = np.zeros((128, 2, 64), np.float32)
    q = np.arange(64)[None, :]
    sm[:, 0, :] = np.where((j < q) & (j < 64), 0.0, NEG)
    sm[:, 1, :] = np.where((j <= q) & (j < 64), 0.0, NEG)
    return m.reshape(128, -1), sm.reshape(128, -1)


_NC_CACHE = {}


def run(cfg, inputs):
    D, NH, T, DFF, PAST, DT = cfg["D"], cfg["NH"], cfg["T"], cfg["DFF"], cfg["PAST"], cfg["DT"]
    W = NH * 64
    KD = D // 128
    key = tuple(sorted(cfg.items()))
    nc = build(cfg)
    f = lambda a: np.ascontiguousarray(np.asarray(a, dtype=np.float32))
    xp, xs = f(inputs["x_prompt"]), f(inputs["x_sample"])
    cp, cs = f(inputs["c_prompt"]), f(inputs["c_sample"])
    NSUB = T // 128
    in_maps = []
    cst = make_consts()
    for c in range(8):
        b, r = c // 4, c % 4
        own = np.arange(r, NSUB, 4)
        xo = xp[b].reshape(NSUB, 128, D)[own].reshape(-1, D)
        crows = np.stack([cp[b], cs[2 * c], cs[2 * c + 1]], axis=1)
        cTm = crows.reshape(KD, 128, 3).transpose(1, 0, 2).reshape(128, KD * 3)
        mk, smk = make_masks(r)
        selv = np.zeros((128, 4), np.float32); selv[:, r] = 1.0
        go = np.concatenate([f(inputs["g_sb_out"])[0], f(inputs["g_fox_out"])[0]])
        goT = go.reshape(-1, 128).T
        m = {
            "xfull": xp[b], "xown": xo, "xsam": xs[2 * c:2 * c + 2].reshape(128, D), "cT": cTm,
            "csk": f(inputs["cache_sb_k"])[0, 2 * c:2 * c + 2].reshape(2, PAST, W),
            "csv": f(inputs["cache_sb_v"])[0, 2 * c:2 * c + 2].reshape(2, PAST, W),
            "cfk": f(inputs["cache_fox_k"])[0, 2 * c:2 * c + 2].reshape(2, PAST, W),
            "cfv": f(inputs["cache_fox_v"])[0, 2 * c:2 * c + 2].reshape(2, PAST, W),
            "clfT": f(inputs["cache_fox_logf"])[0, 2 * c:2 * c + 2].transpose(2, 0, 1).reshape(NH, 2 * PAST),
            "w_ada": f(inputs["w_ada"])[0], "b_ada": f(inputs["b_ada"])[0], "w_in": f(inputs["w_in"])[0],
            "w_o": f(inputs["w_o"])[0], "w_gate": f(inputs["w_gate"])[0], "w_up": f(inputs["w_up"])[0],
            "w_down": f(inputs["w_down"])[0], "g_mix": f(inputs["g_mix"])[0], "g_ffn": f(inputs["g_ffn"])[0],
            "g_final": f(inputs["g_final"]), "goT": goT, "b_f": f(inputs["b_f"])[0],
            "cst": cst, "masks": mk, "smask": smk, "sel": selv,
        }
        in_maps.append({k: np.ascontiguousarray(v, dtype=np.float32) for k, v in m.items()})
    if cfg.get("_prep_only"):
        return nc, in_maps
    res = run_bass_kernel_spmd(nc, in_maps, core_ids=list(range(8)))
    R = res.results
    yp = np.zeros((2, T, D), np.float32)
    outs_p = {k: np.zeros((1, 2, T, NH, 64), np.float32) for k in ("o_sbk", "o_sbv", "o_fxk", "o_fxv")}
    lfp = np.zeros((1, 2, T, NH), np.float32)
    ysm = np.zeros((16, DT, D), np.float32)
    outs_s = {k: np.zeros((1, 16, DT, NH, 64), np.float32) for k in ("s_sbk", "s_sbv", "s_fxk", "s_fxv")}
    lfs = np.zeros((1, 16, DT, NH), np.float32)
    for c in range(8):
        b, r = c // 4, c % 4
        own = np.arange(r, NSUB, 4)
        yp[b].reshape(NSUB, 128, D)[own] = R[c]["y_own"].reshape(-1, 128, D)
        for k in outs_p:
            outs_p[k][0, b].reshape(NSUB, 128, NH, 64)[own] = R[c][k].reshape(-1, 128, NH, 64)
        lfp[0, b].reshape(NSUB, 128, NH)[own] = R[c]["o_lf"].reshape(-1, 128, NH)
        ysm[2 * c:2 * c + 2] = R[c]["y_sam"].reshape(2, DT, D)
        for k in outs_s:
            outs_s[k][0, 2 * c:2 * c + 2] = R[c][k].reshape(2, DT, NH, 64)
        lfs[0, 2 * c:2 * c + 2] = R[c]["s_lf"].reshape(2, DT, NH)
    return (yp, ysm, outs_p["o_sbk"], outs_p["o_sbv"], outs_p["o_fxk"], outs_p["o_fxv"], lfp,
            outs_s["s_sbk"], outs_s["s_sbv"], outs_s["s_fxk"], outs_s["s_fxv"], lfs)


def kernel(**inputs):
    return run(CFG_FULL, inputs)
```

```python
import numpy as np
import ml_dtypes
from contextlib import ExitStack
import concourse.bass as bass
import concourse.mybir as mybir
from concourse.bass_utils import run_bass_kernel_spmd

F32 = mybir.dt.float32
BF16 = mybir.dt.bfloat16
U8 = mybir.dt.uint8
AF = mybir.ActivationFunctionType
ALU = mybir.AluOpType
EPS = 1e-6
NEG = -30000.0
ENGS = ("pe", "act", "dve", "pool", "sp")
MAXV = 30000

CFG_FULL = dict(D=1024, NH=8, T=8192, DFF=2816, PAST=1024, DT=64)


class Res:
    __slots__ = ("name", "w", "rs", "cw", "stg")

    def __init__(self, name, cw=False, stg=False):
        self.name = name
        self.w = {}
        self.rs = {}
        self.cw = cw
        self.stg = stg


DMA_SLOTS = {"sp": 4, "pool": 2}


class Prog:
    def __init__(self, nc, es):
        self.nc, self.es = nc, es
        self.q = {e: [] for e in ENGS}
        self.sem = {}
        self.cnt = {}
        self.seen = {e: {} for e in ENGS}
        self.nsem = 0
        for e in ENGS:
            self._rot(e)
        self.slots = {e: [[self._newsem(f"d{e}{k}"), 0] for k in range(n)] for e, n in DMA_SLOTS.items()}
        self.dn = {e: 0 for e in DMA_SLOTS}

    def _newsem(self, nm):
        self.nsem += 1
        return self.es.enter_context(self.nc.semaphore(f"{nm}_{self.nsem}"))

    def _rot(self, e):
        self.sem[e] = self._newsem("s" + e)
        self.cnt[e] = 0

    def _deps(self, eng, reads, writes, cdma=False):
        evs = {}

        def add(ev):
            s, v = ev[0], ev[1]
            k = id(s)
            if k not in evs or evs[k][1] < v:
                evs[k] = (s, v)

        for r in reads:
            for ev in r.w.values():
                add(ev)
        for w in writes:
            for ev in w.w.values():
                if cdma and w.cw and ev[2]:
                    continue
                add(ev)
            for ev in w.rs.values():
                add(ev)
        waits = []
        seen = self.seen[eng]
        for k, (s, v) in evs.items():
            if eng == "pe" and s is self.sem["pe"]:
                continue
            if seen.get(k, 0) >= v:
                continue
            seen[k] = v
            waits.append((s, v))
        return waits

    def _mark(self, ev, reads, writes, cdma=False):
        k = id(ev[0])
        for r in reads:
            r.rs[k] = (ev[0], ev[1])
        for w in writes:
            if cdma and w.cw:
                w.w[k] = (ev[0], ev[1], True)
            else:
                w.w = {k: (ev[0], ev[1], False)}
            w.rs = {}

    def op(self, eng, fn, reads=(), writes=()):
        waits = self._deps(eng, reads, writes)
        if self.cnt[eng] >= MAXV:
            self._rot(eng)
        self.cnt[eng] += 1
        ev = (self.sem[eng], self.cnt[eng])
        self.q[eng].append((waits, fn, (ev[0], 1)))
        self._mark(ev, reads, writes)

    def dma(self, eng, out, in_, reads=(), writes=()):
        nsl = len(self.slots[eng])
        slot = self.slots[eng][self.dn[eng] % nsl]
        self.dn[eng] += 1
        if slot[1] >= MAXV:
            slot[0] = self._newsem(f"d{eng}r")
            slot[1] = 0
        sem = slot[0]
        waits = self._deps(eng, reads, writes, cdma=True)
        if slot[1] > 0 and self.seen[eng].get(id(sem), 0) < slot[1]:
            self.seen[eng][id(sem)] = slot[1]
            waits.append((sem, slot[1]))
        slot[1] += 16
        ev = (sem, slot[1])
        self.q[eng].append((waits, lambda e: e.dma_start(out=out, in_=in_, allow_slow_non_contiguous=True), (sem, 16)))
        self._mark(ev, reads, writes, cdma=True)

    def barrier(self):
        evs = [(self.sem[e], self.cnt[e]) for e in ENGS if self.cnt[e] > 0]
        for e in self.slots:
            for (sm, c) in self.slots[e]:
                if c > 0:
                    evs.append((sm, c))
        for e in ENGS:
            waits = []
            for s, v in evs:
                if s is self.sem[e]:
                    continue
                if self.seen[e].get(id(s), 0) >= v:
                    continue
                self.seen[e][id(s)] = v
                waits.append((s, v))
            if waits:
                self.q[e].append((waits, None, None))

    def emit(self, blk):
        def run(eng_name):
            def body(e):
                for waits, fn, inc in self.q[eng_name]:
                    for s, v in waits:
                        e.wait_ge(s, v)
                    if fn is not None:
                        ins = fn(e)
                        ins.then_inc(inc[0], inc[1])
            return body

        blk.tensor(run("pe"))
        blk.scalar(run("act"))
        blk.vector(run("dve"))
        blk.gpsimd(run("pool"))
        blk.sync(run("sp"))

    def mm(self, out, lhsT, rhs, start, stop, reads, writes):
        self.op("pe", lambda e: e.matmul(out, lhsT=lhsT, rhs=rhs, start=start, stop=stop,
                                         skip_group_check=True), reads, writes)

    def tr(self, out, in_, ident, reads, writes):
        self.op("pe", lambda e: e.transpose(out, in_, ident), reads, writes)

    def act(self, out, in_, func, reads, writes, bias=0.0, scale=1.0, accum_out=None):
        if accum_out is None:
            self.op("act", lambda e: e.activation(out=out, in_=in_, func=func, bias=bias, scale=scale),
                    reads, writes)
        else:
            self.op("act", lambda e: e.activation(out=out, in_=in_, func=func, bias=bias, scale=scale,
                                                  accum_out=accum_out), reads, writes)

    def ts(self, eng, out, in0, s1, s2, op0, op1, reads, writes):
        if s2 is None:
            self.op(eng, lambda e: e.tensor_scalar(out=out, in0=in0, scalar1=s1, scalar2=None, op0=op0),
                    reads, writes)
        else:
            self.op(eng, lambda e: e.tensor_scalar(out=out, in0=in0, scalar1=s1, scalar2=s2, op0=op0, op1=op1),
                    reads, writes)

    def tt(self, eng, out, in0, in1, op, reads, writes):
        self.op(eng, lambda e: e.tensor_tensor(out=out, in0=in0, in1=in1, op=op), reads, writes)

    def stt(self, eng, out, in0, scalar, in1, op0, op1, reads, writes):
        self.op(eng, lambda e: e.scalar_tensor_tensor(out=out, in0=in0, scalar=scalar, in1=in1, op0=op0, op1=op1),
                reads, writes)

    def cp(self, eng, out, in_, reads, writes):
        self.op(eng, lambda e: e.tensor_copy(out=out, in_=in_), reads, writes)

    def memset(self, eng, ap, val, writes):
        self.op(eng, lambda e: e.memset(ap, val), (), writes)


class Arena:
    def __init__(self, t, nbytes):
        self.t, self.n, self.off = t, nbytes, 0

    def mark(self):
        return self.off

    def release(self, m):
        self.off = m

    def alloc(self, shape, dt, name=""):
        esz = 4 if dt == F32 else 2
        n = 1
        for s in shape[1:]:
            n *= s
        nb = (n * esz + 63) // 64 * 64
        assert self.off + nb <= self.n, f"arena overflow {name}: {self.off}+{nb}>{self.n}"
        self.hw = max(getattr(self, "hw", 0), self.off + nb)
        v = self.t[:, self.off:self.off + n * esz].bitcast(dt)
        self.off += nb
        if len(shape) == 3:
            v = v.rearrange("p (a b) -> p a b", a=shape[1])
        elif len(shape) == 4:
            v = v.rearrange("p (a b c) -> p a b c", a=shape[1], b=shape[2])
        if shape[0] < 128:
            v = v[0:shape[0]]
        return v


def build(cfg):
    D, NH, T, DFF, PAST, DT = cfg["D"], cfg["NH"], cfg["T"], cfg["DFF"], cfg["PAST"], cfg["DT"]
    KD = D // 128
    W = NH * 64
    WC = W // 128
    MIX = 2 * W
    MC = MIX // 128
    IN = 6 * W + NH
    FFC = DFF // 128
    NZ = T // 2048
    TO = T // 4
    NSUBO = TO // 128
    PB = PAST // 128
    SKB = PB + 1
    SK = SKB * 128
    NQS = DT
    assert DT == 64 and T % 2048 == 0 and D % 512 == 0 and W % 128 == 0
    CQ, CK, CV = 0, W, 2 * W
    FQ, FK, FV, LF = 3 * W, 4 * W, 5 * W, 6 * W

    nc = bass.Bass("TRN2", target_bir_lowering=False)

    def din(name, shape):
        return nc.dram_tensor(name, list(shape), F32, kind="ExternalInput").ap()

    def dout(name, shape):
        return nc.dram_tensor(name, list(shape), F32, kind="ExternalOutput").ap()

    def dscr(name, shape, dt):
        return nc.dram_tensor(name, list(shape), dt, kind="Internal").ap()

    xfull = din("xfull", (T, D)); xown = din("xown", (TO, D)); xsam = din("xsam", (128, D))
    cT = din("cT", (128, KD * 3))
    csk = din("csk", (2, PAST, W)); csv = din("csv", (2, PAST, W))
    cfk = din("cfk", (2, PAST, W)); cfv = din("cfv", (2, PAST, W)); clfT = din("clfT", (NH, 2 * PAST))
    w_ada = din("w_ada", (D, 6 * D)); b_ada = din("b_ada", (6 * D,))
    w_in = din("w_in", (D, IN)); w_o = din("w_o", (MIX, D))
    w_gate = din("w_gate", (D, DFF)); w_up = din("w_up", (D, DFF)); w_down = din("w_down", (DFF, D))
    g_mix = din("g_mix", (D,)); g_ffn = din("g_ffn", (D,)); g_final = din("g_final", (D,))
    goT = din("goT", (128, MC)); b_f = din("b_f", (NH,))
    cst = din("cst", (128, 640)); masks = din("masks", (128, 2 * 16 * 512)); smask = din("smask", (128, 2 * 64))
    sel = din("sel", (128, 4))

    y_own = dout("y_own", (TO, D)); y_sam = dout("y_sam", (128, D))
    o_sbk = dout("o_sbk", (TO, W)); o_sbv = dout("o_sbv", (TO, W)); o_fxk = dout("o_fxk", (TO, W))
    o_fxv = dout("o_fxv", (TO, W)); o_lf = dout("o_lf", (TO, NH))
    s_sbk = dout("s_sbk", (128, W)); s_sbv = dout("s_sbv", (128, W)); s_fxk = dout("s_fxk", (128, W))
    s_fxv = dout("s_fxv", (128, W)); s_lf = dout("s_lf", (128, NH))

    KTs = dscr("KTs", (NH // 2, 2, 128, T), BF16)
    KAs = dscr("KAs", (NH, 6, T), BF16)
    QAs = dscr("QAs", (NH, 6, TO), BF16)
    Vs = dscr("Vs", (2, T, W), BF16)
    QTs = dscr("QTs", (NH // 2, 2, 128, TO), BF16)
    Os = dscr("Os", (TO, MIX), F32)
    KTss = dscr("KTss", (2, NH, 2, 70, SK), BF16)
    Vss = dscr("Vss", (2, 2, SK, W), BF16)
    QTss = dscr("QTss", (2, NH, 2, 70, NQS), BF16)
    Oss = dscr("Oss", (128, MIX), F32)
    adas = dscr("adas", (3, 6, D), F32)
    wos = dscr("wos", (MIX, D), BF16); wgs = dscr("wgs", (D, DFF), BF16)
    wus = dscr("wus", (D, DFF), BF16); wds = dscr("wds", (DFF, D), BF16)

    es = ExitStack()
    ARENA_BYTES = cfg.get("ARENA", 207 * 1024)
    arena_t = es.enter_context(nc.sbuf_tensor("arena", [128, ARENA_BYTES], U8))
    ps = es.enter_context(nc.psum_tensor("ps", [128, 4096], F32))
    P = Prog(nc, es)
    AR = Arena(arena_t, ARENA_BYTES)

    def bank(i):
        return ps[:, i * 512:(i + 1) * 512]

    def bank_bf(i):
        return ps[:, i * 512:(i + 1) * 512].bitcast(BF16)

    RB = [Res(f"bank{i}") for i in range(8)]
    R_scr = {k: Res(k) for k in ("KTs", "Vs", "QTs", "Os", "KTss", "Vss", "QTss", "Oss", "adas", "KAs", "QAs")}
    R_in = Res("inputs")
    R_out = Res("outputs")

    identf = AR.alloc([128, 128], F32, "identf")
    identb = AR.alloc([128, 128], BF16, "identb")
    triIb = AR.alloc([128, 128], BF16, "triI")
    compb = AR.alloc([128, 128], BF16, "comp")
    zrow = AR.alloc([128, 128], BF16, "zrow")
    onesb = AR.alloc([128, 512], BF16, "onesb")
    selt = AR.alloc([128, 4], F32, "sel")
    goTt = AR.alloc([128, MC], F32, "goT")
    bft = AR.alloc([128, 1], F32, "bft")
    nbft = AR.alloc([128, 1], F32, "nbft")
    bfrow = AR.alloc([128, NH], F32, "bfrow")
    scT = AR.alloc([128, KD, 3], BF16, "scT")
    fmix = AR.alloc([128, 2, 3, KD], F32, "fmix")
    fffn = AR.alloc([128, 2, 3, KD], F32, "fffn")
    R_c = Res("consts")
    R_f = Res("fvecs")

    P.dma("sp", identf, cst[:, 0:128], (R_in,), (R_c,))
    P.dma("pool", identb, cst[:, 0:128], (R_in,), (R_c,))
    P.dma("pool", triIb, cst[:, 128:256], (R_in,), (R_c,))
    P.dma("pool", compb, cst[:, 256:384], (R_in,), (R_c,))
    P.dma("pool", zrow, cst[:, 384:512], (R_in,), (R_c,))
    P.dma("sp", selt, sel, (R_in,), (R_c,))
    P.dma("sp", goTt, goT, (R_in,), (R_c,))
    P.dma("sp", bft[0:NH, :], b_f.rearrange("(h o) -> h o", o=1), (R_in,), (R_c,))
    P.dma("sp", bfrow, b_f.partition_broadcast(128), (R_in,), (R_c,))
    P.memset("dve", onesb, 1.0, (R_c,))
    P.ts("dve", nbft[0:NH, :], bft[0:NH, :], -1.0, None, ALU.mult, None, (R_c,), (R_c,))

    m0 = AR.mark()
    ctile = AR.alloc([128, KD * 3], F32, "ctile")
    ctmp = AR.alloc([128, KD * 3], F32, "ctmp")
    arow = AR.alloc([128, 6 * D], F32, "arow")[0:3]
    brow = AR.alloc([128, 6 * D], F32, "brow")[0:3]
    grow = AR.alloc([128, 2 * D], F32, "grow")[0:3]
    wadab = [AR.alloc([128, KD, 512], BF16, f"wada{i}") for i in range(2)]
    R_ct, R_arow, R_wada = Res("ct"), Res("arow", stg=True), [Res("wada0"), Res("wada1")]
    P.dma("sp", ctile, cT, (R_in,), (R_ct,))
    P.dma("sp", brow, b_ada.partition_broadcast(3), (R_in,), (R_arow,))
    P.dma("sp", grow[:, 0:D], g_mix.partition_broadcast(3), (R_in,), (R_arow,))
    P.dma("sp", grow[:, D:2 * D], g_ffn.partition_broadcast(3), (R_in,), (R_arow,))
    P.act(ctmp, ctile, AF.Exp, (R_ct,), (R_ct,), scale=-1.0)
    P.ts("dve", ctmp, ctmp, 1.0, None, ALU.add, None, (R_ct,), (R_ct,))
    P.op("dve", lambda e: e.reciprocal(out=ctmp, in_=ctmp), (R_ct,), (R_ct,))
    P.tt("dve", scT.rearrange("p k c -> p (k c)"), ctile, ctmp, ALU.mult, (R_ct,), (R_c,))
    wadav = w_ada.rearrange("(k p) c -> p k c", p=128)
    NAC = 6 * D // 512
    for j in range(NAC):
        wb, rw = wadab[j % 2], R_wada[j % 2]
        P.dma("pool", wb, wadav[:, :, j * 512:(j + 1) * 512], (R_in,), (rw,))
        bk = bank(j % 2)
        for k in range(KD):
            P.mm(bk[0:3, :], scT[:, k, :], wb[:, k, :], k == 0, k == KD - 1, (rw, R_c), (RB[j % 2],))
        P.tt("dve", arow[:, j * 512:(j + 1) * 512], bk[0:3, :], brow[:, j * 512:(j + 1) * 512], ALU.add,
             (RB[j % 2], R_arow), (R_arow,))
    for idx, gsrc in ((1, 0), (4, 1)):
        P.stt("dve", arow[:, idx * D:(idx + 1) * D], arow[:, idx * D:(idx + 1) * D], 1.0,
              grow[:, gsrc * D:(gsrc + 1) * D], ALU.add, ALU.mult, (R_arow,), (R_arow,))
    for idx in (2, 5):
        P.ts("dve", arow[:, idx * D:(idx + 1) * D], arow[:, idx * D:(idx + 1) * D], 1.0, None, ALU.add, None,
             (R_arow,), (R_arow,))
    P.dma("sp", adas.rearrange("c s d -> c (s d)"), arow, (R_arow,), (R_scr["adas"],))
    with nc.allow_non_contiguous_dma(reason="tiny feature-major vector loads"):
        for cnd in range(3):
            for (dst, a_sc, a_sh) in ((fmix, 1, 0), (fffn, 4, 3)):
                P.dma("sp", dst[:, 0, cnd, :], adas[cnd, a_sc, :].rearrange("(k p) -> p k", p=128),
                      (R_scr["adas"],), (R_f,))
                P.dma("sp", dst[:, 1, cnd, :], adas[cnd, a_sh, :].rearrange("(k p) -> p k", p=128),
                      (R_scr["adas"],), (R_f,))
    P.barrier()
    AR.release(m0)

    def norm_to_fm(xt, G, width, tmpsq, ss, xs_b, hT, conds, fvec, Rx, Rtmp, RhT, nchunk, tbank, part="all"):
        if part in ("all", "pre"):
            for g in range(G):
                P.memset("dve", ss[:, g:g + 1], 0.0, (Rtmp,))
                P.act(tmpsq[:, 0:width], xt[:, g, :], AF.Square, (Rx, Rtmp), (Rtmp,), accum_out=ss[:, g:g + 1])
            P.ts("dve", ss[:, 0:G], ss[:, 0:G], 1.0 / width, EPS, ALU.mult, ALU.add, (Rtmp,), (Rtmp,))
            P.act(ss[:, 0:G], ss[:, 0:G], AF.Ln, (Rtmp,), (Rtmp,))
            P.act(ss[:, 0:G], ss[:, 0:G], AF.Exp, (Rtmp,), (Rtmp,), scale=-0.5)
            for g in range(G):
                P.ts("dve", xs_b[:, g, :], xt[:, g, :], ss[:, g:g + 1], None, ALU.mult, None, (Rx, Rtmp), (Rtmp,))
        if part == "pre":
            return
        for k in range(nchunk):
            bi = tbank[k % len(tbank)]
            pb = bank_bf(bi)
            for g in range(G):
                P.tr(pb[:, g * 128:(g + 1) * 128], xs_b[:, g, k * 128:(k + 1) * 128], identb, (Rtmp, R_c), (RB[bi],))
            for (lo, hi, cnd) in conds:
                sc_ap, sh_ap = fvec(cnd, k)
                src = pb[:, 0:G * 128].rearrange("p (g t) -> p g t", g=G)[:, :, lo:hi]
                dst = hT[:, k, 0:G * 128].rearrange("p (g t) -> p g t", g=G)[:, :, lo:hi]
                if sh_ap is None:
                    P.ts("dve", dst, src, sc_ap, None, ALU.mult, None, (RB[bi], R_f, R_c), (RhT,))
                else:
                    P.ts("dve", dst, src, sc_ap, sh_ap, ALU.mult, ALU.add, (RB[bi], R_f, R_c), (RhT,))

    zbig = AR.alloc([128, 512], BF16, "zbig")
    mL = AR.mark()
    LGT = AR.alloc([128, T], F32, "LGT")[0:NH]
    LGS = AR.alloc([128, 2 * (PAST + DT)], F32, "LGS")[0:NH]
    LGSv = LGS.rearrange("h (s n) -> h s n", s=2)
    lgtok = AR.alloc([128, NSUBO + 1, NH], F32, "lgtok")
    R_LGT, R_LGS, R_lgtok = Res("LGT"), Res("LGS"), Res("lgtok", stg=True)
    P.memset("dve", zbig, 0.0, (R_c,))
    mA = AR.mark()
    winb = AR.alloc([128, KD, IN], BF16, "winb")
    R_win = Res("win")
    winv = w_in.rearrange("(k p) c -> p k c", p=128)
    for k in range(KD):
        P.dma("pool", winb[:, k, :], winv[:, k, :], (R_in,), (R_win,))
    R_wsc = {"wo": Res("wos"), "wg": Res("wgs"), "wu": Res("wus"), "wd": Res("wds")}
    for (nm, dst_, src_) in (("wo", wos, w_o), ("wg", wgs, w_gate), ("wu", wus, w_up), ("wd", wds, w_down)):
        nr = src_.shape[0]
        for r0 in range(0, nr, 256):
            P.dma("pool", dst_[r0:min(nr, r0 + 256), :], src_[r0:min(nr, r0 + 256), :], (R_in,), (R_wsc[nm],))
    GA = 4
    xt2 = [AR.alloc([128, GA, D], F32, f"xt{i}") for i in range(2)]
    R_xt = [Res("xt0"), Res("xt1")]
    tmpsq = AR.alloc([128, D], F32, "tmpsq")
    ssA = AR.alloc([128, 8], F32, "ssA")
    xs_b2 = [AR.alloc([128, GA, D], BF16, f"xs_b{i}") for i in range(2)]
    hTA2 = [AR.alloc([128, KD, GA * 128], BF16, f"hTA{i}") for i in range(2)]
    R_tmpA2, R_hTA2 = [Res("tmpA0"), Res("tmpA1")], [Res("hTA0"), Res("hTA1")]
    cur = {"i": 0}

    class _H:
        def __getitem__(self, key):
            return hTA2[cur["i"]][key]
    hTA = _H()

    class _R:
        pass

    NST = 2
    fmo = [AR.alloc([128, 512], BF16, f"fmo{i}") for i in range(NST)]
    R_fmo = [Res(f"fmo{i}", stg=True) for i in range(NST)]
    tmo = [AR.alloc([128, 512], BF16, f"tmo{i}") for i in range(NST)]
    R_tmo = [Res(f"tmo{i}", stg=True) for i in range(NST)]
    tmf = [AR.alloc([128, 512], F32, f"tmf{i}") for i in range(NST)]
    R_tmf = [Res(f"tmf{i}", stg=True) for i in range(NST)]
    cnt = {"fm": 0, "tm": 0, "tf": 0, "x": 0}

    def fvec_mix(cnd, k):
        return fmix[:, 0, cnd, k:k + 1], fmix[:, 1, cnd, k:k + 1]

    def fm_chunk(N, col0, scale):
        bi = 2 + (cnt["fm"] % 2)
        bk = bank(bi)
        for k in range(KD):
            P.mm(bk[:, 0:N], winb[:, k, col0:col0 + 128], hTA[:, k, 0:N], k == 0, k == KD - 1,
                 (R_win, R_hTA2[cur["i"]]), (RB[bi],))
        i = cnt["fm"] % NST
        cnt["fm"] += 1
        P.act(fmo[i][:, 0:N], bk[:, 0:N], AF.Copy, (RB[bi],), (R_fmo[i],), scale=scale)
        return fmo[i], R_fmo[i]

    def tm_block(g, col0, ncols):
        bi = 4 + (cnt["tm"] % 2)
        cnt["tm"] += 1
        bk = bank(bi)
        for k in range(KD):
            P.mm(bk[:, 0:ncols], hTA[:, k, g * 128:(g + 1) * 128], winb[:, k, col0:col0 + ncols], k == 0,
                 k == KD - 1, (R_win, R_hTA2[cur["i"]]), (RB[bi],))
        return bk, bi

    def to_bf(bk, bi, ncols):
        i = cnt["tf"] % NST
        cnt["tf"] += 1
        P.cp("dve", tmo[i][:, 0:ncols], bk[:, 0:ncols], (RB[bi],), (R_tmo[i],))
        return tmo[i], R_tmo[i]

    def to_f32(bk, bi, ncols):
        i = cnt["tf"] % NST
        cnt["tf"] += 1
        P.act(tmf[i][:, 0:ncols], bk[:, 0:ncols], AF.Copy, (RB[bi],), (R_tmf[i],))
        return tmf[i], R_tmf[i]

    ssA2 = [ssA, AR.alloc([128, 8], F32, "ssA1")]
    tmpsq2 = [tmpsq, AR.alloc([128, D], BF16, "tmpsq1")]

    def front(xsrc, G, conds, i, part):
        if part == "pre":
            P.dma("pool", xt2[i][:, 0:G, :], xsrc.rearrange("(g p) d -> p g d", p=128), (R_in,), (R_xt[i],))
        norm_to_fm(xt2[i], G, D, tmpsq2[i], ssA2[i], xs_b2[i], hTA2[i], conds, fvec_mix, R_xt[i], R_tmpA2[i],
                   R_hTA2[i], KD, (0, 1), part=part)

    def logits_fm(N):
        bk = bank(6)
        for k in range(KD):
            P.mm(bk[0:NH, 0:N], winb[:, k, LF:LF + NH], hTA[:, k, 0:N], k == 0, k == KD - 1,
                 (R_win, R_hTA2[cur["i"]]), (RB[6],))
        return bk

    def logits_tm(g, slot):
        bk = bank(7)
        for k in range(KD):
            P.mm(bk[:, 0:NH], hTA[:, k, g * 128:(g + 1) * 128], winb[:, k, LF:LF + NH], k == 0, k == KD - 1,
                 (R_win, R_hTA2[cur["i"]]), (RB[7],))
        P.tt("dve", lgtok[:, slot, :], bk[:, 0:NH], bfrow, ALU.add, (RB[7], R_c), (R_lgtok,))

    CBLK = [(c0, min(512, W - c0)) for c0 in range(0, W, 512)]

    tiles = []

    def a1_p1(tt_):
        for s_, col0 in ((0, CK), (1, FK)):
            for c in range(WC):
                t_, r_ = fm_chunk(512, col0 + c * 128, 1.0)
                P.dma("sp", KTs[c, s_, :, tt_ * 512:(tt_ + 1) * 512], t_[:, 0:512], (r_,), (R_scr["KTs"],))

    def a1_p2(tt_):
        for g in range(GA):
            tok0 = tt_ * 512 + g * 128
            for s_, col0 in ((0, CV), (1, FV)):
                for (c0, nn) in CBLK:
                    bk, bi = tm_block(g, col0 + c0, nn)
                    t_, r_ = to_bf(bk, bi, nn)
                    P.dma("sp", Vs[s_, tok0:tok0 + 128, c0:c0 + nn], t_[:, 0:nn], (r_,), (R_scr["Vs"],))
        bk = logits_fm(512)
        P.cp("dve", LGT[:, tt_ * 512:(tt_ + 1) * 512], bk[0:NH, 0:512], (RB[6],), (R_LGT,))

    for tt_ in range(T // 512):
        tiles.append((xfull[tt_ * 512:(tt_ + 1) * 512, :], GA, [(0, 128, 0)],
                      (lambda tt_=tt_: a1_p1(tt_)), (lambda tt_=tt_: a1_p2(tt_))))

    def a2_p1(z):
        for s_, col0 in ((0, CQ), (1, FQ)):
            for c in range(WC):
                t_, r_ = fm_chunk(512, col0 + c * 128, 0.125)
                P.dma("sp", QTs[c, s_, :, z * 512:(z + 1) * 512], t_[:, 0:512], (r_,), (R_scr["QTs"],))

    def a2_p2(z):
        for g in range(GA):
            r0 = z * 512 + g * 128
            for (col0, od) in ((CK, o_sbk), (CV, o_sbv), (FK, o_fxk), (FV, o_fxv)):
                for (c0, nn) in CBLK:
                    bk, bi = tm_block(g, col0 + c0, nn)
                    t_, r_ = to_f32(bk, bi, nn)
                    P.dma("sp", od[r0:r0 + 128, c0:c0 + nn], t_[:, 0:nn], (r_,), (R_out,))
            logits_tm(g, z * 4 + g)

    for z in range(NZ):
        tiles.append((xown[z * 512:(z + 1) * 512, :], GA, [(0, 128, 0)],
                      (lambda z=z: a2_p1(z)), (lambda z=z: a2_p2(z))))

    def as_p1():
        for (s_, colq, colk) in ((0, CQ, CK), (1, FQ, FK)):
            for kind, col0, scale in (("q", colq, 0.125), ("k", colk, 1.0)):
                for c in range(WC):
                    t_, r_ = fm_chunk(128, col0 + c * 128, scale)
                    for half in range(2):
                        h = 2 * c + half
                        for sbi in range(2):
                            src = t_[half * 64:(half + 1) * 64, sbi * 64:(sbi + 1) * 64]
                            if kind == "q":
                                P.dma("sp", QTss[sbi, h, s_, 0:64, :], src, (r_,), (R_scr["QTss"],))
                            else:
                                P.dma("sp", KTss[sbi, h, s_, 0:64, 0:64], src, (r_,), (R_scr["KTss"],))

    def as_p2():
        for (col0, od) in ((CK, s_sbk), (CV, s_sbv), (FK, s_fxk), (FV, s_fxv)):
            for (c0, nn) in CBLK:
                bk, bi = tm_block(0, col0 + c0, nn)
                t_, r_ = to_f32(bk, bi, nn)
                P.dma("sp", od[0:128, c0:c0 + nn], t_[:, 0:nn], (r_,), (R_out,))
                if col0 in (CV, FV):
                    sidx = 0 if col0 == CV else 1
                    t2, r2 = to_bf(bk, bi, nn)
                    for sbi in range(2):
                        P.dma("sp", Vss[sbi, sidx, 0:64, c0:c0 + nn], t2[sbi * 64:(sbi + 1) * 64, 0:nn],
                              (r2,), (R_scr["Vss"],))
        logits_tm(0, NSUBO)
        bk = logits_fm(128)
        for sbi in range(2):
            P.cp("dve", LGSv[:, sbi, PAST:PAST + DT], bk[0:NH, sbi * 64:(sbi + 1) * 64], (RB[6],), (R_LGS,))

    tiles.append((xsam, 1, [(0, 64, 1), (64, 128, 2)], as_p1, as_p2))

    front(tiles[0][0], tiles[0][1], tiles[0][2], 0, "pre")
    front(tiles[0][0], tiles[0][1], tiles[0][2], 0, "tr")
    for ti, (xsrc_, G_, conds_, p1_, p2_) in enumerate(tiles):
        nx = tiles[ti + 1] if ti + 1 < len(tiles) else None
        if nx is not None:
            front(nx[0], nx[1], nx[2], (ti + 1) % 2, "pre")
        cur["i"] = ti % 2
        p1_()
        if nx is not None:
            front(nx[0], nx[1], nx[2], (ti + 1) % 2, "tr")
        cur["i"] = ti % 2
        p2_()
    lgflat = lgtok.rearrange("p a h -> p (a h)")
    P.act(lgflat, lgflat, AF.Exp, (R_lgtok,), (R_lgtok,), scale=-1.0)
    P.act(lgflat, lgflat, AF.Ln, (R_lgtok,), (R_lgtok,), bias=1.0)
    P.ts("dve", lgflat, lgflat, -1.0, None, ALU.mult, None, (R_lgtok,), (R_lgtok,))
    with nc.allow_non_contiguous_dma(reason="small logf rows"):
        P.dma("sp", o_lf.rearrange("(a p) h -> p a h", p=128), lgtok[:, 0:NSUBO, :], (R_lgtok,), (R_out,))
        P.dma("sp", s_lf, lgtok[:, NSUBO, :], (R_lgtok,), (R_out,))

    ktin = AR.alloc([128, PB, W], F32, "ktin")
    ktb = AR.alloc([128, PAST], BF16, "ktb")
    R_ktin, R_ktb = Res("ktin"), Res("ktb", stg=True)
    for sbi in range(2):
        for (s, ck, cv) in ((0, csk, csv), (1, cfk, cfv)):
            P.dma("pool", Vss[sbi, s, 128:128 + PAST, :], cv[sbi], (R_in,), (R_scr["Vss"],))
            P.dma("sp", Vss[sbi, s, 64:128, :], zbig[0:64, 0:W], (R_c,), (R_scr["Vss"],))
            P.dma("sp", ktin, ck[sbi].rearrange("(kb p) w -> p kb w", p=128), (R_in,), (R_ktin,))
            for c in range(WC):
                for kb0 in range(0, PB, 4):
                    nk = min(4, PB - kb0)
                    bi = 2 + (cnt["fm"] % 2)
                    cnt["fm"] += 1
                    bk = bank(bi)
                    for j in range(nk):
                        P.tr(bk[:, j * 128:(j + 1) * 128], ktin[:, kb0 + j, c * 128:(c + 1) * 128], identf,
                             (R_ktin, R_c), (RB[bi],))
                    P.act(ktb[:, kb0 * 128:(kb0 + nk) * 128], bk[:, 0:nk * 128], AF.Copy, (RB[bi],), (R_ktb,))
                for half in range(2):
                    P.dma("sp", KTss[sbi, 2 * c + half, s, 0:64, 128:128 + PAST], ktb[half * 64:(half + 1) * 64, :],
                          (R_ktb,), (R_scr["KTss"],))
            P.dma("sp", KTss[sbi, :, s, :, 64:128].rearrange("h r c -> r h c"),
                  zbig[0:70, 0:NH * 64].rearrange("r (h c) -> r h c", h=NH), (R_c,), (R_scr["KTss"],))
    P.barrier()
    AR.release(mA)

    mF = AR.mark()
    Gp = AR.alloc([128, T], F32, "Gp")[0:NH]
    Go = AR.alloc([128, TO], F32, "Go")[0:NH]
    Gs = AR.alloc([128, 2 * (PAST + DT)], F32, "Gs")[0:NH]
    Gsv = Gs.rearrange("h (s n) -> h s n", s=2)
    Gq = AR.alloc([128, 2 * DT], F32, "Gq")[0:NH]
    Gqv = Gq.rearrange("h (s n) -> h s n", s=2)
    pbuf = [AR.alloc([128, max(T, 2 * (PAST + DT))], BF16, f"pbuf{i}")[0:NH] for i in range(2)]
    R_G, R_Go, R_Gs, R_Gq = Res("G"), Res("Go"), Res("Gs"), Res("Gq")
    R_pb = [Res("pb0", stg=True), Res("pb1", stg=True)]
    pcount = [0]

    def cumsum(dst, src, n, rsrc, rdst):
        for c0 in range(0, n, 512):
            nn = min(512, n - c0)
            init = 0.0 if c0 == 0 else dst[:, c0 - 1:c0]
            P.op("dve", lambda e, c0=c0, nn=nn, init=init: e.tensor_tensor_scan(
                out=dst[:, c0:c0 + nn], data0=onesb[0:NH, 0:nn], data1=src[:, c0:c0 + nn], initial=init,
                op0=ALU.mult, op1=ALU.add), (rsrc, rdst, R_c), (rdst,))

    def split_rows(src, n, rsrc, dst_fn):
        for j in range(3):
            i = pcount[0] % 2
            pcount[0] += 1
            pb = pbuf[i]
            P.cp("dve", pb[:, 0:n], src, (rsrc,), (R_pb[i],))
            if j < 2:
                P.tt("dve", src, src, pb[:, 0:n], ALU.subtract, (rsrc, R_pb[i]), (rsrc,))
            for (d_, s_, rd_) in dst_fn(j, pb):
                P.dma("sp", d_, s_, (R_pb[i],), (rd_,))

    def ones_rows(n, dst_list):
        i = pcount[0] % 2
        pcount[0] += 1
        P.memset("dve", pbuf[i][:, 0:n], 1.0, (R_pb[i],))
        for (d_, rd_) in dst_list:
            P.dma("sp", d_, pbuf[i][:, 0:d_.shape[-1]], (R_pb[i],), (rd_,))

    P.act(LGT, LGT, AF.Exp, (R_LGT, R_c), (R_LGT,), scale=-1.0, bias=nbft[0:NH, :])
    P.act(LGT, LGT, AF.Ln, (R_LGT,), (R_LGT,), bias=1.0)
    cumsum(Gp, LGT, T, R_LGT, R_G)
    Gv = Gp.rearrange("h (m r t) -> h m r t", r=4, t=128)
    Gov = Go.rearrange("h (m t) -> h m t", t=128)
    P.ts("dve", Gov, Gv[:, :, 0, :], selt[0:NH, 0:1], None, ALU.mult, None, (R_G, R_c), (R_Go,))
    for rr in range(1, 4):
        P.stt("dve", Gov, Gv[:, :, rr, :], selt[0:NH, rr:rr + 1], Gov, ALU.mult, ALU.add, (R_G, R_c, R_Go), (R_Go,))
    P.ts("dve", Go, Go, -1.0, None, ALU.mult, None, (R_Go,), (R_Go,))
    split_rows(Gp, T, R_G, lambda j, pb: [(KAs[:, 3 + j, :], pb[:, 0:T], R_scr["KAs"])])
    split_rows(Go, TO, R_Go, lambda j, pb: [(QAs[:, j, :], pb[:, 0:TO], R_scr["QAs"])])
    ones_rows(T, [(KAs[:, j, :], R_scr["KAs"]) for j in range(3)] +
              [(QAs[:, 3 + j, :], R_scr["QAs"]) for j in range(3)])
    P.dma("sp", LGSv[:, :, 0:PAST], clfT.rearrange("h (s n) -> h s n", s=2), (R_in,), (R_LGS,))
    P.ts("dve", LGSv[:, :, 0:PAST], LGSv[:, :, 0:PAST], -1.0, None, ALU.mult, None, (R_LGS,), (R_LGS,))
    P.act(LGSv[:, :, PAST:PAST + DT], LGSv[:, :, PAST:PAST + DT], AF.Exp, (R_LGS, R_c), (R_LGS,), scale=-1.0,
          bias=nbft[0:NH, :])
    P.act(LGSv[:, :, PAST:PAST + DT], LGSv[:, :, PAST:PAST + DT], AF.Ln, (R_LGS,), (R_LGS,), bias=1.0)
    for sbi in range(2):
        cumsum(Gsv[:, sbi, :], LGSv[:, sbi, :], PAST + DT, R_LGS, R_Gs)
    P.ts("dve", Gqv, Gsv[:, :, PAST:PAST + DT], -1.0, None, ALU.mult, None, (R_Gs,), (R_Gq,))

    def kdst_s(j, pb):
        pv = pb[:, 0:2 * (PAST + DT)].rearrange("h (s n) -> h s n", s=2)
        out = []
        for sbi in range(2):
            out.append((KTss[sbi, :, 1, 67 + j, 0:DT], pv[:, sbi, PAST:PAST + DT], R_scr["KTss"]))
            out.append((KTss[sbi, :, 1, 67 + j, 128:128 + PAST], pv[:, sbi, 0:PAST], R_scr["KTss"]))
        return out

    def qdst_s(j, pb):
        pv = pb[:, 0:2 * DT].rearrange("h (s n) -> h s n", s=2)
        return [(QTss[sbi, :, 1, 64 + j, :], pv[:, sbi, :], R_scr["QTss"]) for sbi in range(2)]

    split_rows(Gs, 2 * (PAST + DT), R_Gs, kdst_s)
    split_rows(Gq, 2 * DT, R_Gq, qdst_s)
    ones_rows(SK, [(KTss[sbi, :, 1, 64 + j, :], R_scr["KTss"]) for sbi in range(2) for j in range(3)] +
              [(QTss[sbi, :, 1, 67 + j, :], R_scr["QTss"]) for sbi in range(2) for j in range(3)])
    P.barrier()
    AR.release(mL)

    mB = AR.mark()
    maskb = AR.alloc([128, 2, 16, 512], BF16, "maskb")
    smaskb = AR.alloc([128, 2, 64], BF16, "smaskb")
    R_mask = Res("mask")
    mv = masks.rearrange("p (s k q) -> p s k q", s=2, k=16)
    for s in range(2):
        for k4 in range(0, 16, 4):
            P.dma("pool", maskb[:, s, k4:k4 + 4, :], mv[:, s, k4:k4 + 4, :], (R_in,), (R_mask,))
    P.dma("pool", smaskb, smask.rearrange("p (s q) -> p s q", s=2), (R_in,), (R_mask,))
    NKB = T // 128
    KTt = [[AR.alloc([128, T], BF16, f"KTt{s}{i}") for i in range(2)] for s in range(2)]
    Vt = [[AR.alloc([128, NKB, 128], BF16, f"Vt{s}{i}") for i in range(2)] for s in range(2)]
    QTt = [[AR.alloc([128, TO], BF16, f"QTt{s}{i}") for i in range(2)] for s in range(2)]
    R_KT = [[Res(f"KT{s}{i}") for i in range(2)] for s in range(2)]
    R_V = [[Res(f"V{s}{i}") for i in range(2)] for s in range(2)]
    R_QT = [[Res(f"QT{s}{i}") for i in range(2)] for s in range(2)]
    for s in range(2):
        for i in range(2):
            P.memset("pool", Vt[s][i], 0.0, (R_V[s][i],))
            P.memset("pool", KTt[s][i], 0.0, (R_KT[s][i],))
            P.memset("pool", QTt[s][i], 0.0, (R_QT[s][i],))
        for i in range(2):
            if s == 1:
                P.memset("dve", Vt[s][i][:, :, 64:65], 1.0, (R_V[s][i],))
    SPt = [AR.alloc([128, 512], BF16, f"SPt{i}") for i in range(2)]
    Gt = [AR.alloc([128, 512], F32, f"Gt{i}") for i in range(2)]
    at = [AR.alloc([128, 512], BF16, f"at{i}") for i in range(2)]
    Pt = [AR.alloc([128, 512], BF16, f"Pt{i}") for i in range(2)]
    R_E = [Res("E0"), Res("E1")]; R_SP = [Res("SP0"), Res("SP1")]; R_G2 = [Res("G0"), Res("G1")]
    R_a = [Res("a0"), Res("a1")]; R_P = [Res("P0"), Res("P1")]
    osb_s = AR.alloc([128, 512], F32, "osb_s")
    ofx_s = AR.alloc([128, 512], F32, "ofx_s")
    otok = AR.alloc([128, 4, 2, 64], F32, "otok")
    rec = AR.alloc([128, 4], F32, "rec")
    R_os, R_otok = Res("os"), Res("otok", stg=True)
    BZ, BS, BA, BOS, BOF, BTP = (0, 1), (2, 3), 4, 5, 6, 7
    itc = [0]

    def sweep(KT, QT, Vtile, rK, rQ, rV, q0, Nq, blocks, odst_fn):
        A = bank(BA); Osb = bank(BOS); Ofx = bank(BOF)
        for (bk_, M, rb) in ((A, 128, RB[BA]), (Osb, 128, RB[BOS]), (Ofx, 128, RB[BOF])):
            P.mm(bk_[0:M, 0:Nq], zrow[0:1, 0:M], onesb[0:1, 0:Nq], True, True, (R_c,), (rb,))
        nb = len(blocks)

        def mm1(i):
            kcol, vkb, qlo, msb, mfx = blocks[i]
            b = (itc[0] + i) % 2
            Z = bank(BZ[b]); S = bank(BS[b])
            P.mm(Z[:, qlo:Nq], KT[0][:, kcol:kcol + 128], QT[0][:, q0 + qlo:q0 + Nq], True, msb is None,
                 (rK[0], rQ[0]), (RB[BZ[b]],))
            if msb is not None:
                P.mm(Z[:, qlo:Nq], identb, msb, False, True, (R_c, R_mask), (RB[BZ[b]],))
            P.mm(S[:, qlo:Nq], KT[1][:, kcol:kcol + 128], QT[1][:, q0 + qlo:q0 + Nq], True, mfx is None,
                 (rK[1], rQ[1]), (RB[BS[b]],))
            if mfx is not None:
                P.mm(S[:, qlo:Nq], identb, mfx, False, True, (R_c, R_mask), (RB[BS[b]],))

        mm1(0)
        for i in range(nb):
            kcol, vkb, qlo, msb, mfx = blocks[i]
            b = (itc[0] + i) % 2
            Z = bank(BZ[b]); S = bank(BS[b])
            sl = slice(qlo, Nq)
            P.act(Z[:, sl], Z[:, sl], AF.Exp, (RB[BZ[b]],), (RB[BZ[b]],))
            P.act(SPt[b][:, sl], Z[:, sl], AF.Ln, (RB[BZ[b]],), (R_SP[b],), bias=1.0)
            P.mm(A[:, sl], triIb, SPt[b][:, sl], False, True, (R_c, R_SP[b]), (RB[BA],))
            if i + 1 < nb:
                mm1(i + 1)
            P.act(Pt[b][:, sl], S[:, sl], AF.Exp, (RB[BS[b]],), (R_P[b],))
            P.mm(Ofx[:, sl], Vtile[1][:, vkb, :], Pt[b][:, sl], False, True, (rV[1], R_P[b]), (RB[BOF],))
            if i > 0:
                pk, pv, pq, _, _ = blocks[i - 1]
                pb_ = (itc[0] + i - 1) % 2
                P.mm(Osb[:, pq:Nq], Vtile[0][:, pv, :], at[pb_][:, pq:Nq], False, True,
                     (rV[0], R_a[pb_]), (RB[BOS],))
            P.act(Gt[b][:, sl], A[:, sl], AF.Exp, (RB[BA],), (R_G2[b],))
            P.mm(A[:, sl], compb, SPt[b][:, sl], False, True, (R_c, R_SP[b]), (RB[BA],))
            P.tt("dve", at[b][:, sl], Z[:, sl], Gt[b][:, sl], ALU.mult, (RB[BZ[b]], R_G2[b]), (R_a[b],))
        pk, pv, pq, _, _ = blocks[nb - 1]
        pb_ = (itc[0] + nb - 1) % 2
        P.mm(Osb[:, pq:Nq], Vtile[0][:, pv, :], at[pb_][:, pq:Nq], False, True, (rV[0], R_a[pb_]), (RB[BOS],))
        itc[0] += nb
        P.cp("dve", osb_s[0:64, 0:Nq], Osb[0:64, 0:Nq], (RB[BOS],), (R_os,))
        P.cp("dve", ofx_s[0:65, 0:Nq], Ofx[0:65, 0:Nq], (RB[BOF],), (R_os,))
        nt = min(128, Nq)
        ng = max(1, Nq // 128)
        TP = bank(BTP); TF = bank(BS[1])
        for i in range(ng):
            P.tr(TP[0:nt, i * 64:(i + 1) * 64], osb_s[0:64, i * 128:i * 128 + nt], identf[0:64, 0:64],
                 (R_os, R_c), (RB[BTP],))
            P.tr(TF[0:nt, i * 65:(i + 1) * 65], ofx_s[0:65, i * 128:i * 128 + nt], identf[0:65, 0:65],
                 (R_os, R_c), (RB[BS[1]],))
        TFv = TF[0:nt, 0:ng * 65].rearrange("p (g c) -> p g c", c=65)
        P.op("dve", lambda e: e.reciprocal(out=rec[0:nt, 0:ng], in_=TFv[:, :, 64]), (RB[BS[1]],), (R_otok,))
        P.cp("dve", otok[0:nt, 0:ng, 0, :], TP[0:nt, 0:ng * 64].rearrange("p (g c) -> p g c", c=64),
             (RB[BTP],), (R_otok,))
        for i in range(ng):
            P.ts("dve", otok[0:nt, i, 1, :], TFv[:, i, 0:64], rec[0:nt, i:i + 1], None, ALU.mult, None,
                 (RB[BS[1]], R_otok), (R_otok,))
        for (d_, s_, rd_) in odst_fn(otok, nt, ng):
            P.dma("sp", d_, s_, (R_otok,), (rd_,))

    hp_count = [0]

    def load_pair(ktsrc, vsrc, qsrc, nkeys, nkb, nq, rk_scr, rv_scr, rq_scr):
        i = hp_count[0] % 2
        hp_count[0] += 1
        for s in range(2):
            for (r0, r1, src) in ktsrc(s):
                P.dma("sp", KTt[s][i][r0:r1, 0:nkeys], src, rk_scr, (R_KT[s][i],))
            for (r0, r1, src) in qsrc(s):
                P.dma("sp", QTt[s][i][r0:r1, 0:nq], src, rq_scr, (R_QT[s][i],))
            P.dma("sp", Vt[s][i][:, 0:nkb, 0:64], vsrc(s).rearrange("(kb k) d -> k kb d", k=128),
                  (rv_scr,), (R_V[s][i],))
        return ([KTt[0][i], KTt[1][i]], [QTt[0][i], QTt[1][i]], [Vt[0][i], Vt[1][i]],
                [R_KT[0][i], R_KT[1][i]], [R_QT[0][i], R_QT[1][i]], [R_V[0][i], R_V[1][i]])

    with nc.allow_non_contiguous_dma(reason="head-sliced V rows / O columns (128-256B segments)"):
        jobs = []
        for p in range(NH):
            def ld(p=p):
                hs = slice((p % 2) * 64, (p % 2) * 64 + 64)
                return load_pair(
                    lambda s: [(0, 64, KTs[p // 2, s, hs, :])] + ([(64, 70, KAs[p])] if s == 1 else []),
                    lambda s: Vs[s, :, p * 64:(p + 1) * 64],
                    lambda s: [(0, 64, QTs[p // 2, s, hs, :])] + ([(64, 70, QAs[p])] if s == 1 else []),
                    T, NKB, TO, (R_scr["KTs"], R_scr["KAs"]), R_scr["Vs"], (R_scr["QTs"], R_scr["QAs"]))

            def sw(ld_, p=p):
                KT, QT, Vl, rK, rQ, rV = ld_
                for z in range(NZ):
                    blocks = []
                    for kbz in range(15, -1, -1):
                        qlo = (kbz // 4) * 128
                        kb = 16 * z + kbz
                        blocks.append((kb * 128, kb, qlo, maskb[:, 0, kbz, qlo:512], maskb[:, 1, kbz, qlo:512]))
                    for kb in range(16 * z - 1, -1, -1):
                        blocks.append((kb * 128, kb, 0, None, None))

                    def odst(ot, nt, ng, z=z, p=p):
                        rows = Os[z * 512:(z + 1) * 512, :].rearrange("(g t) c -> t g c", t=128)
                        return [(rows[:, :, p * 64:(p + 1) * 64], ot[:, :, 0, :], R_scr["Os"]),
                                (rows[:, :, W + p * 64:W + (p + 1) * 64], ot[:, :, 1, :], R_scr["Os"])]

                    sweep(KT, QT, Vl, rK, rQ, rV, z * 512, 512, blocks, odst)
            jobs.append((ld, sw))
        for sbi in range(2):
            for p in range(NH):
                def ld(sbi=sbi, p=p):
                    return load_pair(
                        lambda s: [(0, 64 if s == 0 else 70, KTss[sbi, p, s, 0:(64 if s == 0 else 70), :])],
                        lambda s: Vss[sbi, s, :, p * 64:(p + 1) * 64],
                        lambda s: [(0, 64 if s == 0 else 70, QTss[sbi, p, s, 0:(64 if s == 0 else 70), :])],
                        SK, SKB, NQS, (R_scr["KTss"],), R_scr["Vss"], (R_scr["QTss"],))

                def sw(ld_, sbi=sbi, p=p):
                    KT, QT, Vl, rK, rQ, rV = ld_
                    blocks = [(0, 0, 0, smaskb[:, 0, :], smaskb[:, 1, :])]
                    for kb in range(PB - 1, -1, -1):
                        blocks.append((128 + kb * 128, 1 + kb, 0, None, None))

                    def odst_s(ot, nt, ng, sbi=sbi, p=p):
                        return [(Oss[sbi * 64:(sbi + 1) * 64, p * 64:(p + 1) * 64], ot[0:64, 0, 0, :], R_scr["Oss"]),
                                (Oss[sbi * 64:(sbi + 1) * 64, W + p * 64:W + (p + 1) * 64], ot[0:64, 0, 1, :],
                                 R_scr["Oss"])]

                    sweep(KT, QT, Vl, rK, rQ, rV, 0, NQS, blocks, odst_s)
                jobs.append((ld, sw))
        nxt = jobs[0][0]()
        for ji, (ld, sw) in enumerate(jobs):
            cur_ld = nxt
            if ji + 1 < len(jobs):
                nxt = jobs[ji + 1][0]()
            sw(cur_ld)
    P.barrier()
    AR.release(mB)

    GC = 2
    wob = AR.alloc([128, MC, D], BF16, "wob")
    wgb = AR.alloc([128, KD, DFF], BF16, "wgb")
    wub = AR.alloc([128, KD, DFF], BF16, "wub")
    wdb = AR.alloc([128, FFC, D], BF16, "wdb")
    R_wo, R_wg, R_wu, R_wd = Res("wo"), Res("wg"), Res("wu"), Res("wd")
    for (dst, src, nk, rw_, rs_) in ((wob, wos, MC, R_wo, R_wsc["wo"]), (wgb, wgs, KD, R_wg, R_wsc["wg"]),
                                     (wub, wus, KD, R_wu, R_wsc["wu"]), (wdb, wds, FFC, R_wd, R_wsc["wd"])):
        sv = src.rearrange("(k p) c -> p k c", p=128)
        for k0 in range(0, nk, 4):
            k1 = min(nk, k0 + 4)
            P.dma("sp", dst[:, k0:k1, :], sv[:, k0:k1, :], (rs_,), (rw_,))
    gp2 = AR.alloc([128, D], F32, "gp2"); gp5 = AR.alloc([128, D], F32, "gp5")
    gfin = AR.alloc([128, D], F32, "gfin")
    R_g = Res("gates")
    P.dma("sp", gp2, adas[0, 2, :].partition_broadcast(128), (R_scr["adas"],), (R_g,))
    P.dma("sp", gp5, adas[0, 5, :].partition_broadcast(128), (R_scr["adas"],), (R_g,))
    P.dma("sp", gfin, g_final.partition_broadcast(128), (R_in,), (R_g,))
    xc = AR.alloc([128, GC, D], F32, "xc")
    ocy = AR.alloc([128, GC, max(D, MIX)], F32, "ocy")
    oc = ocy[:, :, 0:MIX]
    yc = ocy[:, :, 0:D]
    nb_ = AR.alloc([128, GC, max(D, MIX)], BF16, "nb_")
    fT = AR.alloc([128, max(KD, MC), GC * 128], BF16, "fT")
    actT = AR.alloc([128, FFC, GC * 128], BF16, "actT")
    tsq = AR.alloc([128, max(D, MIX)], BF16, "tsq")
    ssC = AR.alloc([128, 8], F32, "ssC")
    tmpf = AR.alloc([128, 512], F32, "tmpf")
    sg = AR.alloc([128, GC * 128], F32, "sg")
    R_xc, R_oc, R_nb, R_fT, R_actT, R_tmpf, R_sg = (Res("xc"), Res("oc", stg=True), Res("nb"), Res("fT"),
                                                   Res("actT"), Res("tmpf"), Res("sg"))
    R_yc = R_oc

    def fvec_ffn(cnd, k):
        return fffn[:, 0, cnd, k:k + 1], fffn[:, 1, cnd, k:k + 1]

    def post_group(xsrc, osrc, ydst, G, conds, g2, g5):
        N = G * 128
        P.dma("sp", xc[:, 0:G, :], xsrc.rearrange("(g p) d -> p g d", p=128), (R_in,), (R_xc,))
        P.dma("sp", oc[:, 0:G, :], osrc.rearrange("(g p) d -> p g d", p=128), (R_scr["Os"], R_scr["Oss"]), (R_oc,))
        for s in range(2):
            norm_to_fm(oc[:, :, s * W:(s + 1) * W], G, W, tsq, ssC, nb_[:, :, 0:W],
                       fT[:, s * WC:(s + 1) * WC, :], [(0, 128, 0)],
                       lambda cnd, k, s=s: (goTt[:, s * WC + k:s * WC + k + 1], None),
                       R_oc, R_nb, R_fT, WC, (0, 1))
        for g in range(G):
            for nb2 in range(D // 512):
                cs = slice(nb2 * 512, (nb2 + 1) * 512)
                bi = 2 + ((g * (D // 512) + nb2) % 2)
                bk = bank(bi)
                for c in range(MC):
                    P.mm(bk, fT[:, c, g * 128:(g + 1) * 128], wob[:, c, cs], c == 0, c == MC - 1, (R_fT, R_wo), (RB[bi],))
                P.tt("dve", tmpf, bk, g2[:, cs], ALU.mult, (RB[bi], R_g), (R_tmpf,))
                P.tt("dve", xc[:, g, cs], xc[:, g, cs], tmpf, ALU.add, (R_xc, R_tmpf), (R_xc,))
        norm_to_fm(xc, G, D, tsq, ssC, nb_[:, :, 0:D], fT[:, 0:KD, :], conds, fvec_ffn, R_xc, R_nb, R_fT, KD, (0, 1))
        for fc in range(FFC):
            bg, bu = 4 + (fc % 2), 6 + (fc % 2)
            for k in range(KD):
                P.mm(bank(bg)[:, 0:N], wgb[:, k, fc * 128:(fc + 1) * 128], fT[:, k, 0:N], k == 0, k == KD - 1,
                     (R_wg, R_fT), (RB[bg],))
            for k in range(KD):
                P.mm(bank(bu)[:, 0:N], wub[:, k, fc * 128:(fc + 1) * 128], fT[:, k, 0:N], k == 0, k == KD - 1,
                     (R_wu, R_fT), (RB[bu],))
            P.act(sg[:, 0:N], bank(bg)[:, 0:N], AF.Silu, (RB[bg],), (R_sg,))
            P.tt("dve", actT[:, fc, 0:N], sg[:, 0:N], bank(bu)[:, 0:N], ALU.mult, (R_sg, RB[bu]), (R_actT,))
        for g in range(G):
            for nb2 in range(D // 512):
                cs = slice(nb2 * 512, (nb2 + 1) * 512)
                bi = 2 + ((g * (D // 512) + nb2) % 2)
                bk = bank(bi)
                for fc in range(FFC):
                    P.mm(bk, actT[:, fc, g * 128:(g + 1) * 128], wdb[:, fc, cs], fc == 0, fc == FFC - 1,
                         (R_actT, R_wd), (RB[bi],))
                P.tt("dve", tmpf, bk, g5[:, cs], ALU.mult, (RB[bi], R_g), (R_tmpf,))
                P.tt("dve", xc[:, g, cs], xc[:, g, cs], tmpf, ALU.add, (R_xc, R_tmpf), (R_xc,))
        for g in range(G):
            P.memset("dve", ssC[:, g:g + 1], 0.0, (R_nb,))
            P.act(tsq[:, 0:D], xc[:, g, :], AF.Square, (R_xc, R_nb), (R_nb,), accum_out=ssC[:, g:g + 1])
        P.ts("dve", ssC[:, 0:G], ssC[:, 0:G], 1.0 / D, EPS, ALU.mult, ALU.add, (R_nb,), (R_nb,))
        P.act(ssC[:, 0:G], ssC[:, 0:G], AF.Ln, (R_nb,), (R_nb,))
        P.act(ssC[:, 0:G], ssC[:, 0:G], AF.Exp, (R_nb,), (R_nb,), scale=-0.5)
        for g in range(G):
            P.stt("dve", yc[:, g, :], xc[:, g, :], ssC[:, g:g + 1], gfin, ALU.mult, ALU.mult, (R_xc, R_nb, R_g), (R_yc,))
        P.dma("sp", ydst.rearrange("(g p) d -> p g d", p=128), yc[:, 0:G, :], (R_yc,), (R_out,))

    for gi in range(TO // (GC * 128)):
        rs = slice(gi * GC * 128, (gi + 1) * GC * 128)
        post_group(xown[rs, :], Os[rs, :], y_own[rs, :], GC, [(0, 128, 0)], gp2, gp5)
    for sbi in range(2):
        P.dma("sp", gp2[sbi * 64:(sbi + 1) * 64, :], adas[1 + sbi, 2, :].partition_broadcast(64),
              (R_scr["adas"],), (R_g,))
        P.dma("sp", gp5[sbi * 64:(sbi + 1) * 64, :], adas[1 + sbi, 5, :].partition_broadcast(64),
              (R_scr["adas"],), (R_g,))
    post_group(xsam, Oss, y_sam, 1, [(0, 64, 1), (64, 128, 2)], gp2, gp5)
    P.barrier()

    print("arena high-water", AR.hw, "of", ARENA_BYTES, "ops", {e: len(P.q[e]) for e in ENGS}, "sems", P.nsem, flush=True)
    blk = es.enter_context(nc.Block())
    P.emit(blk)
    es.close()
    return nc


def make_consts():
    c = np.zeros((128, 640), np.float32)
    s = np.arange(128)[:, None]
    j = np.arange(128)[None, :]
    c[:, 0:128] = (s == j)
    c[:, 128:256] = -(s >= j).astype(np.float32)
    c[:, 256:384] = -(s < j).astype(np.float32)
    return c


def make_masks(r):
    m = np.zeros((128, 2, 16, 512), np.float32)
    j = np.arange(128)[:, None]
    for kbz in range(16):
        for i in range(4):
            key = kbz * 128 + j
            q = (4 * i + r) * 128 + np.arange(128)[None, :]
            m[:, 0, kbz, i * 128:(i + 1) * 128] = np.where(key < q, 0.0, NEG)
            m[:, 1, kbz, i * 128:(i + 1) * 128] = np.where(key <= q, 0.0, NEG)
    sm = np.zeros((128, 2, 64), np.float32)
    q = np.arange(64)[None, :]
    sm[:, 0, :] = np.where((j < q) & (j < 64), 0.0, NEG)
    sm[:, 1, :] = np.where((j <= q) & (j < 64), 0.0, NEG)
    return m.reshape(128, -1), sm.reshape(128, -1)


_NC_CACHE = {}


def run(cfg, inputs):
    D, NH, T, DFF, PAST, DT = cfg["D"], cfg["NH"], cfg["T"], cfg["DFF"], cfg["PAST"], cfg["DT"]
    W = NH * 64
    KD = D // 128
    key = tuple(sorted(cfg.items()))
    nc = build(cfg)
    f = lambda a: np.ascontiguousarray(np.asarray(a, dtype=np.float32))
    xp, xs = f(inputs["x_prompt"]), f(inputs["x_sample"])
    cp, cs = f(inputs["c_prompt"]), f(inputs["c_sample"])
    NSUB = T // 128
    in_maps = []
    cst = make_consts()
    for c in range(8):
        b, r = c // 4, c % 4
        own = np.arange(r, NSUB, 4)
        xo = xp[b].reshape(NSUB, 128, D)[own].reshape(-1, D)
        crows = np.stack([cp[b], cs[2 * c], cs[2 * c + 1]], axis=1)
        cTm = crows.reshape(KD, 128, 3).transpose(1, 0, 2).reshape(128, KD * 3)
        mk, smk = make_masks(r)
        selv = np.zeros((128, 4), np.float32); selv[:, r] = 1.0
        go = np.concatenate([f(inputs["g_sb_out"])[0], f(inputs["g_fox_out"])[0]])
        goT = go.reshape(-1, 128).T
        m = {
            "xfull": xp[b], "xown": xo, "xsam": xs[2 * c:2 * c + 2].reshape(128, D), "cT": cTm,
            "csk": f(inputs["cache_sb_k"])[0, 2 * c:2 * c + 2].reshape(2, PAST, W),
            "csv": f(inputs["cache_sb_v"])[0, 2 * c:2 * c + 2].reshape(2, PAST, W),
            "cfk": f(inputs["cache_fox_k"])[0, 2 * c:2 * c + 2].reshape(2, PAST, W),
            "cfv": f(inputs["cache_fox_v"])[0, 2 * c:2 * c + 2].reshape(2, PAST, W),
            "clfT": f(inputs["cache_fox_logf"])[0, 2 * c:2 * c + 2].transpose(2, 0, 1).reshape(NH, 2 * PAST),
            "w_ada": f(inputs["w_ada"])[0], "b_ada": f(inputs["b_ada"])[0], "w_in": f(inputs["w_in"])[0],
            "w_o": f(inputs["w_o"])[0], "w_gate": f(inputs["w_gate"])[0], "w_up": f(inputs["w_up"])[0],
            "w_down": f(inputs["w_down"])[0], "g_mix": f(inputs["g_mix"])[0], "g_ffn": f(inputs["g_ffn"])[0],
            "g_final": f(inputs["g_final"]), "goT": goT, "b_f": f(inputs["b_f"])[0],
            "cst": cst, "masks": mk, "smask": smk, "sel": selv,
        }
        in_maps.append({k: np.ascontiguousarray(v, dtype=np.float32) for k, v in m.items()})
    if cfg.get("_prep_only"):
        return nc, in_maps
    res = run_bass_kernel_spmd(nc, in_maps, core_ids=list(range(8)))
    R = res.results
    yp = np.zeros((2, T, D), np.float32)
    outs_p = {k: np.zeros((1, 2, T, NH, 64), np.float32) for k in ("o_sbk", "o_sbv", "o_fxk", "o_fxv")}
    lfp = np.zeros((1, 2, T, NH), np.float32)
    ysm = np.zeros((16, DT, D), np.float32)
    outs_s = {k: np.zeros((1, 16, DT, NH, 64), np.float32) for k in ("s_sbk", "s_sbv", "s_fxk", "s_fxv")}
    lfs = np.zeros((1, 16, DT, NH), np.float32)
    for c in range(8):
        b, r = c // 4, c % 4
        own = np.arange(r, NSUB, 4)
        yp[b].reshape(NSUB, 128, D)[own] = R[c]["y_own"].reshape(-1, 128, D)
        for k in outs_p:
            outs_p[k][0, b].reshape(NSUB, 128, NH, 64)[own] = R[c][k].reshape(-1, 128, NH, 64)
        lfp[0, b].reshape(NSUB, 128, NH)[own] = R[c]["o_lf"].reshape(-1, 128, NH)
        ysm[2 * c:2 * c + 2] = R[c]["y_sam"].reshape(2, DT, D)
        for k in outs_s:
            outs_s[k][0, 2 * c:2 * c + 2] = R[c][k].reshape(2, DT, NH, 64)
        lfs[0, 2 * c:2 * c + 2] = R[c]["s_lf"].reshape(2, DT, NH)
    return (yp, ysm, outs_p["o_sbk"], outs_p["o_sbv"], outs_p["o_fxk"], outs_p["o_fxv"], lfp,
            outs_s["s_sbk"], outs_s["s_sbv"], outs_s["s_fxk"], outs_s["s_fxv"], lfs)


def kernel(**inputs):
    return run(CFG_FULL, inputs)
```

```python
import numpy as np
import ml_dtypes
from contextlib import ExitStack
import concourse.bass as bass
import concourse.mybir as mybir
from concourse.bass_utils import run_bass_kernel_spmd

F32 = mybir.dt.float32
BF16 = mybir.dt.bfloat16
U8 = mybir.dt.uint8
AF = mybir.ActivationFunctionType
ALU = mybir.AluOpType
EPS = 1e-6
NEG = -30000.0
ENGS = ("pe", "act", "dve", "pool", "sp")
MAXV = 30000

CFG_FULL = dict(D=1024, NH=8, T=8192, DFF=2816, PAST=1024, DT=64)


class Res:
    __slots__ = ("name", "w", "rs", "cw", "stg")

    def __init__(self, name, cw=False, stg=False):
        self.name = name
        self.w = {}
        self.rs = {}
        self.cw = cw
        self.stg = stg


DMA_SLOTS = {"sp": 4, "pool": 2}


class Prog:
    def __init__(self, nc, es):
        self.nc, self.es = nc, es
        self.q = {e: [] for e in ENGS}
        self.sem = {}
        self.cnt = {}
        self.seen = {e: {} for e in ENGS}
        self.nsem = 0
        for e in ENGS:
            self._rot(e)
        self.slots = {e: [[self._newsem(f"d{e}{k}"), 0] for k in range(n)] for e, n in DMA_SLOTS.items()}
        self.dn = {e: 0 for e in DMA_SLOTS}

    def _newsem(self, nm):
        self.nsem += 1
        return self.es.enter_context(self.nc.semaphore(f"{nm}_{self.nsem}"))

    def _rot(self, e):
        self.sem[e] = self._newsem("s" + e)
        self.cnt[e] = 0

    def _deps(self, eng, reads, writes, cdma=False):
        evs = {}

        def add(ev):
            s, v = ev[0], ev[1]
            k = id(s)
            if k not in evs or evs[k][1] < v:
                evs[k] = (s, v)

        for r in reads:
            for ev in r.w.values():
                add(ev)
        for w in writes:
            for ev in w.w.values():
                if cdma and w.cw and ev[2]:
                    continue
                add(ev)
            for ev in w.rs.values():
                add(ev)
        waits = []
        seen = self.seen[eng]
        for k, (s, v) in evs.items():
            if eng == "pe" and s is self.sem["pe"]:
                continue
            if seen.get(k, 0) >= v:
                continue
            seen[k] = v
            waits.append((s, v))
        return waits

    def _mark(self, ev, reads, writes, cdma=False):
        k = id(ev[0])
        for r in reads:
            r.rs[k] = (ev[0], ev[1])
        for w in writes:
            if cdma and w.cw:
                w.w[k] = (ev[0], ev[1], True)
            else:
                w.w = {k: (ev[0], ev[1], False)}
            w.rs = {}

    def op(self, eng, fn, reads=(), writes=()):
        waits = self._deps(eng, reads, writes)
        if self.cnt[eng] >= MAXV:
            self._rot(eng)
        self.cnt[eng] += 1
        ev = (self.sem[eng], self.cnt[eng])
        self.q[eng].append((waits, fn, (ev[0], 1)))
        self._mark(ev, reads, writes)

    def dma(self, eng, out, in_, reads=(), writes=()):
        nsl = len(self.slots[eng])
        slot = self.slots[eng][self.dn[eng] % nsl]
        self.dn[eng] += 1
        if slot[1] >= MAXV:
            slot[0] = self._newsem(f"d{eng}r")
            slot[1] = 0
        sem = slot[0]
        waits = self._deps(eng, reads, writes, cdma=True)
        if slot[1] > 0 and self.seen[eng].get(id(sem), 0) < slot[1]:
            self.seen[eng][id(sem)] = slot[1]
            waits.append((sem, slot[1]))
        slot[1] += 16
        ev = (sem, slot[1])
        self.q[eng].append((waits, lambda e: e.dma_start(out=out, in_=in_, allow_slow_non_contiguous=True), (sem, 16)))
        self._mark(ev, reads, writes, cdma=True)

    def barrier(self):
        evs = [(self.sem[e], self.cnt[e]) for e in ENGS if self.cnt[e] > 0]
        for e in self.slots:
            for (sm, c) in self.slots[e]:
                if c > 0:
                    evs.append((sm, c))
        for e in ENGS:
            waits = []
            for s, v in evs:
                if s is self.sem[e]:
                    continue
                if self.seen[e].get(id(s), 0) >= v:
                    continue
                self.seen[e][id(s)] = v
                waits.append((s, v))
            if waits:
                self.q[e].append((waits, None, None))

    def emit(self, blk):
        def run(eng_name):
            def body(e):
                for waits, fn, inc in self.q[eng_name]:
                    for s, v in waits:
                        e.wait_ge(s, v)
                    if fn is not None:
                        ins = fn(e)
                        ins.then_inc(inc[0], inc[1])
            return body

        blk.tensor(run("pe"))
        blk.scalar(run("act"))
        blk.vector(run("dve"))
        blk.gpsimd(run("pool"))
        blk.sync(run("sp"))

    def mm(self, out, lhsT, rhs, start, stop, reads, writes):
        self.op("pe", lambda e: e.matmul(out, lhsT=lhsT, rhs=rhs, start=start, stop=stop,
                                         skip_group_check=True), reads, writes)

    def tr(self, out, in_, ident, reads, writes):
        self.op("pe", lambda e: e.transpose(out, in_, ident), reads, writes)

    def act(self, out, in_, func, reads, writes, bias=0.0, scale=1.0, accum_out=None):
        if accum_out is None:
            self.op("act", lambda e: e.activation(out=out, in_=in_, func=func, bias=bias, scale=scale),
                    reads, writes)
        else:
            self.op("act", lambda e: e.activation(out=out, in_=in_, func=func, bias=bias, scale=scale,
                                                  accum_out=accum_out), reads, writes)

    def ts(self, eng, out, in0, s1, s2, op0, op1, reads, writes):
        if s2 is None:
            self.op(eng, lambda e: e.tensor_scalar(out=out, in0=in0, scalar1=s1, scalar2=None, op0=op0),
                    reads, writes)
        else:
            self.op(eng, lambda e: e.tensor_scalar(out=out, in0=in0, scalar1=s1, scalar2=s2, op0=op0, op1=op1),
                    reads, writes)

    def tt(self, eng, out, in0, in1, op, reads, writes):
        self.op(eng, lambda e: e.tensor_tensor(out=out, in0=in0, in1=in1, op=op), reads, writes)

    def stt(self, eng, out, in0, scalar, in1, op0, op1, reads, writes):
        self.op(eng, lambda e: e.scalar_tensor_tensor(out=out, in0=in0, scalar=scalar, in1=in1, op0=op0, op1=op1),
                reads, writes)

    def cp(self, eng, out, in_, reads, writes):
        self.op(eng, lambda e: e.tensor_copy(out=out, in_=in_), reads, writes)

    def memset(self, eng, ap, val, writes):
        self.op(eng, lambda e: e.memset(ap, val), (), writes)


class Arena:
    def __init__(self, t, nbytes):
        self.t, self.n, self.off = t, nbytes, 0

    def mark(self):
        return self.off

    def release(self, m):
        self.off = m

    def alloc(self, shape, dt, name=""):
        esz = 4 if dt == F32 else 2
        n = 1
        for s in shape[1:]:
            n *= s
        nb = (n * esz + 63) // 64 * 64
        assert self.off + nb <= self.n, f"arena overflow {name}: {self.off}+{nb}>{self.n}"
        self.hw = max(getattr(self, "hw", 0), self.off + nb)
        v = self.t[:, self.off:self.off + n * esz].bitcast(dt)
        self.off += nb
        if len(shape) == 3:
            v = v.rearrange("p (a b) -> p a b", a=shape[1])
        elif len(shape) == 4:
            v = v.rearrange("p (a b c) -> p a b c", a=shape[1], b=shape[2])
        if shape[0] < 128:
            v = v[0:shape[0]]
        return v


def build(cfg):
    D, NH, T, DFF, PAST, DT = cfg["D"], cfg["NH"], cfg["T"], cfg["DFF"], cfg["PAST"], cfg["DT"]
    KD = D // 128
    W = NH * 64
    WC = W // 128
    MIX = 2 * W
    MC = MIX // 128
    IN = 6 * W + NH
    FFC = DFF // 128
    NZ = T // 2048
    TO = T // 4
    NSUBO = TO // 128
    PB = PAST // 128
    SKB = PB + 1
    SK = SKB * 128
    NQS = DT
    assert DT == 64 and T % 2048 == 0 and D % 512 == 0 and W % 128 == 0
    CQ, CK, CV = 0, W, 2 * W
    FQ, FK, FV, LF = 3 * W, 4 * W, 5 * W, 6 * W

    nc = bass.Bass("TRN2", target_bir_lowering=False)

    def din(name, shape):
        return nc.dram_tensor(name, list(shape), F32, kind="ExternalInput").ap()

    def dout(name, shape):
        return nc.dram_tensor(name, list(shape), F32, kind="ExternalOutput").ap()

    def dscr(name, shape, dt):
        return nc.dram_tensor(name, list(shape), dt, kind="Internal").ap()

    xfull = din("xfull", (T, D)); xown = din("xown", (TO, D)); xsam = din("xsam", (128, D))
    cT = din("cT", (128, KD * 3))
    csk = din("csk", (2, PAST, W)); csv = din("csv", (2, PAST, W))
    cfk = din("cfk", (2, PAST, W)); cfv = din("cfv", (2, PAST, W)); clfT = din("clfT", (NH, 2 * PAST))
    w_ada = din("w_ada", (D, 6 * D)); b_ada = din("b_ada", (6 * D,))
    w_in = din("w_in", (D, IN)); w_o = din("w_o", (MIX, D))
    w_gate = din("w_gate", (D, DFF)); w_up = din("w_up", (D, DFF)); w_down = din("w_down", (DFF, D))
    g_mix = din("g_mix", (D,)); g_ffn = din("g_ffn", (D,)); g_final = din("g_final", (D,))
    goT = din("goT", (128, MC)); b_f = din("b_f", (NH,))
    cst = din("cst", (128, 640)); masks = din("masks", (128, 2 * 16 * 512)); smask = din("smask", (128, 2 * 64))
    sel = din("sel", (128, 4))

    y_own = dout("y_own", (TO, D)); y_sam = dout("y_sam", (128, D))
    o_sbk = dout("o_sbk", (TO, W)); o_sbv = dout("o_sbv", (TO, W)); o_fxk = dout("o_fxk", (TO, W))
    o_fxv = dout("o_fxv", (TO, W)); o_lf = dout("o_lf", (TO, NH))
    s_sbk = dout("s_sbk", (128, W)); s_sbv = dout("s_sbv", (128, W)); s_fxk = dout("s_fxk", (128, W))
    s_fxv = dout("s_fxv", (128, W)); s_lf = dout("s_lf", (128, NH))

    KTs = dscr("KTs", (NH // 2, 2, 128, T), BF16)
    KAs = dscr("KAs", (NH, 6, T), BF16)
    QAs = dscr("QAs", (NH, 6, TO), BF16)
    Vs = dscr("Vs", (2, T, W), BF16)
    QTs = dscr("QTs", (NH // 2, 2, 128, TO), BF16)
    Os = dscr("Os", (TO, MIX), F32)
    KTss = dscr("KTss", (2, NH, 2, 70, SK), BF16)
    Vss = dscr("Vss", (2, 2, SK, W), BF16)
    QTss = dscr("QTss", (2, NH, 2, 70, NQS), BF16)
    Oss = dscr("Oss", (128, MIX), F32)
    adas = dscr("adas", (3, 6, D), F32)
    wos = dscr("wos", (MIX, D), BF16); wgs = dscr("wgs", (D, DFF), BF16)
    wus = dscr("wus", (D, DFF), BF16); wds = dscr("wds", (DFF, D), BF16)

    es = ExitStack()
    ARENA_BYTES = cfg.get("ARENA", 207 * 1024)
    arena_t = es.enter_context(nc.sbuf_tensor("arena", [128, ARENA_BYTES], U8))
    ps = es.enter_context(nc.psum_tensor("ps", [128, 4096], F32))
    P = Prog(nc, es)
    AR = Arena(arena_t, ARENA_BYTES)

    def bank(i):
        return ps[:, i * 512:(i + 1) * 512]

    def bank_bf(i):
        return ps[:, i * 512:(i + 1) * 512].bitcast(BF16)

    RB = [Res(f"bank{i}") for i in range(8)]
    R_scr = {k: Res(k) for k in ("KTs", "Vs", "QTs", "Os", "KTss", "Vss", "QTss", "Oss", "adas", "KAs", "QAs")}
    R_in = Res("inputs")
    R_out = Res("outputs")

    identf = AR.alloc([128, 128], F32, "identf")
    identb = AR.alloc([128, 128], BF16, "identb")
    triIb = AR.alloc([128, 128], BF16, "triI")
    compb = AR.alloc([128, 128], BF16, "comp")
    zrow = AR.alloc([128, 128], BF16, "zrow")
    onesb = AR.alloc([128, 512], BF16, "onesb")
    selt = AR.alloc([128, 4], F32, "sel")
    goTt = AR.alloc([128, MC], F32, "goT")
    bft = AR.alloc([128, 1], F32, "bft")
    nbft = AR.alloc([128, 1], F32, "nbft")
    bfrow = AR.alloc([128, NH], F32, "bfrow")
    scT = AR.alloc([128, KD, 3], F32, "scT")
    fmix = AR.alloc([128, 2, 3, KD], F32, "fmix")
    fffn = AR.alloc([128, 2, 3, KD], F32, "fffn")
    R_c = Res("consts")
    R_f = Res("fvecs")

    P.dma("sp", identf, cst[:, 0:128], (R_in,), (R_c,))
    P.dma("pool", identb, cst[:, 0:128], (R_in,), (R_c,))
    P.dma("pool", triIb, cst[:, 128:256], (R_in,), (R_c,))
    P.dma("pool", compb, cst[:, 256:384], (R_in,), (R_c,))
    P.dma("pool", zrow, cst[:, 384:512], (R_in,), (R_c,))
    P.dma("sp", selt, sel, (R_in,), (R_c,))
    P.dma("sp", goTt, goT, (R_in,), (R_c,))
    P.dma("sp", bft[0:NH, :], b_f.rearrange("(h o) -> h o", o=1), (R_in,), (R_c,))
    P.dma("sp", bfrow, b_f.partition_broadcast(128), (R_in,), (R_c,))
    P.memset("dve", onesb, 1.0, (R_c,))
    P.ts("dve", nbft[0:NH, :], bft[0:NH, :], -1.0, None, ALU.mult, None, (R_c,), (R_c,))

    m0 = AR.mark()
    ctile = AR.alloc([128, KD * 3], F32, "ctile")
    ctmp = AR.alloc([128, KD * 3], F32, "ctmp")
    arow = AR.alloc([128, 6 * D], F32, "arow")[0:3]
    brow = AR.alloc([128, 6 * D], F32, "brow")[0:3]
    grow = AR.alloc([128, 2 * D], F32, "grow")[0:3]
    wadab = [AR.alloc([128, KD, 512], F32, f"wada{i}") for i in range(2)]
    R_ct, R_arow, R_wada = Res("ct"), Res("arow", stg=True), [Res("wada0"), Res("wada1")]
    P.dma("sp", ctile, cT, (R_in,), (R_ct,))
    P.dma("sp", brow, b_ada.partition_broadcast(3), (R_in,), (R_arow,))
    P.dma("sp", grow[:, 0:D], g_mix.partition_broadcast(3), (R_in,), (R_arow,))
    P.dma("sp", grow[:, D:2 * D], g_ffn.partition_broadcast(3), (R_in,), (R_arow,))
    P.act(ctmp, ctile, AF.Exp, (R_ct,), (R_ct,), scale=-1.0)
    P.ts("dve", ctmp, ctmp, 1.0, None, ALU.add, None, (R_ct,), (R_ct,))
    P.op("dve", lambda e: e.reciprocal(out=ctmp, in_=ctmp), (R_ct,), (R_ct,))
    P.tt("dve", scT.rearrange("p k c -> p (k c)"), ctile, ctmp, ALU.mult, (R_ct,), (R_c,))
    wadav = w_ada.rearrange("(k p) c -> p k c", p=128)
    NAC = 6 * D // 512
    for j in range(NAC):
        wb, rw = wadab[j % 2], R_wada[j % 2]
        P.dma("sp", wb, wadav[:, :, j * 512:(j + 1) * 512], (R_in,), (rw,))
        bk = bank(j % 2)
        for k in range(KD):
            P.mm(bk[0:3, :], scT[:, k, :], wb[:, k, :], k == 0, k == KD - 1, (rw, R_c), (RB[j % 2],))
        P.tt("dve", arow[:, j * 512:(j + 1) * 512], bk[0:3, :], brow[:, j * 512:(j + 1) * 512], ALU.add,
             (RB[j % 2], R_arow), (R_arow,))
    for idx, gsrc in ((1, 0), (4, 1)):
        P.stt("dve", arow[:, idx * D:(idx + 1) * D], arow[:, idx * D:(idx + 1) * D], 1.0,
              grow[:, gsrc * D:(gsrc + 1) * D], ALU.add, ALU.mult, (R_arow,), (R_arow,))
    for idx in (2, 5):
        P.ts("dve", arow[:, idx * D:(idx + 1) * D], arow[:, idx * D:(idx + 1) * D], 1.0, None, ALU.add, None,
             (R_arow,), (R_arow,))
    P.dma("sp", adas.rearrange("c s d -> c (s d)"), arow, (R_arow,), (R_scr["adas"],))
    with nc.allow_non_contiguous_dma(reason="tiny feature-major vector loads"):
        for cnd in range(3):
            for (dst, a_sc, a_sh) in ((fmix, 1, 0), (fffn, 4, 3)):
                P.dma("sp", dst[:, 0, cnd, :], adas[cnd, a_sc, :].rearrange("(k p) -> p k", p=128),
                      (R_scr["adas"],), (R_f,))
                P.dma("sp", dst[:, 1, cnd, :], adas[cnd, a_sh, :].rearrange("(k p) -> p k", p=128),
                      (R_scr["adas"],), (R_f,))
    P.barrier()
    AR.release(m0)

    def norm_to_fm(xt, G, width, tmpsq, ss, xs_b, hT, conds, fvec, Rx, Rtmp, RhT, nchunk, tbank, part="all"):
        if part in ("all", "pre"):
            for g in range(G):
                P.memset("dve", ss[:, g:g + 1], 0.0, (Rtmp,))
                P.act(tmpsq[:, 0:width], xt[:, g, :], AF.Square, (Rx, Rtmp), (Rtmp,), accum_out=ss[:, g:g + 1])
            P.ts("dve", ss[:, 0:G], ss[:, 0:G], 1.0 / width, EPS, ALU.mult, ALU.add, (Rtmp,), (Rtmp,))
            P.act(ss[:, 0:G], ss[:, 0:G], AF.Ln, (Rtmp,), (Rtmp,))
            P.act(ss[:, 0:G], ss[:, 0:G], AF.Exp, (Rtmp,), (Rtmp,), scale=-0.5)
            for g in range(G):
                P.ts("dve", xs_b[:, g, :], xt[:, g, :], ss[:, g:g + 1], None, ALU.mult, None, (Rx, Rtmp), (Rtmp,))
        if part == "pre":
            return
        for k in range(nchunk):
            bi = tbank[k % len(tbank)]
            pb = bank_bf(bi)
            for g in range(G):
                P.tr(pb[:, g * 128:(g + 1) * 128], xs_b[:, g, k * 128:(k + 1) * 128], identb, (Rtmp, R_c), (RB[bi],))
            for (lo, hi, cnd) in conds:
                sc_ap, sh_ap = fvec(cnd, k)
                src = pb[:, 0:G * 128].rearrange("p (g t) -> p g t", g=G)[:, :, lo:hi]
                dst = hT[:, k, 0:G * 128].rearrange("p (g t) -> p g t", g=G)[:, :, lo:hi]
                if sh_ap is None:
                    P.ts("dve", dst, src, sc_ap, None, ALU.mult, None, (RB[bi], R_f, R_c), (RhT,))
                else:
                    P.ts("dve", dst, src, sc_ap, sh_ap, ALU.mult, ALU.add, (RB[bi], R_f, R_c), (RhT,))

    zbig = AR.alloc([128, 512], BF16, "zbig")
    mL = AR.mark()
    LGT = AR.alloc([128, T], F32, "LGT")[0:NH]
    LGS = AR.alloc([128, 2 * (PAST + DT)], F32, "LGS")[0:NH]
    LGSv = LGS.rearrange("h (s n) -> h s n", s=2)
    lgtok = AR.alloc([128, NSUBO + 1, NH], F32, "lgtok")
    R_LGT, R_LGS, R_lgtok = Res("LGT"), Res("LGS"), Res("lgtok", stg=True)
    P.memset("dve", zbig, 0.0, (R_c,))
    mA = AR.mark()
    winb = AR.alloc([128, KD, IN], BF16, "winb")
    R_win = Res("win")
    winv = w_in.rearrange("(k p) c -> p k c", p=128)
    for k in range(KD):
        P.dma("pool", winb[:, k, :], winv[:, k, :], (R_in,), (R_win,))
    R_wsc = {"wo": Res("wos"), "wg": Res("wgs"), "wu": Res("wus"), "wd": Res("wds")}
    precast = []
    for (nm, dst_, src_) in (("wo", wos, w_o), ("wg", wgs, w_gate), ("wu", wus, w_up), ("wd", wds, w_down)):
        nr = src_.shape[0]
        for r0 in range(0, nr, 256):
            precast.append((dst_[r0:min(nr, r0 + 256), :], src_[r0:min(nr, r0 + 256), :], R_wsc[nm]))

    def issue_precast(n):
        for _ in range(n):
            if precast:
                d_, s_, r_ = precast.pop(0)
                P.dma("pool", d_, s_, (R_in,), (r_,))
    GA = 4
    xt2 = [AR.alloc([128, GA, D], F32, f"xt{i}") for i in range(2)]
    R_xt = [Res("xt0"), Res("xt1")]
    tmpsq = AR.alloc([128, D], F32, "tmpsq")
    ssA = AR.alloc([128, 8], F32, "ssA")
    xs_b2 = [AR.alloc([128, GA, D], BF16, f"xs_b{i}") for i in range(2)]
    hTA2 = [AR.alloc([128, KD, GA * 128], BF16, f"hTA{i}") for i in range(2)]
    R_tmpA2, R_hTA2 = [Res("tmpA0"), Res("tmpA1")], [Res("hTA0"), Res("hTA1")]
    cur = {"i": 0}

    class _H:
        def __getitem__(self, key):
            return hTA2[cur["i"]][key]
    hTA = _H()

    class _R:
        pass

    NST = 2
    fmo = [AR.alloc([128, 512], BF16, f"fmo{i}") for i in range(NST)]
    R_fmo = [Res(f"fmo{i}", stg=True) for i in range(NST)]
    tmo = [AR.alloc([128, 512], BF16, f"tmo{i}") for i in range(NST)]
    R_tmo = [Res(f"tmo{i}", stg=True) for i in range(NST)]
    tmf = [AR.alloc([128, 512], F32, f"tmf{i}") for i in range(NST)]
    R_tmf = [Res(f"tmf{i}", stg=True) for i in range(NST)]
    cnt = {"fm": 0, "tm": 0, "tf": 0, "x": 0}

    def fvec_mix(cnd, k):
        return fmix[:, 0, cnd, k:k + 1], fmix[:, 1, cnd, k:k + 1]

    def fm_chunk(N, col0, scale):
        bi = 2 + (cnt["fm"] % 2)
        bk = bank(bi)
        for k in range(KD):
            P.mm(bk[:, 0:N], winb[:, k, col0:col0 + 128], hTA[:, k, 0:N], k == 0, k == KD - 1,
                 (R_win, R_hTA2[cur["i"]]), (RB[bi],))
        i = cnt["fm"] % NST
        cnt["fm"] += 1
        P.act(fmo[i][:, 0:N], bk[:, 0:N], AF.Copy, (RB[bi],), (R_fmo[i],), scale=scale)
        return fmo[i], R_fmo[i]

    def tm_block(g, col0, ncols):
        bi = 4 + (cnt["tm"] % 2)
        cnt["tm"] += 1
        bk = bank(bi)
        for k in range(KD):
            P.mm(bk[:, 0:ncols], hTA[:, k, g * 128:(g + 1) * 128], winb[:, k, col0:col0 + ncols], k == 0,
                 k == KD - 1, (R_win, R_hTA2[cur["i"]]), (RB[bi],))
        return bk, bi

    def to_bf(bk, bi, ncols):
        i = cnt["tf"] % NST
        cnt["tf"] += 1
        P.cp("dve", tmo[i][:, 0:ncols], bk[:, 0:ncols], (RB[bi],), (R_tmo[i],))
        return tmo[i], R_tmo[i]

    def to_f32(bk, bi, ncols):
        i = cnt["tf"] % NST
        cnt["tf"] += 1
        P.act(tmf[i][:, 0:ncols], bk[:, 0:ncols], AF.Copy, (RB[bi],), (R_tmf[i],))
        return tmf[i], R_tmf[i]

    ssA2 = [ssA, AR.alloc([128, 8], F32, "ssA1")]
    tmpsq2 = [tmpsq, AR.alloc([128, D], BF16, "tmpsq1")]

    def front(xsrc, G, conds, i, part):
        if part == "pre":
            P.dma("pool", xt2[i][:, 0:G, :], xsrc.rearrange("(g p) d -> p g d", p=128), (R_in,), (R_xt[i],))
        norm_to_fm(xt2[i], G, D, tmpsq2[i], ssA2[i], xs_b2[i], hTA2[i], conds, fvec_mix, R_xt[i], R_tmpA2[i],
                   R_hTA2[i], KD, (0, 1), part=part)

    def logits_fm(N):
        bk = bank(6)
        for k in range(KD):
            P.mm(bk[0:NH, 0:N], winb[:, k, LF:LF + NH], hTA[:, k, 0:N], k == 0, k == KD - 1,
                 (R_win, R_hTA2[cur["i"]]), (RB[6],))
        return bk

    def logits_tm(g, slot):
        bk = bank(7)
        for k in range(KD):
            P.mm(bk[:, 0:NH], hTA[:, k, g * 128:(g + 1) * 128], winb[:, k, LF:LF + NH], k == 0, k == KD - 1,
                 (R_win, R_hTA2[cur["i"]]), (RB[7],))
        P.tt("dve", lgtok[:, slot, :], bk[:, 0:NH], bfrow, ALU.add, (RB[7], R_c), (R_lgtok,))

    CBLK = [(c0, min(512, W - c0)) for c0 in range(0, W, 512)]

    tiles = []

    def a1_p1(tt_):
        for s_, col0 in ((0, CK), (1, FK)):
            for c in range(WC):
                t_, r_ = fm_chunk(512, col0 + c * 128, 1.0)
                P.dma("sp", KTs[c, s_, :, tt_ * 512:(tt_ + 1) * 512], t_[:, 0:512], (r_,), (R_scr["KTs"],))

    def a1_p2(tt_):
        for g in range(GA):
            tok0 = tt_ * 512 + g * 128
            for s_, col0 in ((0, CV), (1, FV)):
                for (c0, nn) in CBLK:
                    bk, bi = tm_block(g, col0 + c0, nn)
                    t_, r_ = to_bf(bk, bi, nn)
                    P.dma("sp", Vs[s_, tok0:tok0 + 128, c0:c0 + nn], t_[:, 0:nn], (r_,), (R_scr["Vs"],))
        bk = logits_fm(512)
        P.cp("dve", LGT[:, tt_ * 512:(tt_ + 1) * 512], bk[0:NH, 0:512], (RB[6],), (R_LGT,))

    for tt_ in range(T // 512):
        tiles.append((xfull[tt_ * 512:(tt_ + 1) * 512, :], GA, [(0, 128, 0)],
                      (lambda tt_=tt_: a1_p1(tt_)), (lambda tt_=tt_: a1_p2(tt_))))

    def a2_p1(z):
        for s_, col0 in ((0, CQ), (1, FQ)):
            for c in range(WC):
                t_, r_ = fm_chunk(512, col0 + c * 128, 0.125)
                P.dma("sp", QTs[c, s_, :, z * 512:(z + 1) * 512], t_[:, 0:512], (r_,), (R_scr["QTs"],))

    def a2_p2(z):
        for g in range(GA):
            r0 = z * 512 + g * 128
            for (col0, od) in ((CK, o_sbk), (CV, o_sbv), (FK, o_fxk), (FV, o_fxv)):
                for (c0, nn) in CBLK:
                    bk, bi = tm_block(g, col0 + c0, nn)
                    t_, r_ = to_f32(bk, bi, nn)
                    P.dma("sp", od[r0:r0 + 128, c0:c0 + nn], t_[:, 0:nn], (r_,), (R_out,))
            logits_tm(g, z * 4 + g)

    for z in range(NZ):
        tiles.append((xown[z * 512:(z + 1) * 512, :], GA, [(0, 128, 0)],
                      (lambda z=z: a2_p1(z)), (lambda z=z: a2_p2(z))))

    def as_p1():
        for (s_, colq, colk) in ((0, CQ, CK), (1, FQ, FK)):
            for kind, col0, scale in (("q", colq, 0.125), ("k", colk, 1.0)):
                for c in range(WC):
                    t_, r_ = fm_chunk(128, col0 + c * 128, scale)
                    for half in range(2):
                        h = 2 * c + half
                        for sbi in range(2):
                            src = t_[half * 64:(half + 1) * 64, sbi * 64:(sbi + 1) * 64]
                            if kind == "q":
                                P.dma("sp", QTss[sbi, h, s_, 0:64, :], src, (r_,), (R_scr["QTss"],))
                            else:
                                P.dma("sp", KTss[sbi, h, s_, 0:64, 0:64], src, (r_,), (R_scr["KTss"],))

    def as_p2():
        for (col0, od) in ((CK, s_sbk), (CV, s_sbv), (FK, s_fxk), (FV, s_fxv)):
            for (c0, nn) in CBLK:
                bk, bi = tm_block(0, col0 + c0, nn)
                t_, r_ = to_f32(bk, bi, nn)
                P.dma("sp", od[0:128, c0:c0 + nn], t_[:, 0:nn], (r_,), (R_out,))
                if col0 in (CV, FV):
                    sidx = 0 if col0 == CV else 1
                    t2, r2 = to_bf(bk, bi, nn)
                    for sbi in range(2):
                        P.dma("sp", Vss[sbi, sidx, 0:64, c0:c0 + nn], t2[sbi * 64:(sbi + 1) * 64, 0:nn],
                              (r2,), (R_scr["Vss"],))
        logits_tm(0, NSUBO)
        bk = logits_fm(128)
        for sbi in range(2):
            P.cp("dve", LGSv[:, sbi, PAST:PAST + DT], bk[0:NH, sbi * 64:(sbi + 1) * 64], (RB[6],), (R_LGS,))

    tiles.append((xsam, 1, [(0, 64, 1), (64, 128, 2)], as_p1, as_p2))

    front(tiles[0][0], tiles[0][1], tiles[0][2], 0, "pre")
    front(tiles[0][0], tiles[0][1], tiles[0][2], 0, "tr")
    for ti, (xsrc_, G_, conds_, p1_, p2_) in enumerate(tiles):
        nx = tiles[ti + 1] if ti + 1 < len(tiles) else None
        if nx is not None:
            front(nx[0], nx[1], nx[2], (ti + 1) % 2, "pre")
        issue_precast(2)
        cur["i"] = ti % 2
        p1_()
        if nx is not None:
            front(nx[0], nx[1], nx[2], (ti + 1) % 2, "tr")
        cur["i"] = ti % 2
        p2_()
    issue_precast(1000)
    lgflat = lgtok.rearrange("p a h -> p (a h)")
    P.act(lgflat, lgflat, AF.Exp, (R_lgtok,), (R_lgtok,), scale=-1.0)
    P.act(lgflat, lgflat, AF.Ln, (R_lgtok,), (R_lgtok,), bias=1.0)
    P.ts("dve", lgflat, lgflat, -1.0, None, ALU.mult, None, (R_lgtok,), (R_lgtok,))
    with nc.allow_non_contiguous_dma(reason="small logf rows"):
        P.dma("sp", o_lf.rearrange("(a p) h -> p a h", p=128), lgtok[:, 0:NSUBO, :], (R_lgtok,), (R_out,))
        P.dma("sp", s_lf, lgtok[:, NSUBO, :], (R_lgtok,), (R_out,))

    P.barrier()
    AR.release(mA)

    mF = AR.mark()
    Gp = AR.alloc([128, T], F32, "Gp")[0:NH]
    Go = AR.alloc([128, TO], F32, "Go")[0:NH]
    Gs = AR.alloc([128, 2 * (PAST + DT)], F32, "Gs")[0:NH]
    Gsv = Gs.rearrange("h (s n) -> h s n", s=2)
    Gq = AR.alloc([128, 2 * DT], F32, "Gq")[0:NH]
    Gqv = Gq.rearrange("h (s n) -> h s n", s=2)
    pbuf = [AR.alloc([128, max(T, 2 * (PAST + DT))], BF16, f"pbuf{i}")[0:NH] for i in range(2)]
    R_G, R_Go, R_Gs, R_Gq = Res("G"), Res("Go"), Res("Gs"), Res("Gq")
    R_pb = [Res("pb0", stg=True), Res("pb1", stg=True)]
    pcount = [0]
    ktin = AR.alloc([128, PB, W], F32, "ktin")
    ktb = AR.alloc([128, PAST], BF16, "ktb")
    R_ktin, R_ktb = Res("ktin"), Res("ktb", stg=True)

    def a2c_piece(sbi, s, ck, cv):
        P.dma("pool", Vss[sbi, s, 128:128 + PAST, :], cv[sbi], (R_in,), (R_scr["Vss"],))
        P.dma("sp", Vss[sbi, s, 64:128, :], zbig[0:64, 0:W], (R_c,), (R_scr["Vss"],))
        P.dma("sp", ktin, ck[sbi].rearrange("(kb p) w -> p kb w", p=128), (R_in,), (R_ktin,))
        for c in range(WC):
            for kb0 in range(0, PB, 4):
                nk = min(4, PB - kb0)
                bi = 2 + (cnt["fm"] % 2)
                cnt["fm"] += 1
                bk = bank(bi)
                for j in range(nk):
                    P.tr(bk[:, j * 128:(j + 1) * 128], ktin[:, kb0 + j, c * 128:(c + 1) * 128], identf,
                         (R_ktin, R_c), (RB[bi],))
                P.act(ktb[:, kb0 * 128:(kb0 + nk) * 128], bk[:, 0:nk * 128], AF.Copy, (RB[bi],), (R_ktb,))
            for half in range(2):
                P.dma("sp", KTss[sbi, 2 * c + half, s, 0:64, 128:128 + PAST], ktb[half * 64:(half + 1) * 64, :],
                      (R_ktb,), (R_scr["KTss"],))
        P.dma("sp", KTss[sbi, :, s, :, 64:128].rearrange("h r c -> r h c"),
              zbig[0:70, 0:NH * 64].rearrange("r (h c) -> r h c", h=NH), (R_c,), (R_scr["KTss"],))

    a2c_jobs = [(sbi, s, ck, cv) for sbi in range(2) for (s, ck, cv) in ((0, csk, csv), (1, cfk, cfv))]

    def cumsum(dst, src, n, rsrc, rdst):
        for c0 in range(0, n, 512):
            nn = min(512, n - c0)
            init = 0.0 if c0 == 0 else dst[:, c0 - 1:c0]
            P.op("dve", lambda e, c0=c0, nn=nn, init=init: e.tensor_tensor_scan(
                out=dst[:, c0:c0 + nn], data0=onesb[0:NH, 0:nn], data1=src[:, c0:c0 + nn], initial=init,
                op0=ALU.mult, op1=ALU.add), (rsrc, rdst, R_c), (rdst,))

    def split_rows(src, n, rsrc, dst_fn):
        for j in range(3):
            i = pcount[0] % 2
            pcount[0] += 1
            pb = pbuf[i]
            P.cp("dve", pb[:, 0:n], src, (rsrc,), (R_pb[i],))
            if j < 2:
                P.tt("dve", src, src, pb[:, 0:n], ALU.subtract, (rsrc, R_pb[i]), (rsrc,))
            for (d_, s_, rd_) in dst_fn(j, pb):
                P.dma("sp", d_, s_, (R_pb[i],), (rd_,))

    def ones_rows(n, dst_list):
        i = pcount[0] % 2
        pcount[0] += 1
        P.memset("dve", pbuf[i][:, 0:n], 1.0, (R_pb[i],))
        for (d_, rd_) in dst_list:
            P.dma("sp", d_, pbuf[i][:, 0:d_.shape[-1]], (R_pb[i],), (rd_,))

    a2c_piece(*a2c_jobs[0])
    P.act(LGT, LGT, AF.Exp, (R_LGT, R_c), (R_LGT,), scale=-1.0, bias=nbft[0:NH, :])
    P.act(LGT, LGT, AF.Ln, (R_LGT,), (R_LGT,), bias=1.0)
    cumsum(Gp, LGT, T, R_LGT, R_G)
    Gv = Gp.rearrange("h (m r t) -> h m r t", r=4, t=128)
    Gov = Go.rearrange("h (m t) -> h m t", t=128)
    P.ts("dve", Gov, Gv[:, :, 0, :], selt[0:NH, 0:1], None, ALU.mult, None, (R_G, R_c), (R_Go,))
    for rr in range(1, 4):
        P.stt("dve", Gov, Gv[:, :, rr, :], selt[0:NH, rr:rr + 1], Gov, ALU.mult, ALU.add, (R_G, R_c, R_Go), (R_Go,))
    P.ts("dve", Go, Go, -1.0, None, ALU.mult, None, (R_Go,), (R_Go,))
    a2c_piece(*a2c_jobs[1])
    split_rows(Gp, T, R_G, lambda j, pb: [(KAs[:, 3 + j, :], pb[:, 0:T], R_scr["KAs"])])
    split_rows(Go, TO, R_Go, lambda j, pb: [(QAs[:, j, :], pb[:, 0:TO], R_scr["QAs"])])
    ones_rows(T, [(KAs[:, j, :], R_scr["KAs"]) for j in range(3)] +
              [(QAs[:, 3 + j, :], R_scr["QAs"]) for j in range(3)])
    a2c_piece(*a2c_jobs[2])
    P.dma("sp", LGSv[:, :, 0:PAST], clfT.rearrange("h (s n) -> h s n", s=2), (R_in,), (R_LGS,))
    P.ts("dve", LGSv[:, :, 0:PAST], LGSv[:, :, 0:PAST], -1.0, None, ALU.mult, None, (R_LGS,), (R_LGS,))
    P.act(LGSv[:, :, PAST:PAST + DT], LGSv[:, :, PAST:PAST + DT], AF.Exp, (R_LGS, R_c), (R_LGS,), scale=-1.0,
          bias=nbft[0:NH, :])
    P.act(LGSv[:, :, PAST:PAST + DT], LGSv[:, :, PAST:PAST + DT], AF.Ln, (R_LGS,), (R_LGS,), bias=1.0)
    for sbi in range(2):
        cumsum(Gsv[:, sbi, :], LGSv[:, sbi, :], PAST + DT, R_LGS, R_Gs)
    P.ts("dve", Gqv, Gsv[:, :, PAST:PAST + DT], -1.0, None, ALU.mult, None, (R_Gs,), (R_Gq,))

    def kdst_s(j, pb):
        pv = pb[:, 0:2 * (PAST + DT)].rearrange("h (s n) -> h s n", s=2)
        out = []
        for sbi in range(2):
            out.append((KTss[sbi, :, 1, 67 + j, 0:DT], pv[:, sbi, PAST:PAST + DT], R_scr["KTss"]))
            out.append((KTss[sbi, :, 1, 67 + j, 128:128 + PAST], pv[:, sbi, 0:PAST], R_scr["KTss"]))
        return out

    def qdst_s(j, pb):
        pv = pb[:, 0:2 * DT].rearrange("h (s n) -> h s n", s=2)
        return [(QTss[sbi, :, 1, 64 + j, :], pv[:, sbi, :], R_scr["QTss"]) for sbi in range(2)]

    a2c_piece(*a2c_jobs[3])
    split_rows(Gs, 2 * (PAST + DT), R_Gs, kdst_s)
    split_rows(Gq, 2 * DT, R_Gq, qdst_s)
    ones_rows(SK, [(KTss[sbi, :, 1, 64 + j, :], R_scr["KTss"]) for sbi in range(2) for j in range(3)] +
              [(QTss[sbi, :, 1, 67 + j, :], R_scr["QTss"]) for sbi in range(2) for j in range(3)])
    P.barrier()
    AR.release(mL)

    mB = AR.mark()
    maskb = AR.alloc([128, 2, 16, 512], BF16, "maskb")
    smaskb = AR.alloc([128, 2, 64], BF16, "smaskb")
    R_mask = Res("mask")
    mv = masks.rearrange("p (s k q) -> p s k q", s=2, k=16)
    for s in range(2):
        for k4 in range(0, 16, 4):
            P.dma("pool", maskb[:, s, k4:k4 + 4, :], mv[:, s, k4:k4 + 4, :], (R_in,), (R_mask,))
    P.dma("pool", smaskb, smask.rearrange("p (s q) -> p s q", s=2), (R_in,), (R_mask,))
    NKB = T // 128
    KTt = [[AR.alloc([128, T], BF16, f"KTt{s}{i}") for i in range(2)] for s in range(2)]
    Vt = [[AR.alloc([128, NKB, 128], BF16, f"Vt{s}{i}") for i in range(2)] for s in range(2)]
    QTt = [[AR.alloc([128, TO], BF16, f"QTt{s}{i}") for i in range(2)] for s in range(2)]
    R_KT = [[Res(f"KT{s}{i}") for i in range(2)] for s in range(2)]
    R_V = [[Res(f"V{s}{i}") for i in range(2)] for s in range(2)]
    R_QT = [[Res(f"QT{s}{i}") for i in range(2)] for s in range(2)]
    for s in range(2):
        for i in range(2):
            P.memset("pool", Vt[s][i], 0.0, (R_V[s][i],))
            P.memset("pool", KTt[s][i], 0.0, (R_KT[s][i],))
            P.memset("pool", QTt[s][i], 0.0, (R_QT[s][i],))
        for i in range(2):
            if s == 1:
                P.memset("dve", Vt[s][i][:, :, 64:65], 1.0, (R_V[s][i],))
    SPt = [AR.alloc([128, 512], BF16, f"SPt{i}") for i in range(2)]
    Gt = [AR.alloc([128, 512], F32, f"Gt{i}") for i in range(2)]
    at = [AR.alloc([128, 512], BF16, f"at{i}") for i in range(2)]
    Pt = [AR.alloc([128, 512], BF16, f"Pt{i}") for i in range(2)]
    R_E = [Res("E0"), Res("E1")]; R_SP = [Res("SP0"), Res("SP1")]; R_G2 = [Res("G0"), Res("G1")]
    R_a = [Res("a0"), Res("a1")]; R_P = [Res("P0"), Res("P1")]
    osb_s2 = [AR.alloc([128, 512], F32, f"osb_s{i}") for i in range(2)]
    ofx_s2 = [AR.alloc([128, 512], F32, f"ofx_s{i}") for i in range(2)]
    R_os2 = [Res("os0"), Res("os1")]
    pending = []
    fin_cnt = [0]
    otok = AR.alloc([128, 4, 2, 64], F32, "otok")
    rec = AR.alloc([128, 4], F32, "rec")
    R_os, R_otok = Res("os"), Res("otok", stg=True)
    BZ, BS, BA, BOS, BOF, BTP = (0, 1), (2, 3), 4, 5, 6, 7
    itc = [0]

    def sweep(KT, QT, Vtile, rK, rQ, rV, q0, Nq, blocks, odst_fn):
        A = bank(BA); Osb = bank(BOS); Ofx = bank(BOF)
        for (bk_, M, rb) in ((A, 128, RB[BA]), (Osb, 128, RB[BOS]), (Ofx, 128, RB[BOF])):
            P.mm(bk_[0:M, 0:Nq], zrow[0:1, 0:M], onesb[0:1, 0:Nq], True, True, (R_c,), (rb,))
        nb = len(blocks)

        def mm1(i):
            kcol, vkb, qlo, msb, mfx = blocks[i]
            b = (itc[0] + i) % 2
            Z = bank(BZ[b]); S = bank(BS[b])
            P.mm(Z[:, qlo:Nq], KT[0][:, kcol:kcol + 128], QT[0][:, q0 + qlo:q0 + Nq], True, msb is None,
                 (rK[0], rQ[0]), (RB[BZ[b]],))
            if msb is not None:
                P.mm(Z[:, qlo:Nq], identb, msb, False, True, (R_c, R_mask), (RB[BZ[b]],))
            P.mm(S[:, qlo:Nq], KT[1][:, kcol:kcol + 128], QT[1][:, q0 + qlo:q0 + Nq], True, mfx is None,
                 (rK[1], rQ[1]), (RB[BS[b]],))
            if mfx is not None:
                P.mm(S[:, qlo:Nq], identb, mfx, False, True, (R_c, R_mask), (RB[BS[b]],))

        mm1(0)
        prev_pending = list(pending)
        del pending[:]
        for i in range(nb):
            if i == min(3, nb - 1):
                for f_ in prev_pending:
                    f_()
                prev_pending = []
            kcol, vkb, qlo, msb, mfx = blocks[i]
            b = (itc[0] + i) % 2
            Z = bank(BZ[b]); S = bank(BS[b])
            sl = slice(qlo, Nq)
            P.act(Z[:, sl], Z[:, sl], AF.Exp, (RB[BZ[b]],), (RB[BZ[b]],))
            P.act(SPt[b][:, sl], Z[:, sl], AF.Ln, (RB[BZ[b]],), (R_SP[b],), bias=1.0)
            P.mm(A[:, sl], triIb, SPt[b][:, sl], False, True, (R_c, R_SP[b]), (RB[BA],))
            if i + 1 < nb:
                mm1(i + 1)
            P.act(Pt[b][:, sl], S[:, sl], AF.Exp, (RB[BS[b]],), (R_P[b],))
            P.mm(Ofx[:, sl], Vtile[1][:, vkb, :], Pt[b][:, sl], False, True, (rV[1], R_P[b]), (RB[BOF],))
            if i > 0:
                pk, pv, pq, _, _ = blocks[i - 1]
                pb_ = (itc[0] + i - 1) % 2
                P.mm(Osb[:, pq:Nq], Vtile[0][:, pv, :], at[pb_][:, pq:Nq], False, True,
                     (rV[0], R_a[pb_]), (RB[BOS],))
            P.act(Gt[b][:, sl], A[:, sl], AF.Exp, (RB[BA],), (R_G2[b],))
            P.mm(A[:, sl], compb, SPt[b][:, sl], False, True, (R_c, R_SP[b]), (RB[BA],))
            P.tt("dve", at[b][:, sl], Z[:, sl], Gt[b][:, sl], ALU.mult, (RB[BZ[b]], R_G2[b]), (R_a[b],))
        pk, pv, pq, _, _ = blocks[nb - 1]
        pb_ = (itc[0] + nb - 1) % 2
        P.mm(Osb[:, pq:Nq], Vtile[0][:, pv, :], at[pb_][:, pq:Nq], False, True, (rV[0], R_a[pb_]), (RB[BOS],))
        itc[0] += nb
        j = fin_cnt[0] % 2
        fin_cnt[0] += 1
        P.cp("dve", osb_s2[j][0:64, 0:Nq], Osb[0:64, 0:Nq], (RB[BOS],), (R_os2[j],))
        P.cp("dve", ofx_s2[j][0:65, 0:Nq], Ofx[0:65, 0:Nq], (RB[BOF],), (R_os2[j],))

        def fin(j=j, Nq=Nq, odst_fn=odst_fn):
            osb_s, ofx_s, R_os = osb_s2[j], ofx_s2[j], R_os2[j]
            nt = min(128, Nq)
            ng = max(1, Nq // 128)
            TP = bank(BTP)
            for i in range(ng):
                P.tr(TP[0:nt, i * 64:(i + 1) * 64], osb_s[0:64, i * 128:i * 128 + nt], identf[0:64, 0:64],
                     (R_os, R_c), (RB[BTP],))
            P.cp("dve", otok[0:nt, 0:ng, 0, :], TP[0:nt, 0:ng * 64].rearrange("p (g c) -> p g c", c=64),
                 (RB[BTP],), (R_otok,))
            for i in range(ng):
                P.tr(TP[0:nt, i * 65:(i + 1) * 65], ofx_s[0:65, i * 128:i * 128 + nt], identf[0:65, 0:65],
                     (R_os, R_c), (RB[BTP],))
            TFv = TP[0:nt, 0:ng * 65].rearrange("p (g c) -> p g c", c=65)
            P.op("dve", lambda e: e.reciprocal(out=rec[0:nt, 0:ng], in_=TFv[:, :, 64]), (RB[BTP],), (R_otok,))
            for i in range(ng):
                P.ts("dve", otok[0:nt, i, 1, :], TFv[:, i, 0:64], rec[0:nt, i:i + 1], None, ALU.mult, None,
                     (RB[BTP], R_otok), (R_otok,))
            for (d_, s_, rd_) in odst_fn(otok, nt, ng):
                P.dma("sp", d_, s_, (R_otok,), (rd_,))

        pending.append(fin)

    hp_count = [0]

    def load_pair(ktsrc, vsrc, qsrc, nkeys, nkb, nq, rk_scr, rv_scr, rq_scr):
        i = hp_count[0] % 2
        hp_count[0] += 1
        for s in range(2):
            for (r0, r1, src) in ktsrc(s):
                P.dma("sp", KTt[s][i][r0:r1, 0:nkeys], src, rk_scr, (R_KT[s][i],))
            for (r0, r1, src) in qsrc(s):
                P.dma("sp", QTt[s][i][r0:r1, 0:nq], src, rq_scr, (R_QT[s][i],))
            P.dma("sp", Vt[s][i][:, 0:nkb, 0:64], vsrc(s).rearrange("(kb k) d -> k kb d", k=128),
                  (rv_scr,), (R_V[s][i],))
        return ([KTt[0][i], KTt[1][i]], [QTt[0][i], QTt[1][i]], [Vt[0][i], Vt[1][i]],
                [R_KT[0][i], R_KT[1][i]], [R_QT[0][i], R_QT[1][i]], [R_V[0][i], R_V[1][i]])

    with nc.allow_non_contiguous_dma(reason="head-sliced V rows / O columns (128-256B segments)"):
        jobs = []
        for p in range(NH):
            def ld(p=p):
                hs = slice((p % 2) * 64, (p % 2) * 64 + 64)
                return load_pair(
                    lambda s: [(0, 64, KTs[p // 2, s, hs, :])] + ([(64, 70, KAs[p])] if s == 1 else []),
                    lambda s: Vs[s, :, p * 64:(p + 1) * 64],
                    lambda s: [(0, 64, QTs[p // 2, s, hs, :])] + ([(64, 70, QAs[p])] if s == 1 else []),
                    T, NKB, TO, (R_scr["KTs"], R_scr["KAs"]), R_scr["Vs"], (R_scr["QTs"], R_scr["QAs"]))

            def sw(ld_, p=p):
                KT, QT, Vl, rK, rQ, rV = ld_
                for z in range(NZ):
                    blocks = []
                    for kbz in range(15, -1, -1):
                        qlo = (kbz // 4) * 128
                        kb = 16 * z + kbz
                        blocks.append((kb * 128, kb, qlo, maskb[:, 0, kbz, qlo:512], maskb[:, 1, kbz, qlo:512]))
                    for kb in range(16 * z - 1, -1, -1):
                        blocks.append((kb * 128, kb, 0, None, None))

                    def odst(ot, nt, ng, z=z, p=p):
                        rows = Os[z * 512:(z + 1) * 512, :].rearrange("(g t) c -> t g c", t=128)
                        return [(rows[:, :, p * 64:(p + 1) * 64], ot[:, :, 0, :], R_scr["Os"]),
                                (rows[:, :, W + p * 64:W + (p + 1) * 64], ot[:, :, 1, :], R_scr["Os"])]

                    sweep(KT, QT, Vl, rK, rQ, rV, z * 512, 512, blocks, odst)
            jobs.append((ld, sw))
        for sbi in range(2):
            for p in range(NH):
                def ld(sbi=sbi, p=p):
                    return load_pair(
                        lambda s: [(0, 64 if s == 0 else 70, KTss[sbi, p, s, 0:(64 if s == 0 else 70), :])],
                        lambda s: Vss[sbi, s, :, p * 64:(p + 1) * 64],
                        lambda s: [(0, 64 if s == 0 else 70, QTss[sbi, p, s, 0:(64 if s == 0 else 70), :])],
                        SK, SKB, NQS, (R_scr["KTss"],), R_scr["Vss"], (R_scr["QTss"],))

                def sw(ld_, sbi=sbi, p=p):
                    KT, QT, Vl, rK, rQ, rV = ld_
                    blocks = [(0, 0, 0, smaskb[:, 0, :], smaskb[:, 1, :])]
                    for kb in range(PB - 1, -1, -1):
                        blocks.append((128 + kb * 128, 1 + kb, 0, None, None))

                    def odst_s(ot, nt, ng, sbi=sbi, p=p):
                        return [(Oss[sbi * 64:(sbi + 1) * 64, p * 64:(p + 1) * 64], ot[0:64, 0, 0, :], R_scr["Oss"]),
                                (Oss[sbi * 64:(sbi + 1) * 64, W + p * 64:W + (p + 1) * 64], ot[0:64, 0, 1, :],
                                 R_scr["Oss"])]

                    sweep(KT, QT, Vl, rK, rQ, rV, 0, NQS, blocks, odst_s)
                jobs.append((ld, sw))
        nxt = jobs[0][0]()
        for ji, (ld, sw) in enumerate(jobs):
            cur_ld = nxt
            if ji + 1 < len(jobs):
                nxt = jobs[ji + 1][0]()
            sw(cur_ld)
    for f_ in pending:
        f_()
    del pending[:]
    P.barrier()
    AR.release(mB)

    GC = 2
    wob = AR.alloc([128, MC, D], BF16, "wob")
    wgb = AR.alloc([128, KD, DFF], BF16, "wgb")
    wub = AR.alloc([128, KD, DFF], BF16, "wub")
    wdb = AR.alloc([128, FFC, D], BF16, "wdb")
    R_wo, R_wg, R_wu, R_wd = Res("wo"), Res("wg"), Res("wu"), Res("wd")
    for (dst, src, nk, rw_, rs_) in ((wob, wos, MC, R_wo, R_wsc["wo"]), (wgb, wgs, KD, R_wg, R_wsc["wg"]),
                                     (wub, wus, KD, R_wu, R_wsc["wu"]), (wdb, wds, FFC, R_wd, R_wsc["wd"])):
        sv = src.rearrange("(k p) c -> p k c", p=128)
        for k0 in range(0, nk, 4):
            k1 = min(nk, k0 + 4)
            P.dma("sp", dst[:, k0:k1, :], sv[:, k0:k1, :], (rs_,), (rw_,))
    gp2 = AR.alloc([128, D], F32, "gp2"); gp5 = AR.alloc([128, D], F32, "gp5")
    gfin = AR.alloc([128, D], F32, "gfin")
    R_g = Res("gates")
    P.dma("sp", gp2, adas[0, 2, :].partition_broadcast(128), (R_scr["adas"],), (R_g,))
    P.dma("sp", gp5, adas[0, 5, :].partition_broadcast(128), (R_scr["adas"],), (R_g,))
    P.dma("sp", gfin, g_final.partition_broadcast(128), (R_in,), (R_g,))
    xc = AR.alloc([128, GC, D], F32, "xc")
    ocy = AR.alloc([128, GC, max(D, MIX)], F32, "ocy")
    oc = ocy[:, :, 0:MIX]
    yc = ocy[:, :, 0:D]
    nb_ = AR.alloc([128, GC, max(D, MIX)], BF16, "nb_")
    fT = AR.alloc([128, max(KD, MC), GC * 128], BF16, "fT")
    actT = AR.alloc([128, FFC, GC * 128], BF16, "actT")
    tsq = AR.alloc([128, max(D, MIX)], BF16, "tsq")
    ssC = AR.alloc([128, 8], F32, "ssC")
    tmpf = AR.alloc([128, 512], F32, "tmpf")
    sg = AR.alloc([128, GC * 128], F32, "sg")
    R_xc, R_oc, R_nb, R_fT, R_actT, R_tmpf, R_sg = (Res("xc"), Res("oc", stg=True), Res("nb"), Res("fT"),
                                                   Res("actT"), Res("tmpf"), Res("sg"))
    R_yc = R_oc

    def fvec_ffn(cnd, k):
        return fffn[:, 0, cnd, k:k + 1], fffn[:, 1, cnd, k:k + 1]

    def post_group(xsrc, osrc, ydst, G, conds, g2, g5):
        N = G * 128
        P.dma("sp", xc[:, 0:G, :], xsrc.rearrange("(g p) d -> p g d", p=128), (R_in,), (R_xc,))
        P.dma("sp", oc[:, 0:G, :], osrc.rearrange("(g p) d -> p g d", p=128), (R_scr["Os"], R_scr["Oss"]), (R_oc,))
        for s in range(2):
            norm_to_fm(oc[:, :, s * W:(s + 1) * W], G, W, tsq, ssC, nb_[:, :, 0:W],
                       fT[:, s * WC:(s + 1) * WC, :], [(0, 128, 0)],
                       lambda cnd, k, s=s: (goTt[:, s * WC + k:s * WC + k + 1], None),
                       R_oc, R_nb, R_fT, WC, (0, 1))
        for g in range(G):
            for nb2 in range(D // 512):
                cs = slice(nb2 * 512, (nb2 + 1) * 512)
                bi = 2 + ((g * (D // 512) + nb2) % 2)
                bk = bank(bi)
                for c in range(MC):
                    P.mm(bk, fT[:, c, g * 128:(g + 1) * 128], wob[:, c, cs], c == 0, c == MC - 1, (R_fT, R_wo), (RB[bi],))
                P.tt("dve", tmpf, bk, g2[:, cs], ALU.mult, (RB[bi], R_g), (R_tmpf,))
                P.tt("dve", xc[:, g, cs], xc[:, g, cs], tmpf, ALU.add, (R_xc, R_tmpf), (R_xc,))
        norm_to_fm(xc, G, D, tsq, ssC, nb_[:, :, 0:D], fT[:, 0:KD, :], conds, fvec_ffn, R_xc, R_nb, R_fT, KD, (0, 1))
        for fc in range(FFC):
            bg, bu = 4 + (fc % 2), 6 + (fc % 2)
            for k in range(KD):
                P.mm(bank(bg)[:, 0:N], wgb[:, k, fc * 128:(fc + 1) * 128], fT[:, k, 0:N], k == 0, k == KD - 1,
                     (R_wg, R_fT), (RB[bg],))
            for k in range(KD):
                P.mm(bank(bu)[:, 0:N], wub[:, k, fc * 128:(fc + 1) * 128], fT[:, k, 0:N], k == 0, k == KD - 1,
                     (R_wu, R_fT), (RB[bu],))
            P.act(sg[:, 0:N], bank(bg)[:, 0:N], AF.Silu, (RB[bg],), (R_sg,))
            P.tt("dve", actT[:, fc, 0:N], sg[:, 0:N], bank(bu)[:, 0:N], ALU.mult, (R_sg, RB[bu]), (R_actT,))
        for g in range(G):
            for nb2 in range(D // 512):
                cs = slice(nb2 * 512, (nb2 + 1) * 512)
                bi = 2 + ((g * (D // 512) + nb2) % 2)
                bk = bank(bi)
                for fc in range(FFC):
                    P.mm(bk, actT[:, fc, g * 128:(g + 1) * 128], wdb[:, fc, cs], fc == 0, fc == FFC - 1,
                         (R_actT, R_wd), (RB[bi],))
                P.tt("dve", tmpf, bk, g5[:, cs], ALU.mult, (RB[bi], R_g), (R_tmpf,))
                P.tt("dve", xc[:, g, cs], xc[:, g, cs], tmpf, ALU.add, (R_xc, R_tmpf), (R_xc,))
        for g in range(G):
            P.memset("dve", ssC[:, g:g + 1], 0.0, (R_nb,))
            P.act(tsq[:, 0:D], xc[:, g, :], AF.Square, (R_xc, R_nb), (R_nb,), accum_out=ssC[:, g:g + 1])
        P.ts("dve", ssC[:, 0:G], ssC[:, 0:G], 1.0 / D, EPS, ALU.mult, ALU.add, (R_nb,), (R_nb,))
        P.act(ssC[:, 0:G], ssC[:, 0:G], AF.Ln, (R_nb,), (R_nb,))
        P.act(ssC[:, 0:G], ssC[:, 0:G], AF.Exp, (R_nb,), (R_nb,), scale=-0.5)
        for g in range(G):
            P.stt("dve", yc[:, g, :], xc[:, g, :], ssC[:, g:g + 1], gfin, ALU.mult, ALU.mult, (R_xc, R_nb, R_g), (R_yc,))
        P.dma("sp", ydst.rearrange("(g p) d -> p g d", p=128), yc[:, 0:G, :], (R_yc,), (R_out,))

    for gi in range(TO // (GC * 128)):
        rs = slice(gi * GC * 128, (gi + 1) * GC * 128)
        post_group(xown[rs, :], Os[rs, :], y_own[rs, :], GC, [(0, 128, 0)], gp2, gp5)
    for sbi in range(2):
        P.dma("sp", gp2[sbi * 64:(sbi + 1) * 64, :], adas[1 + sbi, 2, :].partition_broadcast(64),
              (R_scr["adas"],), (R_g,))
        P.dma("sp", gp5[sbi * 64:(sbi + 1) * 64, :], adas[1 + sbi, 5, :].partition_broadcast(64),
              (R_scr["adas"],), (R_g,))
    post_group(xsam, Oss, y_sam, 1, [(0, 64, 1), (64, 128, 2)], gp2, gp5)
    P.barrier()

    print("arena high-water", AR.hw, "of", ARENA_BYTES, "ops", {e: len(P.q[e]) for e in ENGS}, "sems", P.nsem, flush=True)
    blk = es.enter_context(nc.Block())
    P.emit(blk)
    es.close()
    return nc


def make_consts():
    c = np.zeros((128, 640), np.float32)
    s = np.arange(128)[:, None]
    j = np.arange(128)[None, :]
    c[:, 0:128] = (s == j)
    c[:, 128:256] = -(s >= j).astype(np.float32)
    c[:, 256:384] = -(s < j).astype(np.float32)
    return c


def make_masks(r):
    m = np.zeros((128, 2, 16, 512), np.float32)
    j = np.arange(128)[:, None]
    for kbz in range(16):
        for i in range(4):
            key = kbz * 128 + j
            q = (4 * i + r) * 128 + np.arange(128)[None, :]
            m[:, 0, kbz, i * 128:(i + 1) * 128] = np.where(key < q, 0.0, NEG)
            m[:, 1, kbz, i * 128:(i + 1) * 128] = np.where(key <= q, 0.0, NEG)
    sm = np.zeros((128, 2, 64), np.float32)
    q = np.arange(64)[None, :]
    sm[:, 0, :] = np.where((j < q) & (j < 64), 0.0, NEG)
    sm[:, 1, :] = np.where((j <= q) & (j < 64), 0.0, NEG)
    return m.reshape(128, -1), sm.reshape(128, -1)


_NC_CACHE = {}


def run(cfg, inputs):
    D, NH, T, DFF, PAST, DT = cfg["D"], cfg["NH"], cfg["T"], cfg["DFF"], cfg["PAST"], cfg["DT"]
    W = NH * 64
    KD = D // 128
    key = tuple(sorted(cfg.items()))
    nc = build(cfg)
    f = lambda a: np.ascontiguousarray(np.asarray(a, dtype=np.float32))
    xp, xs = f(inputs["x_prompt"]), f(inputs["x_sample"])
    cp, cs = f(inputs["c_prompt"]), f(inputs["c_sample"])
    NSUB = T // 128
    in_maps = []
    cst = make_consts()
    for c in range(8):
        b, r = c // 4, c % 4
        own = np.arange(r, NSUB, 4)
        xo = xp[b].reshape(NSUB, 128, D)[own].reshape(-1, D)
        crows = np.stack([cp[b], cs[2 * c], cs[2 * c + 1]], axis=1)
        cTm = crows.reshape(KD, 128, 3).transpose(1, 0, 2).reshape(128, KD * 3)
        mk, smk = make_masks(r)
        selv = np.zeros((128, 4), np.float32); selv[:, r] = 1.0
        go = np.concatenate([f(inputs["g_sb_out"])[0], f(inputs["g_fox_out"])[0]])
        goT = go.reshape(-1, 128).T
        m = {
            "xfull": xp[b], "xown": xo, "xsam": xs[2 * c:2 * c + 2].reshape(128, D), "cT": cTm,
            "csk": f(inputs["cache_sb_k"])[0, 2 * c:2 * c + 2].reshape(2, PAST, W),
            "csv": f(inputs["cache_sb_v"])[0, 2 * c:2 * c + 2].reshape(2, PAST, W),
            "cfk": f(inputs["cache_fox_k"])[0, 2 * c:2 * c + 2].reshape(2, PAST, W),
            "cfv": f(inputs["cache_fox_v"])[0, 2 * c:2 * c + 2].reshape(2, PAST, W),
            "clfT": f(inputs["cache_fox_logf"])[0, 2 * c:2 * c + 2].transpose(2, 0, 1).reshape(NH, 2 * PAST),
            "w_ada": f(inputs["w_ada"])[0], "b_ada": f(inputs["b_ada"])[0], "w_in": f(inputs["w_in"])[0],
            "w_o": f(inputs["w_o"])[0], "w_gate": f(inputs["w_gate"])[0], "w_up": f(inputs["w_up"])[0],
            "w_down": f(inputs["w_down"])[0], "g_mix": f(inputs["g_mix"])[0], "g_ffn": f(inputs["g_ffn"])[0],
            "g_final": f(inputs["g_final"]), "goT": goT, "b_f": f(inputs["b_f"])[0],
            "cst": cst, "masks": mk, "smask": smk, "sel": selv,
        }
        in_maps.append({k: np.ascontiguousarray(v, dtype=np.float32) for k, v in m.items()})
    if cfg.get("_prep_only"):
        return nc, in_maps
    res = run_bass_kernel_spmd(nc, in_maps, core_ids=list(range(8)))
    R = res.results
    yp = np.zeros((2, T, D), np.float32)
    outs_p = {k: np.zeros((1, 2, T, NH, 64), np.float32) for k in ("o_sbk", "o_sbv", "o_fxk", "o_fxv")}
    lfp = np.zeros((1, 2, T, NH), np.float32)
    ysm = np.zeros((16, DT, D), np.float32)
    outs_s = {k: np.zeros((1, 16, DT, NH, 64), np.float32) for k in ("s_sbk", "s_sbv", "s_fxk", "s_fxv")}
    lfs = np.zeros((1, 16, DT, NH), np.float32)
    for c in range(8):
        b, r = c // 4, c % 4
        own = np.arange(r, NSUB, 4)
        yp[b].reshape(NSUB, 128, D)[own] = R[c]["y_own"].reshape(-1, 128, D)
        for k in outs_p:
            outs_p[k][0, b].reshape(NSUB, 128, NH, 64)[own] = R[c][k].reshape(-1, 128, NH, 64)
        lfp[0, b].reshape(NSUB, 128, NH)[own] = R[c]["o_lf"].reshape(-1, 128, NH)
        ysm[2 * c:2 * c + 2] = R[c]["y_sam"].reshape(2, DT, D)
        for k in outs_s:
            outs_s[k][0, 2 * c:2 * c + 2] = R[c][k].reshape(2, DT, NH, 64)
        lfs[0, 2 * c:2 * c + 2] = R[c]["s_lf"].reshape(2, DT, NH)
    return (yp, ysm, outs_p["o_sbk"], outs_p["o_sbv"], outs_p["o_fxk"], outs_p["o_fxv"], lfp,
            outs_s["s_sbk"], outs_s["s_sbv"], outs_s["s_fxk"], outs_s["s_fxv"], lfs)


def kernel(**inputs):
    return run(CFG_FULL, inputs)
```

```python
import numpy as np
import ml_dtypes
from contextlib import ExitStack
import concourse.bass as bass
import concourse.mybir as mybir
from concourse.bass_utils import run_bass_kernel_spmd

F32 = mybir.dt.float32
BF16 = mybir.dt.bfloat16
U8 = mybir.dt.uint8
AF = mybir.ActivationFunctionType
ALU = mybir.AluOpType
EPS = 1e-6
NEG = -30000.0
ENGS = ("pe", "act", "dve", "pool", "sp")
MAXV = 30000

CFG_FULL = dict(D=1024, NH=8, T=8192, DFF=2816, PAST=1024, DT=64)


class Res:
    __slots__ = ("name", "w", "rs", "cw", "stg")

    def __init__(self, name, cw=False, stg=False):
        self.name = name
        self.w = {}
        self.rs = {}
        self.cw = cw
        self.stg = stg


DMA_SLOTS = {"sp": 4, "pool": 2}


class Prog:
    def __init__(self, nc, es):
        self.nc, self.es = nc, es
        self.q = {e: [] for e in ENGS}
        self.sem = {}
        self.cnt = {}
        self.seen = {e: {} for e in ENGS}
        self.nsem = 0
        for e in ENGS:
            self._rot(e)
        self.slots = {e: [[self._newsem(f"d{e}{k}"), 0] for k in range(n)] for e, n in DMA_SLOTS.items()}
        self.dn = {e: 0 for e in DMA_SLOTS}

    def _newsem(self, nm):
        self.nsem += 1
        return self.es.enter_context(self.nc.semaphore(f"{nm}_{self.nsem}"))

    def _rot(self, e):
        self.sem[e] = self._newsem("s" + e)
        self.cnt[e] = 0

    def _deps(self, eng, reads, writes, cdma=False):
        evs = {}

        def add(ev):
            s, v = ev[0], ev[1]
            k = id(s)
            if k not in evs or evs[k][1] < v:
                evs[k] = (s, v)

        for r in reads:
            for ev in r.w.values():
                add(ev)
        for w in writes:
            for ev in w.w.values():
                if cdma and w.cw and ev[2]:
                    continue
                add(ev)
            for ev in w.rs.values():
                add(ev)
        waits = []
        seen = self.seen[eng]
        for k, (s, v) in evs.items():
            if eng == "pe" and s is self.sem["pe"]:
                continue
            if seen.get(k, 0) >= v:
                continue
            seen[k] = v
            waits.append((s, v))
        return waits

    def _mark(self, ev, reads, writes, cdma=False):
        k = id(ev[0])
        for r in reads:
            r.rs[k] = (ev[0], ev[1])
        for w in writes:
            if cdma and w.cw:
                w.w[k] = (ev[0], ev[1], True)
            else:
                w.w = {k: (ev[0], ev[1], False)}
            w.rs = {}

    def op(self, eng, fn, reads=(), writes=()):
        waits = self._deps(eng, reads, writes)
        if self.cnt[eng] >= MAXV:
            self._rot(eng)
        self.cnt[eng] += 1
        ev = (self.sem[eng], self.cnt[eng])
        self.q[eng].append((waits, fn, (ev[0], 1)))
        self._mark(ev, reads, writes)

    def dma(self, eng, out, in_, reads=(), writes=()):
        nsl = len(self.slots[eng])
        slot = self.slots[eng][self.dn[eng] % nsl]
        self.dn[eng] += 1
        if slot[1] >= MAXV:
            slot[0] = self._newsem(f"d{eng}r")
            slot[1] = 0
        sem = slot[0]
        waits = self._deps(eng, reads, writes, cdma=True)
        if slot[1] > 0 and self.seen[eng].get(id(sem), 0) < slot[1]:
            self.seen[eng][id(sem)] = slot[1]
            waits.append((sem, slot[1]))
        slot[1] += 16
        ev = (sem, slot[1])
        self.q[eng].append((waits, lambda e: e.dma_start(out=out, in_=in_, allow_slow_non_contiguous=True), (sem, 16)))
        self._mark(ev, reads, writes, cdma=True)

    def barrier(self):
        evs = [(self.sem[e], self.cnt[e]) for e in ENGS if self.cnt[e] > 0]
        for e in self.slots:
            for (sm, c) in self.slots[e]:
                if c > 0:
                    evs.append((sm, c))
        for e in ENGS:
            waits = []
            for s, v in evs:
                if s is self.sem[e]:
                    continue
                if self.seen[e].get(id(s), 0) >= v:
                    continue
                self.seen[e][id(s)] = v
                waits.append((s, v))
            if waits:
                self.q[e].append((waits, None, None))

    def emit(self, blk):
        def run(eng_name):
            def body(e):
                for waits, fn, inc in self.q[eng_name]:
                    for s, v in waits:
                        e.wait_ge(s, v)
                    if fn is not None:
                        ins = fn(e)
                        ins.then_inc(inc[0], inc[1])
            return body

        blk.tensor(run("pe"))
        blk.scalar(run("act"))
        blk.vector(run("dve"))
        blk.gpsimd(run("pool"))
        blk.sync(run("sp"))

    def mm(self, out, lhsT, rhs, start, stop, reads, writes):
        self.op("pe", lambda e: e.matmul(out, lhsT=lhsT, rhs=rhs, start=start, stop=stop,
                                         skip_group_check=True), reads, writes)

    def tr(self, out, in_, ident, reads, writes):
        self.op("pe", lambda e: e.transpose(out, in_, ident), reads, writes)

    def act(self, out, in_, func, reads, writes, bias=0.0, scale=1.0, accum_out=None):
        if accum_out is None:
            self.op("act", lambda e: e.activation(out=out, in_=in_, func=func, bias=bias, scale=scale),
                    reads, writes)
        else:
            self.op("act", lambda e: e.activation(out=out, in_=in_, func=func, bias=bias, scale=scale,
                                                  accum_out=accum_out), reads, writes)

    def ts(self, eng, out, in0, s1, s2, op0, op1, reads, writes):
        if s2 is None:
            self.op(eng, lambda e: e.tensor_scalar(out=out, in0=in0, scalar1=s1, scalar2=None, op0=op0),
                    reads, writes)
        else:
            self.op(eng, lambda e: e.tensor_scalar(out=out, in0=in0, scalar1=s1, scalar2=s2, op0=op0, op1=op1),
                    reads, writes)

    def tt(self, eng, out, in0, in1, op, reads, writes):
        self.op(eng, lambda e: e.tensor_tensor(out=out, in0=in0, in1=in1, op=op), reads, writes)

    def stt(self, eng, out, in0, scalar, in1, op0, op1, reads, writes):
        self.op(eng, lambda e: e.scalar_tensor_tensor(out=out, in0=in0, scalar=scalar, in1=in1, op0=op0, op1=op1),
                reads, writes)

    def cp(self, eng, out, in_, reads, writes):
        self.op(eng, lambda e: e.tensor_copy(out=out, in_=in_), reads, writes)

    def memset(self, eng, ap, val, writes):
        self.op(eng, lambda e: e.memset(ap, val), (), writes)


class Arena:
    def __init__(self, t, nbytes):
        self.t, self.n, self.off = t, nbytes, 0

    def mark(self):
        return self.off

    def release(self, m):
        self.off = m

    def alloc(self, shape, dt, name=""):
        esz = 4 if dt == F32 else 2
        n = 1
        for s in shape[1:]:
            n *= s
        nb = (n * esz + 63) // 64 * 64
        assert self.off + nb <= self.n, f"arena overflow {name}: {self.off}+{nb}>{self.n}"
        self.hw = max(getattr(self, "hw", 0), self.off + nb)
        v = self.t[:, self.off:self.off + n * esz].bitcast(dt)
        self.off += nb
        if len(shape) == 3:
            v = v.rearrange("p (a b) -> p a b", a=shape[1])
        elif len(shape) == 4:
            v = v.rearrange("p (a b c) -> p a b c", a=shape[1], b=shape[2])
        if shape[0] < 128:
            v = v[0:shape[0]]
        return v


def build(cfg):
    D, NH, T, DFF, PAST, DT = cfg["D"], cfg["NH"], cfg["T"], cfg["DFF"], cfg["PAST"], cfg["DT"]
    KD = D // 128
    W = NH * 64
    WC = W // 128
    MIX = 2 * W
    MC = MIX // 128
    IN = 6 * W + NH
    FFC = DFF // 128
    NZ = T // 2048
    TO = T // 4
    NSUBO = TO // 128
    PB = PAST // 128
    SKB = PB + 1
    SK = SKB * 128
    NQS = DT
    assert DT == 64 and T % 2048 == 0 and D % 512 == 0 and W % 128 == 0
    CQ, CK, CV = 0, W, 2 * W
    FQ, FK, FV, LF = 3 * W, 4 * W, 5 * W, 6 * W

    nc = bass.Bass("TRN2", target_bir_lowering=False)

    def din(name, shape):
        return nc.dram_tensor(name, list(shape), F32, kind="ExternalInput").ap()

    def dout(name, shape):
        return nc.dram_tensor(name, list(shape), F32, kind="ExternalOutput").ap()

    def dscr(name, shape, dt):
        return nc.dram_tensor(name, list(shape), dt, kind="Internal").ap()

    xfull = din("xfull", (T, D)); xown = din("xown", (TO, D)); xsam = din("xsam", (128, D))
    cT = din("cT", (128, KD * 3))
    csk = din("csk", (2, PAST, W)); csv = din("csv", (2, PAST, W))
    cfk = din("cfk", (2, PAST, W)); cfv = din("cfv", (2, PAST, W)); clfT = din("clfT", (NH, 2 * PAST))
    w_ada = din("w_ada", (D, 6 * D)); b_ada = din("b_ada", (6 * D,))
    w_in = din("w_in", (D, IN)); w_o = din("w_o", (MIX, D))
    w_gate = din("w_gate", (D, DFF)); w_up = din("w_up", (D, DFF)); w_down = din("w_down", (DFF, D))
    g_mix = din("g_mix", (D,)); g_ffn = din("g_ffn", (D,)); g_final = din("g_final", (D,))
    goT = din("goT", (128, MC)); b_f = din("b_f", (NH,))
    cst = din("cst", (128, 640)); masks = din("masks", (128, 2 * 16 * 512)); smask = din("smask", (128, 2 * 64))
    sel = din("sel", (128, 4))

    y_own = dout("y_own", (TO, D)); y_sam = dout("y_sam", (128, D))
    o_sbk = dout("o_sbk", (TO, W)); o_sbv = dout("o_sbv", (TO, W)); o_fxk = dout("o_fxk", (TO, W))
    o_fxv = dout("o_fxv", (TO, W)); o_lf = dout("o_lf", (TO, NH))
    s_sbk = dout("s_sbk", (128, W)); s_sbv = dout("s_sbv", (128, W)); s_fxk = dout("s_fxk", (128, W))
    s_fxv = dout("s_fxv", (128, W)); s_lf = dout("s_lf", (128, NH))

    KTs = dscr("KTs", (NH // 2, 2, 128, T), BF16)
    KAs = dscr("KAs", (NH, 6, T), BF16)
    QAs = dscr("QAs", (NH, 6, TO), BF16)
    Vs = dscr("Vs", (2, T, W), BF16)
    QTs = dscr("QTs", (NH // 2, 2, 128, TO), BF16)
    Os = dscr("Os", (TO, MIX), F32)
    KTss = dscr("KTss", (2, NH, 2, 70, SK), BF16)
    Vss = dscr("Vss", (2, 2, SK, W), BF16)
    QTss = dscr("QTss", (2, NH, 2, 70, NQS), BF16)
    Oss = dscr("Oss", (128, MIX), F32)
    adas = dscr("adas", (3, 6, D), F32)
    wos = dscr("wos", (MIX, D), BF16); wgs = dscr("wgs", (D, DFF), BF16)
    wus = dscr("wus", (D, DFF), BF16); wds = dscr("wds", (DFF, D), BF16)

    es = ExitStack()
    ARENA_BYTES = cfg.get("ARENA", 207 * 1024)
    arena_t = es.enter_context(nc.sbuf_tensor("arena", [128, ARENA_BYTES], U8))
    ps = es.enter_context(nc.psum_tensor("ps", [128, 4096], F32))
    P = Prog(nc, es)
    AR = Arena(arena_t, ARENA_BYTES)

    def bank(i):
        return ps[:, i * 512:(i + 1) * 512]

    def bank_bf(i):
        return ps[:, i * 512:(i + 1) * 512].bitcast(BF16)

    RB = [Res(f"bank{i}") for i in range(8)]
    R_scr = {k: Res(k) for k in ("KTs", "Vs", "QTs", "Os", "KTss", "Vss", "QTss", "Oss", "adas", "KAs", "QAs")}
    R_in = Res("inputs")
    R_out = Res("outputs")

    identf = AR.alloc([128, 128], F32, "identf")
    identb = AR.alloc([128, 128], BF16, "identb")
    triIb = AR.alloc([128, 128], BF16, "triI")
    compb = AR.alloc([128, 128], BF16, "comp")
    zrow = AR.alloc([128, 128], BF16, "zrow")
    onesb = AR.alloc([128, 512], BF16, "onesb")
    selt = AR.alloc([128, 4], F32, "sel")
    goTt = AR.alloc([128, MC], F32, "goT")
    bft = AR.alloc([128, 1], F32, "bft")
    nbft = AR.alloc([128, 1], F32, "nbft")
    bfrow = AR.alloc([128, NH], F32, "bfrow")
    scT = AR.alloc([128, KD, 3], F32, "scT")
    fmix = AR.alloc([128, 2, 3, KD], F32, "fmix")
    fffn = AR.alloc([128, 2, 3, KD], F32, "fffn")
    R_c = Res("consts")
    R_f = Res("fvecs")

    P.dma("sp", identf, cst[:, 0:128], (R_in,), (R_c,))
    P.dma("pool", identb, cst[:, 0:128], (R_in,), (R_c,))
    P.dma("pool", triIb, cst[:, 128:256], (R_in,), (R_c,))
    P.dma("pool", compb, cst[:, 256:384], (R_in,), (R_c,))
    P.dma("pool", zrow, cst[:, 384:512], (R_in,), (R_c,))
    P.dma("sp", selt, sel, (R_in,), (R_c,))
    P.dma("sp", goTt, goT, (R_in,), (R_c,))
    P.dma("sp", bft[0:NH, :], b_f.rearrange("(h o) -> h o", o=1), (R_in,), (R_c,))
    P.dma("sp", bfrow, b_f.partition_broadcast(128), (R_in,), (R_c,))
    P.memset("dve", onesb, 1.0, (R_c,))
    P.ts("dve", nbft[0:NH, :], bft[0:NH, :], -1.0, None, ALU.mult, None, (R_c,), (R_c,))

    m0 = AR.mark()
    ctile = AR.alloc([128, KD * 3], F32, "ctile")
    ctmp = AR.alloc([128, KD * 3], F32, "ctmp")
    arow = AR.alloc([128, 6 * D], F32, "arow")[0:3]
    brow = AR.alloc([128, 6 * D], F32, "brow")[0:3]
    grow = AR.alloc([128, 2 * D], F32, "grow")[0:3]
    wadab = [AR.alloc([128, KD, 512], F32, f"wada{i}") for i in range(2)]
    R_ct, R_arow, R_wada = Res("ct"), Res("arow", stg=True), [Res("wada0"), Res("wada1")]
    P.dma("sp", ctile, cT, (R_in,), (R_ct,))
    P.dma("sp", brow, b_ada.partition_broadcast(3), (R_in,), (R_arow,))
    P.dma("sp", grow[:, 0:D], g_mix.partition_broadcast(3), (R_in,), (R_arow,))
    P.dma("sp", grow[:, D:2 * D], g_ffn.partition_broadcast(3), (R_in,), (R_arow,))
    P.act(ctmp, ctile, AF.Exp, (R_ct,), (R_ct,), scale=-1.0)
    P.ts("dve", ctmp, ctmp, 1.0, None, ALU.add, None, (R_ct,), (R_ct,))
    P.op("dve", lambda e: e.reciprocal(out=ctmp, in_=ctmp), (R_ct,), (R_ct,))
    P.tt("dve", scT.rearrange("p k c -> p (k c)"), ctile, ctmp, ALU.mult, (R_ct,), (R_c,))
    wadav = w_ada.rearrange("(k p) c -> p k c", p=128)
    NAC = 6 * D // 512
    for j in range(NAC):
        wb, rw = wadab[j % 2], R_wada[j % 2]
        P.dma("sp", wb, wadav[:, :, j * 512:(j + 1) * 512], (R_in,), (rw,))
        bk = bank(j % 2)
        for k in range(KD):
            P.mm(bk[0:3, :], scT[:, k, :], wb[:, k, :], k == 0, k == KD - 1, (rw, R_c), (RB[j % 2],))
        P.tt("dve", arow[:, j * 512:(j + 1) * 512], bk[0:3, :], brow[:, j * 512:(j + 1) * 512], ALU.add,
             (RB[j % 2], R_arow), (R_arow,))
    for idx, gsrc in ((1, 0), (4, 1)):
        P.stt("dve", arow[:, idx * D:(idx + 1) * D], arow[:, idx * D:(idx + 1) * D], 1.0,
              grow[:, gsrc * D:(gsrc + 1) * D], ALU.add, ALU.mult, (R_arow,), (R_arow,))
    for idx in (2, 5):
        P.ts("dve", arow[:, idx * D:(idx + 1) * D], arow[:, idx * D:(idx + 1) * D], 1.0, None, ALU.add, None,
             (R_arow,), (R_arow,))
    P.dma("sp", adas.rearrange("c s d -> c (s d)"), arow, (R_arow,), (R_scr["adas"],))
    with nc.allow_non_contiguous_dma(reason="tiny feature-major vector loads"):
        for cnd in range(3):
            for (dst, a_sc, a_sh) in ((fmix, 1, 0), (fffn, 4, 3)):
                P.dma("sp", dst[:, 0, cnd, :], adas[cnd, a_sc, :].rearrange("(k p) -> p k", p=128),
                      (R_scr["adas"],), (R_f,))
                P.dma("sp", dst[:, 1, cnd, :], adas[cnd, a_sh, :].rearrange("(k p) -> p k", p=128),
                      (R_scr["adas"],), (R_f,))
    P.barrier()
    AR.release(m0)

    def norm_to_fm(xt, G, width, tmpsq, ss, xs_b, hT, conds, fvec, Rx, Rtmp, RhT, nchunk, tbank, part="all"):
        if part in ("all", "pre"):
            for g in range(G):
                P.memset("dve", ss[:, g:g + 1], 0.0, (Rtmp,))
                P.act(tmpsq[:, 0:width], xt[:, g, :], AF.Square, (Rx, Rtmp), (Rtmp,), accum_out=ss[:, g:g + 1])
            P.ts("dve", ss[:, 0:G], ss[:, 0:G], 1.0 / width, EPS, ALU.mult, ALU.add, (Rtmp,), (Rtmp,))
            P.act(ss[:, 0:G], ss[:, 0:G], AF.Ln, (Rtmp,), (Rtmp,))
            P.act(ss[:, 0:G], ss[:, 0:G], AF.Exp, (Rtmp,), (Rtmp,), scale=-0.5)
            for g in range(G):
                P.ts("dve", xs_b[:, g, :], xt[:, g, :], ss[:, g:g + 1], None, ALU.mult, None, (Rx, Rtmp), (Rtmp,))
        if part == "pre":
            return
        for k in range(nchunk):
            bi = tbank[k % len(tbank)]
            pb = bank_bf(bi)
            for g in range(G):
                P.tr(pb[:, g * 128:(g + 1) * 128], xs_b[:, g, k * 128:(k + 1) * 128], identb, (Rtmp, R_c), (RB[bi],))
            for (lo, hi, cnd) in conds:
                sc_ap, sh_ap = fvec(cnd, k)
                src = pb[:, 0:G * 128].rearrange("p (g t) -> p g t", g=G)[:, :, lo:hi]
                dst = hT[:, k, 0:G * 128].rearrange("p (g t) -> p g t", g=G)[:, :, lo:hi]
                if sh_ap is None:
                    P.ts("dve", dst, src, sc_ap, None, ALU.mult, None, (RB[bi], R_f, R_c), (RhT,))
                else:
                    P.ts("dve", dst, src, sc_ap, sh_ap, ALU.mult, ALU.add, (RB[bi], R_f, R_c), (RhT,))

    zbig = AR.alloc([128, 512], BF16, "zbig")
    mL = AR.mark()
    LGT = AR.alloc([128, T], F32, "LGT")[0:NH]
    LGS = AR.alloc([128, 2 * (PAST + DT)], F32, "LGS")[0:NH]
    LGSv = LGS.rearrange("h (s n) -> h s n", s=2)
    lgtok = AR.alloc([128, NSUBO + 1, NH], F32, "lgtok")
    R_LGT, R_LGS, R_lgtok = Res("LGT"), Res("LGS"), Res("lgtok", stg=True)
    P.memset("dve", zbig, 0.0, (R_c,))
    mA = AR.mark()
    winb = AR.alloc([128, KD, IN], BF16, "winb")
    R_win = Res("win")
    winv = w_in.rearrange("(k p) c -> p k c", p=128)
    for k in range(KD):
        P.dma("pool", winb[:, k, :], winv[:, k, :], (R_in,), (R_win,))
    R_wsc = {"wo": Res("wos"), "wg": Res("wgs"), "wu": Res("wus"), "wd": Res("wds")}
    precast = []
    for (nm, dst_, src_) in (("wo", wos, w_o), ("wg", wgs, w_gate), ("wu", wus, w_up), ("wd", wds, w_down)):
        nr = src_.shape[0]
        for r0 in range(0, nr, 256):
            precast.append((dst_[r0:min(nr, r0 + 256), :], src_[r0:min(nr, r0 + 256), :], R_wsc[nm]))

    def issue_precast(n):
        for _ in range(n):
            if precast:
                d_, s_, r_ = precast.pop(0)
                P.dma("pool", d_, s_, (R_in,), (r_,))
    GA = 4
    xt2 = [AR.alloc([128, GA, D], F32, f"xt{i}") for i in range(2)]
    R_xt = [Res("xt0"), Res("xt1")]
    tmpsq = AR.alloc([128, D], F32, "tmpsq")
    ssA = AR.alloc([128, 8], F32, "ssA")
    xs_b2 = [AR.alloc([128, GA, D], BF16, f"xs_b{i}") for i in range(2)]
    hTA2 = [AR.alloc([128, KD, GA * 128], BF16, f"hTA{i}") for i in range(2)]
    R_tmpA2, R_hTA2 = [Res("tmpA0"), Res("tmpA1")], [Res("hTA0"), Res("hTA1")]
    cur = {"i": 0}

    class _H:
        def __getitem__(self, key):
            return hTA2[cur["i"]][key]
    hTA = _H()

    class _R:
        pass

    NST = 2
    fmo = [AR.alloc([128, 512], BF16, f"fmo{i}") for i in range(NST)]
    R_fmo = [Res(f"fmo{i}", stg=True) for i in range(NST)]
    tmo = [AR.alloc([128, 512], BF16, f"tmo{i}") for i in range(NST)]
    R_tmo = [Res(f"tmo{i}", stg=True) for i in range(NST)]
    tmf = [AR.alloc([128, 512], F32, f"tmf{i}") for i in range(NST)]
    R_tmf = [Res(f"tmf{i}", stg=True) for i in range(NST)]
    cnt = {"fm": 0, "tm": 0, "tf": 0, "x": 0}

    def fvec_mix(cnd, k):
        return fmix[:, 0, cnd, k:k + 1], fmix[:, 1, cnd, k:k + 1]

    def fm_chunk(N, col0, scale):
        bi = 2 + (cnt["fm"] % 2)
        bk = bank(bi)
        for k in range(KD):
            P.mm(bk[:, 0:N], winb[:, k, col0:col0 + 128], hTA[:, k, 0:N], k == 0, k == KD - 1,
                 (R_win, R_hTA2[cur["i"]]), (RB[bi],))
        i = cnt["fm"] % NST
        cnt["fm"] += 1
        P.act(fmo[i][:, 0:N], bk[:, 0:N], AF.Copy, (RB[bi],), (R_fmo[i],), scale=scale)
        return fmo[i], R_fmo[i]

    def tm_block(g, col0, ncols):
        bi = 4 + (cnt["tm"] % 2)
        cnt["tm"] += 1
        bk = bank(bi)
        for k in range(KD):
            P.mm(bk[:, 0:ncols], hTA[:, k, g * 128:(g + 1) * 128], winb[:, k, col0:col0 + ncols], k == 0,
                 k == KD - 1, (R_win, R_hTA2[cur["i"]]), (RB[bi],))
        return bk, bi

    def to_bf(bk, bi, ncols):
        i = cnt["tf"] % NST
        cnt["tf"] += 1
        P.cp("dve", tmo[i][:, 0:ncols], bk[:, 0:ncols], (RB[bi],), (R_tmo[i],))
        return tmo[i], R_tmo[i]

    def to_f32(bk, bi, ncols):
        i = cnt["tf"] % NST
        cnt["tf"] += 1
        P.act(tmf[i][:, 0:ncols], bk[:, 0:ncols], AF.Copy, (RB[bi],), (R_tmf[i],))
        return tmf[i], R_tmf[i]

    ssA2 = [ssA, AR.alloc([128, 8], F32, "ssA1")]
    tmpsq2 = [tmpsq, AR.alloc([128, D], BF16, "tmpsq1")]

    def front(xsrc, G, conds, i, part):
        if part == "pre":
            P.dma("pool", xt2[i][:, 0:G, :], xsrc.rearrange("(g p) d -> p g d", p=128), (R_in,), (R_xt[i],))
        norm_to_fm(xt2[i], G, D, tmpsq2[i], ssA2[i], xs_b2[i], hTA2[i], conds, fvec_mix, R_xt[i], R_tmpA2[i],
                   R_hTA2[i], KD, (0, 1), part=part)

    def logits_fm(N):
        bk = bank(6)
        for k in range(KD):
            P.mm(bk[0:NH, 0:N], winb[:, k, LF:LF + NH], hTA[:, k, 0:N], k == 0, k == KD - 1,
                 (R_win, R_hTA2[cur["i"]]), (RB[6],))
        return bk

    def logits_tm(g, slot):
        bk = bank(7)
        for k in range(KD):
            P.mm(bk[:, 0:NH], hTA[:, k, g * 128:(g + 1) * 128], winb[:, k, LF:LF + NH], k == 0, k == KD - 1,
                 (R_win, R_hTA2[cur["i"]]), (RB[7],))
        P.tt("dve", lgtok[:, slot, :], bk[:, 0:NH], bfrow, ALU.add, (RB[7], R_c), (R_lgtok,))

    CBLK = [(c0, min(512, W - c0)) for c0 in range(0, W, 512)]

    tiles = []

    def a1_p1(tt_):
        for s_, col0 in ((0, CK), (1, FK)):
            for c in range(WC):
                t_, r_ = fm_chunk(512, col0 + c * 128, 1.0)
                P.dma("sp", KTs[c, s_, :, tt_ * 512:(tt_ + 1) * 512], t_[:, 0:512], (r_,), (R_scr["KTs"],))

    def a1_p2(tt_):
        for g in range(GA):
            tok0 = tt_ * 512 + g * 128
            for s_, col0 in ((0, CV), (1, FV)):
                for (c0, nn) in CBLK:
                    bk, bi = tm_block(g, col0 + c0, nn)
                    t_, r_ = to_bf(bk, bi, nn)
                    P.dma("sp", Vs[s_, tok0:tok0 + 128, c0:c0 + nn], t_[:, 0:nn], (r_,), (R_scr["Vs"],))
        bk = logits_fm(512)
        P.cp("dve", LGT[:, tt_ * 512:(tt_ + 1) * 512], bk[0:NH, 0:512], (RB[6],), (R_LGT,))

    for tt_ in range(T // 512):
        tiles.append((xfull[tt_ * 512:(tt_ + 1) * 512, :], GA, [(0, 128, 0)],
                      (lambda tt_=tt_: a1_p1(tt_)), (lambda tt_=tt_: a1_p2(tt_))))

    def a2_p1(z):
        for s_, col0 in ((0, CQ), (1, FQ)):
            for c in range(WC):
                t_, r_ = fm_chunk(512, col0 + c * 128, 0.125)
                P.dma("sp", QTs[c, s_, :, z * 512:(z + 1) * 512], t_[:, 0:512], (r_,), (R_scr["QTs"],))

    def a2_p2(z):
        for g in range(GA):
            r0 = z * 512 + g * 128
            for (col0, od) in ((CK, o_sbk), (CV, o_sbv), (FK, o_fxk), (FV, o_fxv)):
                for (c0, nn) in CBLK:
                    bk, bi = tm_block(g, col0 + c0, nn)
                    t_, r_ = to_f32(bk, bi, nn)
                    P.dma("sp", od[r0:r0 + 128, c0:c0 + nn], t_[:, 0:nn], (r_,), (R_out,))
            logits_tm(g, z * 4 + g)

    for z in range(NZ):
        tiles.append((xown[z * 512:(z + 1) * 512, :], GA, [(0, 128, 0)],
                      (lambda z=z: a2_p1(z)), (lambda z=z: a2_p2(z))))

    def as_p1():
        for (s_, colq, colk) in ((0, CQ, CK), (1, FQ, FK)):
            for kind, col0, scale in (("q", colq, 0.125), ("k", colk, 1.0)):
                for c in range(WC):
                    t_, r_ = fm_chunk(128, col0 + c * 128, scale)
                    for half in range(2):
                        h = 2 * c + half
                        for sbi in range(2):
                            src = t_[half * 64:(half + 1) * 64, sbi * 64:(sbi + 1) * 64]
                            if kind == "q":
                                P.dma("sp", QTss[sbi, h, s_, 0:64, :], src, (r_,), (R_scr["QTss"],))
                            else:
                                P.dma("sp", KTss[sbi, h, s_, 0:64, 0:64], src, (r_,), (R_scr["KTss"],))

    def as_p2():
        for (col0, od) in ((CK, s_sbk), (CV, s_sbv), (FK, s_fxk), (FV, s_fxv)):
            for (c0, nn) in CBLK:
                bk, bi = tm_block(0, col0 + c0, nn)
                t_, r_ = to_f32(bk, bi, nn)
                P.dma("sp", od[0:128, c0:c0 + nn], t_[:, 0:nn], (r_,), (R_out,))
                if col0 in (CV, FV):
                    sidx = 0 if col0 == CV else 1
                    t2, r2 = to_bf(bk, bi, nn)
                    for sbi in range(2):
                        P.dma("sp", Vss[sbi, sidx, 0:64, c0:c0 + nn], t2[sbi * 64:(sbi + 1) * 64, 0:nn],
                              (r2,), (R_scr["Vss"],))
        logits_tm(0, NSUBO)
        bk = logits_fm(128)
        for sbi in range(2):
            P.cp("dve", LGSv[:, sbi, PAST:PAST + DT], bk[0:NH, sbi * 64:(sbi + 1) * 64], (RB[6],), (R_LGS,))

    tiles.append((xsam, 1, [(0, 64, 1), (64, 128, 2)], as_p1, as_p2))

    front(tiles[0][0], tiles[0][1], tiles[0][2], 0, "pre")
    front(tiles[0][0], tiles[0][1], tiles[0][2], 0, "tr")
    for ti, (xsrc_, G_, conds_, p1_, p2_) in enumerate(tiles):
        nx = tiles[ti + 1] if ti + 1 < len(tiles) else None
        if nx is not None:
            front(nx[0], nx[1], nx[2], (ti + 1) % 2, "pre")
        issue_precast(2)
        cur["i"] = ti % 2
        p1_()
        if nx is not None:
            front(nx[0], nx[1], nx[2], (ti + 1) % 2, "tr")
        cur["i"] = ti % 2
        p2_()
    issue_precast(1000)
    lgflat = lgtok.rearrange("p a h -> p (a h)")
    P.act(lgflat, lgflat, AF.Exp, (R_lgtok,), (R_lgtok,), scale=-1.0)
    P.act(lgflat, lgflat, AF.Ln, (R_lgtok,), (R_lgtok,), bias=1.0)
    P.ts("dve", lgflat, lgflat, -1.0, None, ALU.mult, None, (R_lgtok,), (R_lgtok,))
    with nc.allow_non_contiguous_dma(reason="small logf rows"):
        P.dma("sp", o_lf.rearrange("(a p) h -> p a h", p=128), lgtok[:, 0:NSUBO, :], (R_lgtok,), (R_out,))
        P.dma("sp", s_lf, lgtok[:, NSUBO, :], (R_lgtok,), (R_out,))

    P.barrier()
    AR.release(mA)

    mF = AR.mark()
    Gp = AR.alloc([128, T], F32, "Gp")[0:NH]
    Go = AR.alloc([128, TO], F32, "Go")[0:NH]
    Gs = AR.alloc([128, 2 * (PAST + DT)], F32, "Gs")[0:NH]
    Gsv = Gs.rearrange("h (s n) -> h s n", s=2)
    Gq = AR.alloc([128, 2 * DT], F32, "Gq")[0:NH]
    Gqv = Gq.rearrange("h (s n) -> h s n", s=2)
    pbuf = [AR.alloc([128, max(T, 2 * (PAST + DT))], BF16, f"pbuf{i}")[0:NH] for i in range(2)]
    R_G, R_Go, R_Gs, R_Gq = Res("G"), Res("Go"), Res("Gs"), Res("Gq")
    R_pb = [Res("pb0", stg=True), Res("pb1", stg=True)]
    pcount = [0]
    ktin = AR.alloc([128, PB, W], F32, "ktin")
    ktb = AR.alloc([128, PAST], BF16, "ktb")
    R_ktin, R_ktb = Res("ktin"), Res("ktb", stg=True)

    def a2c_piece(sbi, s, ck, cv):
        P.dma("pool", Vss[sbi, s, 128:128 + PAST, :], cv[sbi], (R_in,), (R_scr["Vss"],))
        P.dma("sp", Vss[sbi, s, 64:128, :], zbig[0:64, 0:W], (R_c,), (R_scr["Vss"],))
        P.dma("sp", ktin, ck[sbi].rearrange("(kb p) w -> p kb w", p=128), (R_in,), (R_ktin,))
        for c in range(WC):
            for kb0 in range(0, PB, 4):
                nk = min(4, PB - kb0)
                bi = 2 + (cnt["fm"] % 2)
                cnt["fm"] += 1
                bk = bank(bi)
                for j in range(nk):
                    P.tr(bk[:, j * 128:(j + 1) * 128], ktin[:, kb0 + j, c * 128:(c + 1) * 128], identf,
                         (R_ktin, R_c), (RB[bi],))
                P.act(ktb[:, kb0 * 128:(kb0 + nk) * 128], bk[:, 0:nk * 128], AF.Copy, (RB[bi],), (R_ktb,))
            for half in range(2):
                P.dma("sp", KTss[sbi, 2 * c + half, s, 0:64, 128:128 + PAST], ktb[half * 64:(half + 1) * 64, :],
                      (R_ktb,), (R_scr["KTss"],))
        P.dma("sp", KTss[sbi, :, s, :, 64:128].rearrange("h r c -> r h c"),
              zbig[0:70, 0:NH * 64].rearrange("r (h c) -> r h c", h=NH), (R_c,), (R_scr["KTss"],))

    a2c_jobs = [(sbi, s, ck, cv) for sbi in range(2) for (s, ck, cv) in ((0, csk, csv), (1, cfk, cfv))]

    def cumsum(dst, src, n, rsrc, rdst):
        for c0 in range(0, n, 512):
            nn = min(512, n - c0)
            init = 0.0 if c0 == 0 else dst[:, c0 - 1:c0]
            P.op("dve", lambda e, c0=c0, nn=nn, init=init: e.tensor_tensor_scan(
                out=dst[:, c0:c0 + nn], data0=onesb[0:NH, 0:nn], data1=src[:, c0:c0 + nn], initial=init,
                op0=ALU.mult, op1=ALU.add), (rsrc, rdst, R_c), (rdst,))

    def split_rows(src, n, rsrc, dst_fn):
        for j in range(3):
            i = pcount[0] % 2
            pcount[0] += 1
            pb = pbuf[i]
            P.cp("dve", pb[:, 0:n], src, (rsrc,), (R_pb[i],))
            if j < 2:
                P.tt("dve", src, src, pb[:, 0:n], ALU.subtract, (rsrc, R_pb[i]), (rsrc,))
            for (d_, s_, rd_) in dst_fn(j, pb):
                P.dma("sp", d_, s_, (R_pb[i],), (rd_,))

    def ones_rows(n, dst_list):
        i = pcount[0] % 2
        pcount[0] += 1
        P.memset("dve", pbuf[i][:, 0:n], 1.0, (R_pb[i],))
        for (d_, rd_) in dst_list:
            P.dma("sp", d_, pbuf[i][:, 0:d_.shape[-1]], (R_pb[i],), (rd_,))

    a2c_piece(*a2c_jobs[0])
    P.act(LGT, LGT, AF.Exp, (R_LGT, R_c), (R_LGT,), scale=-1.0, bias=nbft[0:NH, :])
    P.act(LGT, LGT, AF.Ln, (R_LGT,), (R_LGT,), bias=1.0)
    cumsum(Gp, LGT, T, R_LGT, R_G)
    Gv = Gp.rearrange("h (m r t) -> h m r t", r=4, t=128)
    Gov = Go.rearrange("h (m t) -> h m t", t=128)
    P.ts("dve", Gov, Gv[:, :, 0, :], selt[0:NH, 0:1], None, ALU.mult, None, (R_G, R_c), (R_Go,))
    for rr in range(1, 4):
        P.stt("dve", Gov, Gv[:, :, rr, :], selt[0:NH, rr:rr + 1], Gov, ALU.mult, ALU.add, (R_G, R_c, R_Go), (R_Go,))
    P.ts("dve", Go, Go, -1.0, None, ALU.mult, None, (R_Go,), (R_Go,))
    a2c_piece(*a2c_jobs[1])
    split_rows(Gp, T, R_G, lambda j, pb: [(KAs[:, 3 + j, :], pb[:, 0:T], R_scr["KAs"])])
    split_rows(Go, TO, R_Go, lambda j, pb: [(QAs[:, j, :], pb[:, 0:TO], R_scr["QAs"])])
    ones_rows(T, [(KAs[:, j, :], R_scr["KAs"]) for j in range(3)] +
              [(QAs[:, 3 + j, :], R_scr["QAs"]) for j in range(3)])
    a2c_piece(*a2c_jobs[2])
    P.dma("sp", LGSv[:, :, 0:PAST], clfT.rearrange("h (s n) -> h s n", s=2), (R_in,), (R_LGS,))
    P.ts("dve", LGSv[:, :, 0:PAST], LGSv[:, :, 0:PAST], -1.0, None, ALU.mult, None, (R_LGS,), (R_LGS,))
    P.act(LGSv[:, :, PAST:PAST + DT], LGSv[:, :, PAST:PAST + DT], AF.Exp, (R_LGS, R_c), (R_LGS,), scale=-1.0,
          bias=nbft[0:NH, :])
    P.act(LGSv[:, :, PAST:PAST + DT], LGSv[:, :, PAST:PAST + DT], AF.Ln, (R_LGS,), (R_LGS,), bias=1.0)
    for sbi in range(2):
        cumsum(Gsv[:, sbi, :], LGSv[:, sbi, :], PAST + DT, R_LGS, R_Gs)
    P.ts("dve", Gqv, Gsv[:, :, PAST:PAST + DT], -1.0, None, ALU.mult, None, (R_Gs,), (R_Gq,))

    def kdst_s(j, pb):
        pv = pb[:, 0:2 * (PAST + DT)].rearrange("h (s n) -> h s n", s=2)
        out = []
        for sbi in range(2):
            out.append((KTss[sbi, :, 1, 67 + j, 0:DT], pv[:, sbi, PAST:PAST + DT], R_scr["KTss"]))
            out.append((KTss[sbi, :, 1, 67 + j, 128:128 + PAST], pv[:, sbi, 0:PAST], R_scr["KTss"]))
        return out

    def qdst_s(j, pb):
        pv = pb[:, 0:2 * DT].rearrange("h (s n) -> h s n", s=2)
        return [(QTss[sbi, :, 1, 64 + j, :], pv[:, sbi, :], R_scr["QTss"]) for sbi in range(2)]

    a2c_piece(*a2c_jobs[3])
    split_rows(Gs, 2 * (PAST + DT), R_Gs, kdst_s)
    split_rows(Gq, 2 * DT, R_Gq, qdst_s)
    ones_rows(SK, [(KTss[sbi, :, 1, 64 + j, :], R_scr["KTss"]) for sbi in range(2) for j in range(3)] +
              [(QTss[sbi, :, 1, 67 + j, :], R_scr["QTss"]) for sbi in range(2) for j in range(3)])
    P.barrier()
    AR.release(mL)

    mB = AR.mark()
    maskb = AR.alloc([128, 2, 16, 512], BF16, "maskb")
    smaskb = AR.alloc([128, 2, 64], BF16, "smaskb")
    R_mask = Res("mask")
    mv = masks.rearrange("p (s k q) -> p s k q", s=2, k=16)
    for s in range(2):
        for k4 in range(0, 16, 4):
            P.dma("pool", maskb[:, s, k4:k4 + 4, :], mv[:, s, k4:k4 + 4, :], (R_in,), (R_mask,))
    P.dma("pool", smaskb, smask.rearrange("p (s q) -> p s q", s=2), (R_in,), (R_mask,))
    NKB = T // 128
    KTt = [[AR.alloc([128, T], BF16, f"KTt{s}{i}") for i in range(2)] for s in range(2)]
    Vt = [[AR.alloc([128, NKB, 128], BF16, f"Vt{s}{i}") for i in range(2)] for s in range(2)]
    QTt = [[AR.alloc([128, TO], BF16, f"QTt{s}{i}") for i in range(2)] for s in range(2)]
    R_KT = [[Res(f"KT{s}{i}") for i in range(2)] for s in range(2)]
    R_V = [[Res(f"V{s}{i}") for i in range(2)] for s in range(2)]
    R_QT = [[Res(f"QT{s}{i}") for i in range(2)] for s in range(2)]
    for s in range(2):
        for i in range(2):
            P.memset("pool", Vt[s][i], 0.0, (R_V[s][i],))
            P.memset("pool", KTt[s][i], 0.0, (R_KT[s][i],))
            P.memset("pool", QTt[s][i], 0.0, (R_QT[s][i],))
        for i in range(2):
            if s == 1:
                P.memset("dve", Vt[s][i][:, :, 64:65], 1.0, (R_V[s][i],))
    SPt = [AR.alloc([128, 512], BF16, f"SPt{i}") for i in range(2)]
    Gt = [AR.alloc([128, 512], F32, f"Gt{i}") for i in range(2)]
    at = [AR.alloc([128, 512], BF16, f"at{i}") for i in range(2)]
    Pt = [AR.alloc([128, 512], BF16, f"Pt{i}") for i in range(2)]
    R_E = [Res("E0"), Res("E1")]; R_SP = [Res("SP0"), Res("SP1")]; R_G2 = [Res("G0"), Res("G1")]
    R_a = [Res("a0"), Res("a1")]; R_P = [Res("P0"), Res("P1")]
    osb_s2 = [AR.alloc([128, 512], F32, f"osb_s{i}") for i in range(2)]
    ofx_s2 = [AR.alloc([128, 512], F32, f"ofx_s{i}") for i in range(2)]
    R_os2 = [Res("os0"), Res("os1")]
    pending = []
    fin_cnt = [0]
    otok = AR.alloc([128, 4, 2, 64], F32, "otok")
    rec = AR.alloc([128, 4], F32, "rec")
    R_os, R_otok = Res("os"), Res("otok", stg=True)
    BZ, BS, BA, BOS, BOF, BTP = (0, 1, 7), (2, 3), 4, 5, 6, 0
    itc = [0]

    def sweep(KT, QT, Vtile, rK, rQ, rV, q0, Nq, blocks, odst_fn):
        A = bank(BA); Osb = bank(BOS); Ofx = bank(BOF)
        for (bk_, M, rb) in ((A, 128, RB[BA]), (Osb, 128, RB[BOS]), (Ofx, 128, RB[BOF])):
            P.mm(bk_[0:M, 0:Nq], zrow[0:1, 0:M], onesb[0:1, 0:Nq], True, True, (R_c,), (rb,))
        nb = len(blocks)

        def mm1(i):
            kcol, vkb, qlo, msb, mfx = blocks[i]
            b = (itc[0] + i) % 2
            b3 = (itc[0] + i) % 3
            Z = bank(BZ[b3]); S = bank(BS[b])
            P.mm(Z[:, qlo:Nq], KT[0][:, kcol:kcol + 128], QT[0][:, q0 + qlo:q0 + Nq], True, msb is None,
                 (rK[0], rQ[0]), (RB[BZ[b3]],))
            if msb is not None:
                P.mm(Z[:, qlo:Nq], identb, msb, False, True, (R_c, R_mask), (RB[BZ[b3]],))
            P.mm(S[:, qlo:Nq], KT[1][:, kcol:kcol + 128], QT[1][:, q0 + qlo:q0 + Nq], True, mfx is None,
                 (rK[1], rQ[1]), (RB[BS[b]],))
            if mfx is not None:
                P.mm(S[:, qlo:Nq], identb, mfx, False, True, (R_c, R_mask), (RB[BS[b]],))

        def actE(i):
            kcol, vkb, qlo, msb, mfx = blocks[i]
            b3 = (itc[0] + i) % 3
            Z = bank(BZ[b3])
            P.act(Z[:, qlo:Nq], Z[:, qlo:Nq], AF.Exp, (RB[BZ[b3]],), (RB[BZ[b3]],))

        for f_ in pending:
            f_()
        del pending[:]
        mm1(0)
        actE(0)
        for i in range(nb):
            kcol, vkb, qlo, msb, mfx = blocks[i]
            b = (itc[0] + i) % 2
            b3 = (itc[0] + i) % 3
            Z = bank(BZ[b3]); S = bank(BS[b])
            sl = slice(qlo, Nq)
            if i + 1 < nb:
                mm1(i + 1)
            P.act(SPt[b][:, sl], Z[:, sl], AF.Ln, (RB[BZ[b3]],), (R_SP[b],), bias=1.0)
            P.mm(A[:, sl], triIb, SPt[b][:, sl], False, True, (R_c, R_SP[b]), (RB[BA],))
            P.act(Pt[b][:, sl], S[:, sl], AF.Exp, (RB[BS[b]],), (R_P[b],))
            P.mm(Ofx[:, sl], Vtile[1][:, vkb, :], Pt[b][:, sl], False, True, (rV[1], R_P[b]), (RB[BOF],))
            if i + 1 < nb:
                actE(i + 1)
            if i > 0:
                pk, pv, pq, _, _ = blocks[i - 1]
                pb_ = (itc[0] + i - 1) % 2
                P.mm(Osb[:, pq:Nq], Vtile[0][:, pv, :], at[pb_][:, pq:Nq], False, True,
                     (rV[0], R_a[pb_]), (RB[BOS],))
            P.act(Gt[b][:, sl], A[:, sl], AF.Exp, (RB[BA],), (R_G2[b],))
            P.mm(A[:, sl], compb, SPt[b][:, sl], False, True, (R_c, R_SP[b]), (RB[BA],))
            P.tt("dve", at[b][:, sl], Z[:, sl], Gt[b][:, sl], ALU.mult, (RB[BZ[b3]], R_G2[b]), (R_a[b],))
        pk, pv, pq, _, _ = blocks[nb - 1]
        pb_ = (itc[0] + nb - 1) % 2
        P.mm(Osb[:, pq:Nq], Vtile[0][:, pv, :], at[pb_][:, pq:Nq], False, True, (rV[0], R_a[pb_]), (RB[BOS],))
        itc[0] += nb
        j = fin_cnt[0] % 2
        fin_cnt[0] += 1
        P.cp("dve", osb_s2[j][0:64, 0:Nq], Osb[0:64, 0:Nq], (RB[BOS],), (R_os2[j],))
        P.cp("dve", ofx_s2[j][0:65, 0:Nq], Ofx[0:65, 0:Nq], (RB[BOF],), (R_os2[j],))

        def fin(j=j, Nq=Nq, odst_fn=odst_fn):
            osb_s, ofx_s, R_os = osb_s2[j], ofx_s2[j], R_os2[j]
            nt = min(128, Nq)
            ng = max(1, Nq // 128)
            TP = bank(BTP)
            for i in range(ng):
                P.tr(TP[0:nt, i * 64:(i + 1) * 64], osb_s[0:64, i * 128:i * 128 + nt], identf[0:64, 0:64],
                     (R_os, R_c), (RB[BTP],))
            P.cp("dve", otok[0:nt, 0:ng, 0, :], TP[0:nt, 0:ng * 64].rearrange("p (g c) -> p g c", c=64),
                 (RB[BTP],), (R_otok,))
            for i in range(ng):
                P.tr(TP[0:nt, i * 65:(i + 1) * 65], ofx_s[0:65, i * 128:i * 128 + nt], identf[0:65, 0:65],
                     (R_os, R_c), (RB[BTP],))
            TFv = TP[0:nt, 0:ng * 65].rearrange("p (g c) -> p g c", c=65)
            P.op("dve", lambda e: e.reciprocal(out=rec[0:nt, 0:ng], in_=TFv[:, :, 64]), (RB[BTP],), (R_otok,))
            for i in range(ng):
                P.ts("dve", otok[0:nt, i, 1, :], TFv[:, i, 0:64], rec[0:nt, i:i + 1], None, ALU.mult, None,
                     (RB[BTP], R_otok), (R_otok,))
            for (d_, s_, rd_) in odst_fn(otok, nt, ng):
                P.dma("sp", d_, s_, (R_otok,), (rd_,))

        pending.append(fin)

    hp_count = [0]

    def load_pair(ktsrc, vsrc, qsrc, nkeys, nkb, nq, rk_scr, rv_scr, rq_scr):
        i = hp_count[0] % 2
        hp_count[0] += 1
        for s in range(2):
            for (r0, r1, src) in ktsrc(s):
                P.dma("sp", KTt[s][i][r0:r1, 0:nkeys], src, rk_scr, (R_KT[s][i],))
            for (r0, r1, src) in qsrc(s):
                P.dma("sp", QTt[s][i][r0:r1, 0:nq], src, rq_scr, (R_QT[s][i],))
            P.dma("sp", Vt[s][i][:, 0:nkb, 0:64], vsrc(s).rearrange("(kb k) d -> k kb d", k=128),
                  (rv_scr,), (R_V[s][i],))
        return ([KTt[0][i], KTt[1][i]], [QTt[0][i], QTt[1][i]], [Vt[0][i], Vt[1][i]],
                [R_KT[0][i], R_KT[1][i]], [R_QT[0][i], R_QT[1][i]], [R_V[0][i], R_V[1][i]])

    with nc.allow_non_contiguous_dma(reason="head-sliced V rows / O columns (128-256B segments)"):
        jobs = []
        for p in range(NH):
            def ld(p=p):
                hs = slice((p % 2) * 64, (p % 2) * 64 + 64)
                return load_pair(
                    lambda s: [(0, 64, KTs[p // 2, s, hs, :])] + ([(64, 70, KAs[p])] if s == 1 else []),
                    lambda s: Vs[s, :, p * 64:(p + 1) * 64],
                    lambda s: [(0, 64, QTs[p // 2, s, hs, :])] + ([(64, 70, QAs[p])] if s == 1 else []),
                    T, NKB, TO, (R_scr["KTs"], R_scr["KAs"]), R_scr["Vs"], (R_scr["QTs"], R_scr["QAs"]))

            def sw(ld_, p=p):
                KT, QT, Vl, rK, rQ, rV = ld_
                for z in range(NZ):
                    blocks = []
                    for kbz in range(15, -1, -1):
                        qlo = (kbz // 4) * 128
                        kb = 16 * z + kbz
                        blocks.append((kb * 128, kb, qlo, maskb[:, 0, kbz, qlo:512], maskb[:, 1, kbz, qlo:512]))
                    for kb in range(16 * z - 1, -1, -1):
                        blocks.append((kb * 128, kb, 0, None, None))

                    def odst(ot, nt, ng, z=z, p=p):
                        rows = Os[z * 512:(z + 1) * 512, :].rearrange("(g t) c -> t g c", t=128)
                        return [(rows[:, :, p * 64:(p + 1) * 64], ot[:, :, 0, :], R_scr["Os"]),
                                (rows[:, :, W + p * 64:W + (p + 1) * 64], ot[:, :, 1, :], R_scr["Os"])]

                    sweep(KT, QT, Vl, rK, rQ, rV, z * 512, 512, blocks, odst)
            jobs.append((ld, sw))
        for sbi in range(2):
            for p in range(NH):
                def ld(sbi=sbi, p=p):
                    return load_pair(
                        lambda s: [(0, 64 if s == 0 else 70, KTss[sbi, p, s, 0:(64 if s == 0 else 70), :])],
                        lambda s: Vss[sbi, s, :, p * 64:(p + 1) * 64],
                        lambda s: [(0, 64 if s == 0 else 70, QTss[sbi, p, s, 0:(64 if s == 0 else 70), :])],
                        SK, SKB, NQS, (R_scr["KTss"],), R_scr["Vss"], (R_scr["QTss"],))

                def sw(ld_, sbi=sbi, p=p):
                    KT, QT, Vl, rK, rQ, rV = ld_
                    blocks = [(0, 0, 0, smaskb[:, 0, :], smaskb[:, 1, :])]
                    for kb in range(PB - 1, -1, -1):
                        blocks.append((128 + kb * 128, 1 + kb, 0, None, None))

                    def odst_s(ot, nt, ng, sbi=sbi, p=p):
                        return [(Oss[sbi * 64:(sbi + 1) * 64, p * 64:(p + 1) * 64], ot[0:64, 0, 0, :], R_scr["Oss"]),
                                (Oss[sbi * 64:(sbi + 1) * 64, W + p * 64:W + (p + 1) * 64], ot[0:64, 0, 1, :],
                                 R_scr["Oss"])]

                    sweep(KT, QT, Vl, rK, rQ, rV, 0, NQS, blocks, odst_s)
                jobs.append((ld, sw))
        nxt = jobs[0][0]()
        for ji, (ld, sw) in enumerate(jobs):
            cur_ld = nxt
            if ji + 1 < len(jobs):
                nxt = jobs[ji + 1][0]()
            sw(cur_ld)
    for f_ in pending:
        f_()
    del pending[:]
    P.barrier()
    AR.release(mB)

    GC = 2
    wob = AR.alloc([128, MC, D], BF16, "wob")
    wgb = AR.alloc([128, KD, DFF], BF16, "wgb")
    wub = AR.alloc([128, KD, DFF], BF16, "wub")
    wdb = AR.alloc([128, FFC, D], BF16, "wdb")
    R_wo, R_wg, R_wu, R_wd = Res("wo"), Res("wg"), Res("wu"), Res("wd")
    for (dst, src, nk, rw_, rs_) in ((wob, wos, MC, R_wo, R_wsc["wo"]), (wgb, wgs, KD, R_wg, R_wsc["wg"]),
                                     (wub, wus, KD, R_wu, R_wsc["wu"]), (wdb, wds, FFC, R_wd, R_wsc["wd"])):
        sv = src.rearrange("(k p) c -> p k c", p=128)
        for k0 in range(0, nk, 4):
            k1 = min(nk, k0 + 4)
            P.dma("sp", dst[:, k0:k1, :], sv[:, k0:k1, :], (rs_,), (rw_,))
    gp2 = AR.alloc([128, D], F32, "gp2"); gp5 = AR.alloc([128, D], F32, "gp5")
    gfin = AR.alloc([128, D], F32, "gfin")
    R_g = Res("gates")
    P.dma("sp", gp2, adas[0, 2, :].partition_broadcast(128), (R_scr["adas"],), (R_g,))
    P.dma("sp", gp5, adas[0, 5, :].partition_broadcast(128), (R_scr["adas"],), (R_g,))
    P.dma("sp", gfin, g_final.partition_broadcast(128), (R_in,), (R_g,))
    xc = AR.alloc([128, GC, D], F32, "xc")
    ocy = AR.alloc([128, GC, max(D, MIX)], F32, "ocy")
    oc = ocy[:, :, 0:MIX]
    yc = ocy[:, :, 0:D]
    nb_ = AR.alloc([128, GC, max(D, MIX)], BF16, "nb_")
    fT = AR.alloc([128, max(KD, MC), GC * 128], BF16, "fT")
    actT = AR.alloc([128, FFC, GC * 128], BF16, "actT")
    tsq = AR.alloc([128, max(D, MIX)], BF16, "tsq")
    ssC = AR.alloc([128, 8], F32, "ssC")
    tmpf = AR.alloc([128, 512], F32, "tmpf")
    sg = AR.alloc([128, GC * 128], F32, "sg")
    R_xc, R_oc, R_nb, R_fT, R_actT, R_tmpf, R_sg = (Res("xc"), Res("oc", stg=True), Res("nb"), Res("fT"),
                                                   Res("actT"), Res("tmpf"), Res("sg"))
    R_yc = R_oc

    def fvec_ffn(cnd, k):
        return fffn[:, 0, cnd, k:k + 1], fffn[:, 1, cnd, k:k + 1]

    def post_group(xsrc, osrc, ydst, G, conds, g2, g5):
        N = G * 128
        P.dma("sp", xc[:, 0:G, :], xsrc.rearrange("(g p) d -> p g d", p=128), (R_in,), (R_xc,))
        P.dma("sp", oc[:, 0:G, :], osrc.rearrange("(g p) d -> p g d", p=128), (R_scr["Os"], R_scr["Oss"]), (R_oc,))
        for s in range(2):
            norm_to_fm(oc[:, :, s * W:(s + 1) * W], G, W, tsq, ssC, nb_[:, :, 0:W],
                       fT[:, s * WC:(s + 1) * WC, :], [(0, 128, 0)],
                       lambda cnd, k, s=s: (goTt[:, s * WC + k:s * WC + k + 1], None),
                       R_oc, R_nb, R_fT, WC, (0, 1))
        for g in range(G):
            for nb2 in range(D // 512):
                cs = slice(nb2 * 512, (nb2 + 1) * 512)
                bi = 2 + ((g * (D // 512) + nb2) % 2)
                bk = bank(bi)
                for c in range(MC):
                    P.mm(bk, fT[:, c, g * 128:(g + 1) * 128], wob[:, c, cs], c == 0, c == MC - 1, (R_fT, R_wo), (RB[bi],))
                P.tt("dve", tmpf, bk, g2[:, cs], ALU.mult, (RB[bi], R_g), (R_tmpf,))
                P.tt("dve", xc[:, g, cs], xc[:, g, cs], tmpf, ALU.add, (R_xc, R_tmpf), (R_xc,))
        norm_to_fm(xc, G, D, tsq, ssC, nb_[:, :, 0:D], fT[:, 0:KD, :], conds, fvec_ffn, R_xc, R_nb, R_fT, KD, (0, 1))
        for fc in range(FFC):
            bg, bu = 4 + (fc % 2), 6 + (fc % 2)
            for k in range(KD):
                P.mm(bank(bg)[:, 0:N], wgb[:, k, fc * 128:(fc + 1) * 128], fT[:, k, 0:N], k == 0, k == KD - 1,
                     (R_wg, R_fT), (RB[bg],))
            for k in range(KD):
                P.mm(bank(bu)[:, 0:N], wub[:, k, fc * 128:(fc + 1) * 128], fT[:, k, 0:N], k == 0, k == KD - 1,
                     (R_wu, R_fT), (RB[bu],))
            P.act(sg[:, 0:N], bank(bg)[:, 0:N], AF.Silu, (RB[bg],), (R_sg,))
            P.tt("dve", actT[:, fc, 0:N], sg[:, 0:N], bank(bu)[:, 0:N], ALU.mult, (R_sg, RB[bu]), (R_actT,))
        for g in range(G):
            for nb2 in range(D // 512):
                cs = slice(nb2 * 512, (nb2 + 1) * 512)
                bi = 2 + ((g * (D // 512) + nb2) % 2)
                bk = bank(bi)
                for fc in range(FFC):
                    P.mm(bk, actT[:, fc, g * 128:(g + 1) * 128], wdb[:, fc, cs], fc == 0, fc == FFC - 1,
                         (R_actT, R_wd), (RB[bi],))
                P.tt("dve", tmpf, bk, g5[:, cs], ALU.mult, (RB[bi], R_g), (R_tmpf,))
                P.tt("dve", xc[:, g, cs], xc[:, g, cs], tmpf, ALU.add, (R_xc, R_tmpf), (R_xc,))
        for g in range(G):
            P.memset("dve", ssC[:, g:g + 1], 0.0, (R_nb,))
            P.act(tsq[:, 0:D], xc[:, g, :], AF.Square, (R_xc, R_nb), (R_nb,), accum_out=ssC[:, g:g + 1])
        P.ts("dve", ssC[:, 0:G], ssC[:, 0:G], 1.0 / D, EPS, ALU.mult, ALU.add, (R_nb,), (R_nb,))
        P.act(ssC[:, 0:G], ssC[:, 0:G], AF.Ln, (R_nb,), (R_nb,))
        P.act(ssC[:, 0:G], ssC[:, 0:G], AF.Exp, (R_nb,), (R_nb,), scale=-0.5)
        for g in range(G):
            P.stt("dve", yc[:, g, :], xc[:, g, :], ssC[:, g:g + 1], gfin, ALU.mult, ALU.mult, (R_xc, R_nb, R_g), (R_yc,))
        P.dma("sp", ydst.rearrange("(g p) d -> p g d", p=128), yc[:, 0:G, :], (R_yc,), (R_out,))

    for gi in range(TO // (GC * 128)):
        rs = slice(gi * GC * 128, (gi + 1) * GC * 128)
        post_group(xown[rs, :], Os[rs, :], y_own[rs, :], GC, [(0, 128, 0)], gp2, gp5)
    for sbi in range(2):
        P.dma("sp", gp2[sbi * 64:(sbi + 1) * 64, :], adas[1 + sbi, 2, :].partition_broadcast(64),
              (R_scr["adas"],), (R_g,))
        P.dma("sp", gp5[sbi * 64:(sbi + 1) * 64, :], adas[1 + sbi, 5, :].partition_broadcast(64),
              (R_scr["adas"],), (R_g,))
    post_group(xsam, Oss, y_sam, 1, [(0, 64, 1), (64, 128, 2)], gp2, gp5)
    P.barrier()

    print("arena high-water", AR.hw, "of", ARENA_BYTES, "ops", {e: len(P.q[e]) for e in ENGS}, "sems", P.nsem, flush=True)
    blk = es.enter_context(nc.Block())
    P.emit(blk)
    es.close()
    return nc


def make_consts():
    c = np.zeros((128, 640), np.float32)
    s = np.arange(128)[:, None]
    j = np.arange(128)[None, :]
    c[:, 0:128] = (s == j)
    c[:, 128:256] = -(s >= j).astype(np.float32)
    c[:, 256:384] = -(s < j).astype(np.float32)
    return c


def make_masks(r):
    m = np.zeros((128, 2, 16, 512), np.float32)
    j = np.arange(128)[:, None]
    for kbz in range(16):
        for i in range(4):
            key = kbz * 128 + j
            q = (4 * i + r) * 128 + np.arange(128)[None, :]
            m[:, 0, kbz, i * 128:(i + 1) * 128] = np.where(key < q, 0.0, NEG)
            m[:, 1, kbz, i * 128:(i + 1) * 128] = np.where(key <= q, 0.0, NEG)
    sm = np.zeros((128, 2, 64), np.float32)
    q = np.arange(64)[None, :]
    sm[:, 0, :] = np.where((j < q) & (j < 64), 0.0, NEG)
    sm[:, 1, :] = np.where((j <= q) & (j < 64), 0.0, NEG)
    return m.reshape(128, -1), sm.reshape(128, -1)


_NC_CACHE = {}


def run(cfg, inputs):
    D, NH, T, DFF, PAST, DT = cfg["D"], cfg["NH"], cfg["T"], cfg["DFF"], cfg["PAST"], cfg["DT"]
    W = NH * 64
    KD = D // 128
    key = tuple(sorted(cfg.items()))
    nc = build(cfg)
    f = lambda a: np.ascontiguousarray(np.asarray(a, dtype=np.float32))
    xp, xs = f(inputs["x_prompt"]), f(inputs["x_sample"])
    cp, cs = f(inputs["c_prompt"]), f(inputs["c_sample"])
    NSUB = T // 128
    in_maps = []
    cst = make_consts()
    for c in range(8):
        b, r = c // 4, c % 4
        own = np.arange(r, NSUB, 4)
        xo = xp[b].reshape(NSUB, 128, D)[own].reshape(-1, D)
        crows = np.stack([cp[b], cs[2 * c], cs[2 * c + 1]], axis=1)
        cTm = crows.reshape(KD, 128, 3).transpose(1, 0, 2).reshape(128, KD * 3)
        mk, smk = make_masks(r)
        selv = np.zeros((128, 4), np.float32); selv[:, r] = 1.0
        go = np.concatenate([f(inputs["g_sb_out"])[0], f(inputs["g_fox_out"])[0]])
        goT = go.reshape(-1, 128).T
        m = {
            "xfull": xp[b], "xown": xo, "xsam": xs[2 * c:2 * c + 2].reshape(128, D), "cT": cTm,
            "csk": f(inputs["cache_sb_k"])[0, 2 * c:2 * c + 2].reshape(2, PAST, W),
            "csv": f(inputs["cache_sb_v"])[0, 2 * c:2 * c + 2].reshape(2, PAST, W),
            "cfk": f(inputs["cache_fox_k"])[0, 2 * c:2 * c + 2].reshape(2, PAST, W),
            "cfv": f(inputs["cache_fox_v"])[0, 2 * c:2 * c + 2].reshape(2, PAST, W),
            "clfT": f(inputs["cache_fox_logf"])[0, 2 * c:2 * c + 2].transpose(2, 0, 1).reshape(NH, 2 * PAST),
            "w_ada": f(inputs["w_ada"])[0], "b_ada": f(inputs["b_ada"])[0], "w_in": f(inputs["w_in"])[0],
            "w_o": f(inputs["w_o"])[0], "w_gate": f(inputs["w_gate"])[0], "w_up": f(inputs["w_up"])[0],
            "w_down": f(inputs["w_down"])[0], "g_mix": f(inputs["g_mix"])[0], "g_ffn": f(inputs["g_ffn"])[0],
            "g_final": f(inputs["g_final"]), "goT": goT, "b_f": f(inputs["b_f"])[0],
            "cst": cst, "masks": mk, "smask": smk, "sel": selv,
        }
        in_maps.append({k: np.ascontiguousarray(v, dtype=np.float32) for k, v in m.items()})
    if cfg.get("_prep_only"):
        return nc, in_maps
    res = run_bass_kernel_spmd(nc, in_maps, core_ids=list(range(8)))
    R = res.results
    yp = np.zeros((2, T, D), np.float32)
    outs_p = {k: np.zeros((1, 2, T, NH, 64), np.float32) for k in ("o_sbk", "o_sbv", "o_fxk", "o_fxv")}
    lfp = np.zeros((1, 2, T, NH), np.float32)
    ysm = np.zeros((16, DT, D), np.float32)
    outs_s = {k: np.zeros((1, 16, DT, NH, 64), np.float32) for k in ("s_sbk", "s_sbv", "s_fxk", "s_fxv")}
    lfs = np.zeros((1, 16, DT, NH), np.float32)
    for c in range(8):
        b, r = c // 4, c % 4
        own = np.arange(r, NSUB, 4)
        yp[b].reshape(NSUB, 128, D)[own] = R[c]["y_own"].reshape(-1, 128, D)
        for k in outs_p:
            outs_p[k][0, b].reshape(NSUB, 128, NH, 64)[own] = R[c][k].reshape(-1, 128, NH, 64)
        lfp[0, b].reshape(NSUB, 128, NH)[own] = R[c]["o_lf"].reshape(-1, 128, NH)
        ysm[2 * c:2 * c + 2] = R[c]["y_sam"].reshape(2, DT, D)
        for k in outs_s:
            outs_s[k][0, 2 * c:2 * c + 2] = R[c][k].reshape(2, DT, NH, 64)
        lfs[0, 2 * c:2 * c + 2] = R[c]["s_lf"].reshape(2, DT, NH)
    return (yp, ysm, outs_p["o_sbk"], outs_p["o_sbv"], outs_p["o_fxk"], outs_p["o_fxv"], lfp,
            outs_s["s_sbk"], outs_s["s_sbv"], outs_s["s_fxk"], outs_s["s_fxv"], lfs)


def kernel(**inputs):
    return run(CFG_FULL, inputs)
```

```python
import numpy as np
import ml_dtypes
from contextlib import ExitStack
import concourse.bass as bass
import concourse.mybir as mybir
from concourse.bass_utils import run_bass_kernel_spmd

F32 = mybir.dt.float32
BF16 = mybir.dt.bfloat16
U8 = mybir.dt.uint8
AF = mybir.ActivationFunctionType
ALU = mybir.AluOpType
EPS = 1e-6
NEG = -30000.0
ENGS = ("pe", "act", "dve", "pool", "sp")
MAXV = 30000

CFG_FULL = dict(D=1024, NH=8, T=8192, DFF=2816, PAST=1024, DT=64)


class Res:
    __slots__ = ("name", "w", "rs", "cw", "stg")

    def __init__(self, name, cw=False, stg=False):
        self.name = name
        self.w = {}
        self.rs = {}
        self.cw = cw
        self.stg = stg


DMA_SLOTS = {"sp": 4, "pool": 2}


class Prog:
    def __init__(self, nc, es):
        self.nc, self.es = nc, es
        self.q = {e: [] for e in ENGS}
        self.sem = {}
        self.cnt = {}
        self.seen = {e: {} for e in ENGS}
        self.nsem = 0
        for e in ENGS:
            self._rot(e)
        self.slots = {e: [[self._newsem(f"d{e}{k}"), 0] for k in range(n)] for e, n in DMA_SLOTS.items()}
        self.dn = {e: 0 for e in DMA_SLOTS}

    def _newsem(self, nm):
        self.nsem += 1
        return self.es.enter_context(self.nc.semaphore(f"{nm}_{self.nsem}"))

    def _rot(self, e):
        self.sem[e] = self._newsem("s" + e)
        self.cnt[e] = 0

    def _deps(self, eng, reads, writes, cdma=False):
        evs = {}

        def add(ev):
            s, v = ev[0], ev[1]
            k = id(s)
            if k not in evs or evs[k][1] < v:
                evs[k] = (s, v)

        for r in reads:
            for ev in r.w.values():
                add(ev)
        for w in writes:
            for ev in w.w.values():
                if cdma and w.cw and ev[2]:
                    continue
                add(ev)
            for ev in w.rs.values():
                add(ev)
        waits = []
        seen = self.seen[eng]
        for k, (s, v) in evs.items():
            if eng == "pe" and s is self.sem["pe"]:
                continue
            if seen.get(k, 0) >= v:
                continue
            seen[k] = v
            waits.append((s, v))
        return waits

    def _mark(self, ev, reads, writes, cdma=False):
        k = id(ev[0])
        for r in reads:
            r.rs[k] = (ev[0], ev[1])
        for w in writes:
            if cdma and w.cw:
                w.w[k] = (ev[0], ev[1], True)
            else:
                w.w = {k: (ev[0], ev[1], False)}
            w.rs = {}

    def op(self, eng, fn, reads=(), writes=()):
        waits = self._deps(eng, reads, writes)
        if self.cnt[eng] >= MAXV:
            self._rot(eng)
        self.cnt[eng] += 1
        ev = (self.sem[eng], self.cnt[eng])
        self.q[eng].append((waits, fn, (ev[0], 1)))
        self._mark(ev, reads, writes)

    def dma(self, eng, out, in_, reads=(), writes=()):
        nsl = len(self.slots[eng])
        slot = self.slots[eng][self.dn[eng] % nsl]
        self.dn[eng] += 1
        if slot[1] >= MAXV:
            slot[0] = self._newsem(f"d{eng}r")
            slot[1] = 0
        sem = slot[0]
        waits = self._deps(eng, reads, writes, cdma=True)
        if slot[1] > 0 and self.seen[eng].get(id(sem), 0) < slot[1]:
            self.seen[eng][id(sem)] = slot[1]
            waits.append((sem, slot[1]))
        slot[1] += 16
        ev = (sem, slot[1])
        self.q[eng].append((waits, lambda e: e.dma_start(out=out, in_=in_, allow_slow_non_contiguous=True), (sem, 16)))
        self._mark(ev, reads, writes, cdma=True)

    def barrier(self):
        evs = [(self.sem[e], self.cnt[e]) for e in ENGS if self.cnt[e] > 0]
        for e in self.slots:
            for (sm, c) in self.slots[e]:
                if c > 0:
                    evs.append((sm, c))
        for e in ENGS:
            waits = []
            for s, v in evs:
                if s is self.sem[e]:
                    continue
                if self.seen[e].get(id(s), 0) >= v:
                    continue
                self.seen[e][id(s)] = v
                waits.append((s, v))
            if waits:
                self.q[e].append((waits, None, None))

    def emit(self, blk):
        def run(eng_name):
            def body(e):
                for waits, fn, inc in self.q[eng_name]:
                    for s, v in waits:
                        e.wait_ge(s, v)
                    if fn is not None:
                        ins = fn(e)
                        ins.then_inc(inc[0], inc[1])
            return body

        blk.tensor(run("pe"))
        blk.scalar(run("act"))
        blk.vector(run("dve"))
        blk.gpsimd(run("pool"))
        blk.sync(run("sp"))

    def mm(self, out, lhsT, rhs, start, stop, reads, writes):
        self.op("pe", lambda e: e.matmul(out, lhsT=lhsT, rhs=rhs, start=start, stop=stop,
                                         skip_group_check=True), reads, writes)

    def tr(self, out, in_, ident, reads, writes):
        self.op("pe", lambda e: e.transpose(out, in_, ident), reads, writes)

    def act(self, out, in_, func, reads, writes, bias=0.0, scale=1.0, accum_out=None):
        if accum_out is None:
            self.op("act", lambda e: e.activation(out=out, in_=in_, func=func, bias=bias, scale=scale),
                    reads, writes)
        else:
            self.op("act", lambda e: e.activation(out=out, in_=in_, func=func, bias=bias, scale=scale,
                                                  accum_out=accum_out), reads, writes)

    def ts(self, eng, out, in0, s1, s2, op0, op1, reads, writes):
        if s2 is None:
            self.op(eng, lambda e: e.tensor_scalar(out=out, in0=in0, scalar1=s1, scalar2=None, op0=op0),
                    reads, writes)
        else:
            self.op(eng, lambda e: e.tensor_scalar(out=out, in0=in0, scalar1=s1, scalar2=s2, op0=op0, op1=op1),
                    reads, writes)

    def tt(self, eng, out, in0, in1, op, reads, writes):
        self.op(eng, lambda e: e.tensor_tensor(out=out, in0=in0, in1=in1, op=op), reads, writes)

    def stt(self, eng, out, in0, scalar, in1, op0, op1, reads, writes):
        self.op(eng, lambda e: e.scalar_tensor_tensor(out=out, in0=in0, scalar=scalar, in1=in1, op0=op0, op1=op1),
                reads, writes)

    def cp(self, eng, out, in_, reads, writes):
        self.op(eng, lambda e: e.tensor_copy(out=out, in_=in_), reads, writes)

    def memset(self, eng, ap, val, writes):
        self.op(eng, lambda e: e.memset(ap, val), (), writes)


class Arena:
    def __init__(self, t, nbytes):
        self.t, self.n, self.off = t, nbytes, 0

    def mark(self):
        return self.off

    def release(self, m):
        self.off = m

    def alloc(self, shape, dt, name=""):
        esz = 4 if dt == F32 else 2
        n = 1
        for s in shape[1:]:
            n *= s
        nb = (n * esz + 63) // 64 * 64
        assert self.off + nb <= self.n, f"arena overflow {name}: {self.off}+{nb}>{self.n}"
        self.hw = max(getattr(self, "hw", 0), self.off + nb)
        v = self.t[:, self.off:self.off + n * esz].bitcast(dt)
        self.off += nb
        if len(shape) == 3:
            v = v.rearrange("p (a b) -> p a b", a=shape[1])
        elif len(shape) == 4:
            v = v.rearrange("p (a b c) -> p a b c", a=shape[1], b=shape[2])
        if shape[0] < 128:
            v = v[0:shape[0]]
        return v


def build(cfg):
    D, NH, T, DFF, PAST, DT = cfg["D"], cfg["NH"], cfg["T"], cfg["DFF"], cfg["PAST"], cfg["DT"]
    KD = D // 128
    W = NH * 64
    WC = W // 128
    MIX = 2 * W
    MC = MIX // 128
    IN = 6 * W + NH
    FFC = DFF // 128
    NZ = T // 2048
    TO = T // 4
    NSUBO = TO // 128
    PB = PAST // 128
    SKB = PB + 1
    SK = SKB * 128
    NQS = DT
    assert DT == 64 and T % 2048 == 0 and D % 512 == 0 and W % 128 == 0
    CQ, CK, CV = 0, W, 2 * W
    FQ, FK, FV, LF = 3 * W, 4 * W, 5 * W, 6 * W

    nc = bass.Bass("TRN2", target_bir_lowering=False)

    def din(name, shape):
        return nc.dram_tensor(name, list(shape), F32, kind="ExternalInput").ap()

    def dout(name, shape):
        return nc.dram_tensor(name, list(shape), F32, kind="ExternalOutput").ap()

    def dscr(name, shape, dt):
        return nc.dram_tensor(name, list(shape), dt, kind="Internal").ap()

    xfull = din("xfull", (T, D)); xown = din("xown", (TO, D)); xsam = din("xsam", (128, D))
    cT = din("cT", (128, KD * 3))
    csk = din("csk", (2, PAST, W)); csv = din("csv", (2, PAST, W))
    cfk = din("cfk", (2, PAST, W)); cfv = din("cfv", (2, PAST, W)); clfT = din("clfT", (NH, 2 * PAST))
    w_ada = din("w_ada", (D, 6 * D)); b_ada = din("b_ada", (6 * D,))
    w_in = din("w_in", (D, IN)); w_o = din("w_o", (MIX, D))
    w_gate = din("w_gate", (D, DFF)); w_up = din("w_up", (D, DFF)); w_down = din("w_down", (DFF, D))
    g_mix = din("g_mix", (D,)); g_ffn = din("g_ffn", (D,)); g_final = din("g_final", (D,))
    goT = din("goT", (128, MC)); b_f = din("b_f", (NH,))
    cst = din("cst", (128, 640)); masks = din("masks", (128, 2 * 16 * 512)); smask = din("smask", (128, 2 * 64))
    sel = din("sel", (128, 4))

    y_own = dout("y_own", (TO, D)); y_sam = dout("y_sam", (128, D))
    o_sbk = dout("o_sbk", (TO, W)); o_sbv = dout("o_sbv", (TO, W)); o_fxk = dout("o_fxk", (TO, W))
    o_fxv = dout("o_fxv", (TO, W)); o_lf = dout("o_lf", (TO, NH))
    s_sbk = dout("s_sbk", (128, W)); s_sbv = dout("s_sbv", (128, W)); s_fxk = dout("s_fxk", (128, W))
    s_fxv = dout("s_fxv", (128, W)); s_lf = dout("s_lf", (128, NH))

    KTs = dscr("KTs", (NH // 2, 2, 128, T), BF16)
    KAs = dscr("KAs", (NH, 6, T), BF16)
    QAs = dscr("QAs", (NH, 6, TO), BF16)
    Vs = dscr("Vs", (2, T, W), BF16)
    QTs = dscr("QTs", (NH // 2, 2, 128, TO), BF16)
    Os = dscr("Os", (TO, MIX), F32)
    KTss = dscr("KTss", (2, NH, 2, 70, SK), BF16)
    Vss = dscr("Vss", (2, 2, SK, W), BF16)
    QTss = dscr("QTss", (2, NH, 2, 70, NQS), BF16)
    Oss = dscr("Oss", (128, MIX), F32)
    adas = dscr("adas", (3, 6, D), F32)
    wos = dscr("wos", (MIX, D), BF16); wgs = dscr("wgs", (D, DFF), BF16)
    wus = dscr("wus", (D, DFF), BF16); wds = dscr("wds", (DFF, D), BF16)

    es = ExitStack()
    ARENA_BYTES = cfg.get("ARENA", 207 * 1024)
    arena_t = es.enter_context(nc.sbuf_tensor("arena", [128, ARENA_BYTES], U8))
    ps = es.enter_context(nc.psum_tensor("ps", [128, 4096], F32))
    P = Prog(nc, es)
    AR = Arena(arena_t, ARENA_BYTES)

    def bank(i):
        return ps[:, i * 512:(i + 1) * 512]

    def bank_bf(i):
        return ps[:, i * 512:(i + 1) * 512].bitcast(BF16)

    RB = [Res(f"bank{i}") for i in range(8)]
    R_scr = {k: Res(k) for k in ("KTs", "Vs", "QTs", "Os", "KTss", "Vss", "QTss", "Oss", "adas", "KAs", "QAs")}
    R_in = Res("inputs")
    R_out = Res("outputs")

    identf = AR.alloc([128, 128], F32, "identf")
    identb = AR.alloc([128, 128], BF16, "identb")
    triIb = AR.alloc([128, 128], BF16, "triI")
    compb = AR.alloc([128, 128], BF16, "comp")
    zrow = AR.alloc([128, 128], BF16, "zrow")
    onesb = AR.alloc([128, 512], BF16, "onesb")
    selt = AR.alloc([128, 4], F32, "sel")
    goTt = AR.alloc([128, MC], F32, "goT")
    bft = AR.alloc([128, 1], F32, "bft")
    nbft = AR.alloc([128, 1], F32, "nbft")
    bfrow = AR.alloc([128, NH], F32, "bfrow")
    scT = AR.alloc([128, KD, 3], F32, "scT")
    fmix = AR.alloc([128, 2, 3, KD], F32, "fmix")
    fffn = AR.alloc([128, 2, 3, KD], F32, "fffn")
    R_c = Res("consts")
    R_f = Res("fvecs")

    P.dma("sp", identf, cst[:, 0:128], (R_in,), (R_c,))
    P.dma("pool", identb, cst[:, 0:128], (R_in,), (R_c,))
    P.dma("pool", triIb, cst[:, 128:256], (R_in,), (R_c,))
    P.dma("pool", compb, cst[:, 256:384], (R_in,), (R_c,))
    P.dma("pool", zrow, cst[:, 384:512], (R_in,), (R_c,))
    P.dma("sp", selt, sel, (R_in,), (R_c,))
    P.dma("sp", goTt, goT, (R_in,), (R_c,))
    P.dma("sp", bft[0:NH, :], b_f.rearrange("(h o) -> h o", o=1), (R_in,), (R_c,))
    P.dma("sp", bfrow, b_f.partition_broadcast(128), (R_in,), (R_c,))
    P.memset("dve", onesb, 1.0, (R_c,))
    P.ts("dve", nbft[0:NH, :], bft[0:NH, :], -1.0, None, ALU.mult, None, (R_c,), (R_c,))

    m0 = AR.mark()
    ctile = AR.alloc([128, KD * 3], F32, "ctile")
    ctmp = AR.alloc([128, KD * 3], F32, "ctmp")
    arow = AR.alloc([128, 6 * D], F32, "arow")[0:3]
    brow = AR.alloc([128, 6 * D], F32, "brow")[0:3]
    grow = AR.alloc([128, 2 * D], F32, "grow")[0:3]
    wadab = [AR.alloc([128, KD, 512], F32, f"wada{i}") for i in range(2)]
    R_ct, R_arow, R_wada = Res("ct"), Res("arow", stg=True), [Res("wada0"), Res("wada1")]
    P.dma("sp", ctile, cT, (R_in,), (R_ct,))
    P.dma("sp", brow, b_ada.partition_broadcast(3), (R_in,), (R_arow,))
    P.dma("sp", grow[:, 0:D], g_mix.partition_broadcast(3), (R_in,), (R_arow,))
    P.dma("sp", grow[:, D:2 * D], g_ffn.partition_broadcast(3), (R_in,), (R_arow,))
    P.act(ctmp, ctile, AF.Exp, (R_ct,), (R_ct,), scale=-1.0)
    P.ts("dve", ctmp, ctmp, 1.0, None, ALU.add, None, (R_ct,), (R_ct,))
    P.op("dve", lambda e: e.reciprocal(out=ctmp, in_=ctmp), (R_ct,), (R_ct,))
    P.tt("dve", scT.rearrange("p k c -> p (k c)"), ctile, ctmp, ALU.mult, (R_ct,), (R_c,))
    wadav = w_ada.rearrange("(k p) c -> p k c", p=128)
    NAC = 6 * D // 512
    for j in range(NAC):
        wb, rw = wadab[j % 2], R_wada[j % 2]
        P.dma("sp", wb, wadav[:, :, j * 512:(j + 1) * 512], (R_in,), (rw,))
        bk = bank(j % 2)
        for k in range(KD):
            P.mm(bk[0:3, :], scT[:, k, :], wb[:, k, :], k == 0, k == KD - 1, (rw, R_c), (RB[j % 2],))
        P.tt("dve", arow[:, j * 512:(j + 1) * 512], bk[0:3, :], brow[:, j * 512:(j + 1) * 512], ALU.add,
             (RB[j % 2], R_arow), (R_arow,))
    for idx, gsrc in ((1, 0), (4, 1)):
        P.stt("dve", arow[:, idx * D:(idx + 1) * D], arow[:, idx * D:(idx + 1) * D], 1.0,
              grow[:, gsrc * D:(gsrc + 1) * D], ALU.add, ALU.mult, (R_arow,), (R_arow,))
    for idx in (2, 5):
        P.ts("dve", arow[:, idx * D:(idx + 1) * D], arow[:, idx * D:(idx + 1) * D], 1.0, None, ALU.add, None,
             (R_arow,), (R_arow,))
    P.dma("sp", adas.rearrange("c s d -> c (s d)"), arow, (R_arow,), (R_scr["adas"],))
    bkf = bank(2)
    vecs = (1, 0, 4, 3)
    for vi, vec in enumerate(vecs):
        for k in range(KD):
            j = vi * KD + k
            P.tr(bkf[:, j * 3:(j + 1) * 3], arow[0:3, vec * D + k * 128: vec * D + (k + 1) * 128], identf[0:3, 0:3],
                 (R_arow, R_c), (RB[2],))
    bv = bkf[:, 0:4 * KD * 3].rearrange("p (v k c) -> p v k c", v=4, k=KD)
    for vi, (dst, a_) in enumerate(((fmix, 0), (fmix, 1), (fffn, 0), (fffn, 1))):
        P.cp("dve", dst[:, a_, :, :], bv[:, vi, :, :].rearrange("p k c -> p c k"), (RB[2],), (R_f,))
    P.barrier()
    AR.release(m0)

    def norm_to_fm(xt, G, width, tmpsq, ss, xs_b, hT, conds, fvec, Rx, Rtmp, RhT, nchunk, tbank, part="all"):
        if part in ("all", "pre"):
            for g in range(G):
                P.memset("dve", ss[:, g:g + 1], 0.0, (Rtmp,))
                P.act(tmpsq[:, 0:width], xt[:, g, :], AF.Square, (Rx, Rtmp), (Rtmp,), accum_out=ss[:, g:g + 1])
            P.ts("dve", ss[:, 0:G], ss[:, 0:G], 1.0 / width, EPS, ALU.mult, ALU.add, (Rtmp,), (Rtmp,))
            P.act(ss[:, 0:G], ss[:, 0:G], AF.Ln, (Rtmp,), (Rtmp,))
            P.act(ss[:, 0:G], ss[:, 0:G], AF.Exp, (Rtmp,), (Rtmp,), scale=-0.5)
            for g in range(G):
                P.ts("dve", xs_b[:, g, :], xt[:, g, :], ss[:, g:g + 1], None, ALU.mult, None, (Rx, Rtmp), (Rtmp,))
        if part == "pre":
            return
        for k in range(nchunk):
            bi = tbank[k % len(tbank)]
            pb = bank_bf(bi)
            for g in range(G):
                P.tr(pb[:, g * 128:(g + 1) * 128], xs_b[:, g, k * 128:(k + 1) * 128], identb, (Rtmp, R_c), (RB[bi],))
            for (lo, hi, cnd) in conds:
                sc_ap, sh_ap = fvec(cnd, k)
                src = pb[:, 0:G * 128].rearrange("p (g t) -> p g t", g=G)[:, :, lo:hi]
                dst = hT[:, k, 0:G * 128].rearrange("p (g t) -> p g t", g=G)[:, :, lo:hi]
                if sh_ap is None:
                    P.ts("dve", dst, src, sc_ap, None, ALU.mult, None, (RB[bi], R_f, R_c), (RhT,))
                else:
                    P.ts("dve", dst, src, sc_ap, sh_ap, ALU.mult, ALU.add, (RB[bi], R_f, R_c), (RhT,))

    zbig = AR.alloc([128, 512], BF16, "zbig")
    mL = AR.mark()
    LGT = AR.alloc([128, T], F32, "LGT")[0:NH]
    LGS = AR.alloc([128, 2 * (PAST + DT)], F32, "LGS")[0:NH]
    LGSv = LGS.rearrange("h (s n) -> h s n", s=2)
    lgtok = AR.alloc([128, NSUBO + 1, NH], F32, "lgtok")
    R_LGT, R_LGS, R_lgtok = Res("LGT"), Res("LGS"), Res("lgtok", stg=True)
    P.memset("dve", zbig, 0.0, (R_c,))
    mA = AR.mark()
    winb = AR.alloc([128, KD, IN], BF16, "winb")
    R_win = Res("win")
    winv = w_in.rearrange("(k p) c -> p k c", p=128)
    for k in range(KD):
        P.dma("pool", winb[:, k, :], winv[:, k, :], (R_in,), (R_win,))
    R_wsc = {"wo": Res("wos"), "wg": Res("wgs"), "wu": Res("wus"), "wd": Res("wds")}
    precast = []
    for (nm, dst_, src_) in (("wo", wos, w_o), ("wg", wgs, w_gate), ("wu", wus, w_up), ("wd", wds, w_down)):
        nr = src_.shape[0]
        for r0 in range(0, nr, 256):
            precast.append((dst_[r0:min(nr, r0 + 256), :], src_[r0:min(nr, r0 + 256), :], R_wsc[nm]))

    def issue_precast(n):
        for _ in range(n):
            if precast:
                d_, s_, r_ = precast.pop(0)
                P.dma("pool", d_, s_, (R_in,), (r_,))
    GA = 4
    xt2 = [AR.alloc([128, GA, D], F32, f"xt{i}") for i in range(2)]
    R_xt = [Res("xt0"), Res("xt1")]
    tmpsq = AR.alloc([128, D], F32, "tmpsq")
    ssA = AR.alloc([128, 8], F32, "ssA")
    xs_b2 = [AR.alloc([128, GA, D], BF16, f"xs_b{i}") for i in range(2)]
    hTA2 = [AR.alloc([128, KD, GA * 128], BF16, f"hTA{i}") for i in range(2)]
    R_tmpA2, R_hTA2 = [Res("tmpA0"), Res("tmpA1")], [Res("hTA0"), Res("hTA1")]
    cur = {"i": 0}

    class _H:
        def __getitem__(self, key):
            return hTA2[cur["i"]][key]
    hTA = _H()

    class _R:
        pass

    NST = 2
    fmo = [AR.alloc([128, 512], BF16, f"fmo{i}") for i in range(NST)]
    R_fmo = [Res(f"fmo{i}", stg=True) for i in range(NST)]
    tmo = [AR.alloc([128, 512], BF16, f"tmo{i}") for i in range(NST)]
    R_tmo = [Res(f"tmo{i}", stg=True) for i in range(NST)]
    tmf = [AR.alloc([128, 512], F32, f"tmf{i}") for i in range(NST)]
    R_tmf = [Res(f"tmf{i}", stg=True) for i in range(NST)]
    cnt = {"fm": 0, "tm": 0, "tf": 0, "x": 0}

    def fvec_mix(cnd, k):
        return fmix[:, 0, cnd, k:k + 1], fmix[:, 1, cnd, k:k + 1]

    def fm_chunk(N, col0, scale):
        bi = 2 + (cnt["fm"] % 2)
        bk = bank(bi)
        for k in range(KD):
            P.mm(bk[:, 0:N], winb[:, k, col0:col0 + 128], hTA[:, k, 0:N], k == 0, k == KD - 1,
                 (R_win, R_hTA2[cur["i"]]), (RB[bi],))
        i = cnt["fm"] % NST
        cnt["fm"] += 1
        P.act(fmo[i][:, 0:N], bk[:, 0:N], AF.Copy, (RB[bi],), (R_fmo[i],), scale=scale)
        return fmo[i], R_fmo[i]

    def tm_block(g, col0, ncols):
        bi = 4 + (cnt["tm"] % 2)
        cnt["tm"] += 1
        bk = bank(bi)
        for k in range(KD):
            P.mm(bk[:, 0:ncols], hTA[:, k, g * 128:(g + 1) * 128], winb[:, k, col0:col0 + ncols], k == 0,
                 k == KD - 1, (R_win, R_hTA2[cur["i"]]), (RB[bi],))
        return bk, bi

    def to_bf(bk, bi, ncols):
        i = cnt["tf"] % NST
        cnt["tf"] += 1
        P.cp("dve", tmo[i][:, 0:ncols], bk[:, 0:ncols], (RB[bi],), (R_tmo[i],))
        return tmo[i], R_tmo[i]

    def to_f32(bk, bi, ncols):
        i = cnt["tf"] % NST
        cnt["tf"] += 1
        P.act(tmf[i][:, 0:ncols], bk[:, 0:ncols], AF.Copy, (RB[bi],), (R_tmf[i],))
        return tmf[i], R_tmf[i]

    ssA2 = [ssA, AR.alloc([128, 8], F32, "ssA1")]
    tmpsq2 = [tmpsq, AR.alloc([128, D], BF16, "tmpsq1")]

    def front(xsrc, G, conds, i, part):
        if part == "pre":
            P.dma("pool", xt2[i][:, 0:G, :], xsrc.rearrange("(g p) d -> p g d", p=128), (R_in,), (R_xt[i],))
        norm_to_fm(xt2[i], G, D, tmpsq2[i], ssA2[i], xs_b2[i], hTA2[i], conds, fvec_mix, R_xt[i], R_tmpA2[i],
                   R_hTA2[i], KD, (0, 1), part=part)

    def logits_fm(N):
        bk = bank(6)
        for k in range(KD):
            P.mm(bk[0:NH, 0:N], winb[:, k, LF:LF + NH], hTA[:, k, 0:N], k == 0, k == KD - 1,
                 (R_win, R_hTA2[cur["i"]]), (RB[6],))
        return bk

    def logits_tm(g, slot):
        bk = bank(7)
        for k in range(KD):
            P.mm(bk[:, 0:NH], hTA[:, k, g * 128:(g + 1) * 128], winb[:, k, LF:LF + NH], k == 0, k == KD - 1,
                 (R_win, R_hTA2[cur["i"]]), (RB[7],))
        P.tt("dve", lgtok[:, slot, :], bk[:, 0:NH], bfrow, ALU.add, (RB[7], R_c), (R_lgtok,))

    CBLK = [(c0, min(512, W - c0)) for c0 in range(0, W, 512)]

    tiles = []

    def a1_p1(tt_):
        for s_, col0 in ((0, CK), (1, FK)):
            for c in range(WC):
                t_, r_ = fm_chunk(512, col0 + c * 128, 1.0)
                P.dma("sp", KTs[c, s_, :, tt_ * 512:(tt_ + 1) * 512], t_[:, 0:512], (r_,), (R_scr["KTs"],))

    def a1_p2(tt_):
        for g in range(GA):
            tok0 = tt_ * 512 + g * 128
            for s_, col0 in ((0, CV), (1, FV)):
                for (c0, nn) in CBLK:
                    bk, bi = tm_block(g, col0 + c0, nn)
                    t_, r_ = to_bf(bk, bi, nn)
                    P.dma("sp", Vs[s_, tok0:tok0 + 128, c0:c0 + nn], t_[:, 0:nn], (r_,), (R_scr["Vs"],))
        bk = logits_fm(512)
        P.cp("dve", LGT[:, tt_ * 512:(tt_ + 1) * 512], bk[0:NH, 0:512], (RB[6],), (R_LGT,))

    for tt_ in range(T // 512):
        tiles.append((xfull[tt_ * 512:(tt_ + 1) * 512, :], GA, [(0, 128, 0)],
                      (lambda tt_=tt_: a1_p1(tt_)), (lambda tt_=tt_: a1_p2(tt_))))

    def a2_p1(z):
        for s_, col0 in ((0, CQ), (1, FQ)):
            for c in range(WC):
                t_, r_ = fm_chunk(512, col0 + c * 128, 0.125)
                P.dma("sp", QTs[c, s_, :, z * 512:(z + 1) * 512], t_[:, 0:512], (r_,), (R_scr["QTs"],))

    def a2_p2(z):
        for g in range(GA):
            r0 = z * 512 + g * 128
            for (col0, od) in ((CK, o_sbk), (CV, o_sbv), (FK, o_fxk), (FV, o_fxv)):
                for (c0, nn) in CBLK:
                    bk, bi = tm_block(g, col0 + c0, nn)
                    t_, r_ = to_f32(bk, bi, nn)
                    P.dma("sp", od[r0:r0 + 128, c0:c0 + nn], t_[:, 0:nn], (r_,), (R_out,))
            logits_tm(g, z * 4 + g)

    for z in range(NZ):
        tiles.append((xown[z * 512:(z + 1) * 512, :], GA, [(0, 128, 0)],
                      (lambda z=z: a2_p1(z)), (lambda z=z: a2_p2(z))))

    def as_p1():
        for (s_, colq, colk) in ((0, CQ, CK), (1, FQ, FK)):
            for kind, col0, scale in (("q", colq, 0.125), ("k", colk, 1.0)):
                for c in range(WC):
                    t_, r_ = fm_chunk(128, col0 + c * 128, scale)
                    for half in range(2):
                        h = 2 * c + half
                        for sbi in range(2):
                            src = t_[half * 64:(half + 1) * 64, sbi * 64:(sbi + 1) * 64]
                            if kind == "q":
                                P.dma("sp", QTss[sbi, h, s_, 0:64, :], src, (r_,), (R_scr["QTss"],))
                            else:
                                P.dma("sp", KTss[sbi, h, s_, 0:64, 0:64], src, (r_,), (R_scr["KTss"],))

    def as_p2():
        for (col0, od) in ((CK, s_sbk), (CV, s_sbv), (FK, s_fxk), (FV, s_fxv)):
            for (c0, nn) in CBLK:
                bk, bi = tm_block(0, col0 + c0, nn)
                t_, r_ = to_f32(bk, bi, nn)
                P.dma("sp", od[0:128, c0:c0 + nn], t_[:, 0:nn], (r_,), (R_out,))
                if col0 in (CV, FV):
                    sidx = 0 if col0 == CV else 1
                    t2, r2 = to_bf(bk, bi, nn)
                    for sbi in range(2):
                        P.dma("sp", Vss[sbi, sidx, 0:64, c0:c0 + nn], t2[sbi * 64:(sbi + 1) * 64, 0:nn],
                              (r2,), (R_scr["Vss"],))
        logits_tm(0, NSUBO)
        bk = logits_fm(128)
        for sbi in range(2):
            P.cp("dve", LGSv[:, sbi, PAST:PAST + DT], bk[0:NH, sbi * 64:(sbi + 1) * 64], (RB[6],), (R_LGS,))

    tiles.append((xsam, 1, [(0, 64, 1), (64, 128, 2)], as_p1, as_p2))

    ktin = AR.alloc([128, PB, W], F32, "ktin")
    ktb = AR.alloc([128, PAST], BF16, "ktb")
    R_ktin, R_ktb = Res("ktin"), Res("ktb", stg=True)
    a2c_jobs = [(sbi, s, ck, cv) for sbi in range(2) for (s, ck, cv) in ((0, csk, csv), (1, cfk, cfv))]

    def a2c_load(sbi, s, ck, cv):
        P.dma("pool", Vss[sbi, s, 128:128 + PAST, :], cv[sbi], (R_in,), (R_scr["Vss"],))
        P.dma("sp", Vss[sbi, s, 64:128, :], zbig[0:64, 0:W], (R_c,), (R_scr["Vss"],))
        P.dma("sp", ktin, ck[sbi].rearrange("(kb p) w -> p kb w", p=128), (R_in,), (R_ktin,))

    def a2c_comp(sbi, s, ck, cv):
        for c in range(WC):
            for kb0 in range(0, PB, 4):
                nk = min(4, PB - kb0)
                bi = 2 + (cnt["fm"] % 2)
                cnt["fm"] += 1
                bk = bank(bi)
                for j in range(nk):
                    P.tr(bk[:, j * 128:(j + 1) * 128], ktin[:, kb0 + j, c * 128:(c + 1) * 128], identf,
                         (R_ktin, R_c), (RB[bi],))
                P.act(ktb[:, kb0 * 128:(kb0 + nk) * 128], bk[:, 0:nk * 128], AF.Copy, (RB[bi],), (R_ktb,))
            for half in range(2):
                P.dma("sp", KTss[sbi, 2 * c + half, s, 0:64, 128:128 + PAST], ktb[half * 64:(half + 1) * 64, :],
                      (R_ktb,), (R_scr["KTss"],))
        P.dma("sp", KTss[sbi, :, s, :, 64:128].rearrange("h r c -> r h c"),
              zbig[0:70, 0:NH * 64].rearrange("r (h c) -> r h c", h=NH), (R_c,), (R_scr["KTss"],))

    a2c_sched = {}
    for j_, job in enumerate(a2c_jobs):
        a2c_sched.setdefault(1 + 4 * j_, []).append(("load", job))
        a2c_sched.setdefault(2 + 4 * j_, []).append(("comp", job))

    def run_a2c(ti):
        for key in sorted(k for k in a2c_sched if k <= ti):
            for kind, job in a2c_sched.pop(key):
                (a2c_load if kind == "load" else a2c_comp)(*job)

    front(tiles[0][0], tiles[0][1], tiles[0][2], 0, "pre")
    front(tiles[0][0], tiles[0][1], tiles[0][2], 0, "tr")
    for ti, (xsrc_, G_, conds_, p1_, p2_) in enumerate(tiles):
        nx = tiles[ti + 1] if ti + 1 < len(tiles) else None
        if nx is not None:
            front(nx[0], nx[1], nx[2], (ti + 1) % 2, "pre")
        issue_precast(2)
        cur["i"] = ti % 2
        p1_()
        run_a2c(ti)
        if nx is not None:
            front(nx[0], nx[1], nx[2], (ti + 1) % 2, "tr")
        cur["i"] = ti % 2
        p2_()
    issue_precast(1000)
    run_a2c(10 ** 9)
    lgflat = lgtok.rearrange("p a h -> p (a h)")
    P.act(lgflat, lgflat, AF.Exp, (R_lgtok,), (R_lgtok,), scale=-1.0)
    P.act(lgflat, lgflat, AF.Ln, (R_lgtok,), (R_lgtok,), bias=1.0)
    P.ts("dve", lgflat, lgflat, -1.0, None, ALU.mult, None, (R_lgtok,), (R_lgtok,))
    with nc.allow_non_contiguous_dma(reason="small logf rows"):
        P.dma("sp", o_lf.rearrange("(a p) h -> p a h", p=128), lgtok[:, 0:NSUBO, :], (R_lgtok,), (R_out,))
        P.dma("sp", s_lf, lgtok[:, NSUBO, :], (R_lgtok,), (R_out,))

    P.barrier()
    AR.release(mA)

    mF = AR.mark()
    Gp = AR.alloc([128, T], F32, "Gp")[0:NH]
    Go = AR.alloc([128, TO], F32, "Go")[0:NH]
    Gs = AR.alloc([128, 2 * (PAST + DT)], F32, "Gs")[0:NH]
    Gsv = Gs.rearrange("h (s n) -> h s n", s=2)
    Gq = AR.alloc([128, 2 * DT], F32, "Gq")[0:NH]
    Gqv = Gq.rearrange("h (s n) -> h s n", s=2)
    pbuf = [AR.alloc([128, max(T, 2 * (PAST + DT))], BF16, f"pbuf{i}")[0:NH] for i in range(2)]
    R_G, R_Go, R_Gs, R_Gq = Res("G"), Res("Go"), Res("Gs"), Res("Gq")
    R_pb = [Res("pb0", stg=True), Res("pb1", stg=True)]
    pcount = [0]

    def cumsum(dst, src, n, rsrc, rdst):
        for c0 in range(0, n, 512):
            nn = min(512, n - c0)
            init = 0.0 if c0 == 0 else dst[:, c0 - 1:c0]
            P.op("dve", lambda e, c0=c0, nn=nn, init=init: e.tensor_tensor_scan(
                out=dst[:, c0:c0 + nn], data0=onesb[0:NH, 0:nn], data1=src[:, c0:c0 + nn], initial=init,
                op0=ALU.mult, op1=ALU.add), (rsrc, rdst, R_c), (rdst,))

    def split_rows(src, n, rsrc, dst_fn):
        for j in range(3):
            i = pcount[0] % 2
            pcount[0] += 1
            pb = pbuf[i]
            P.cp("dve", pb[:, 0:n], src, (rsrc,), (R_pb[i],))
            if j < 2:
                P.tt("dve", src, src, pb[:, 0:n], ALU.subtract, (rsrc, R_pb[i]), (rsrc,))
            for (d_, s_, rd_) in dst_fn(j, pb):
                P.dma("sp", d_, s_, (R_pb[i],), (rd_,))

    def ones_rows(n, dst_list):
        i = pcount[0] % 2
        pcount[0] += 1
        P.memset("dve", pbuf[i][:, 0:n], 1.0, (R_pb[i],))
        for (d_, rd_) in dst_list:
            P.dma("sp", d_, pbuf[i][:, 0:d_.shape[-1]], (R_pb[i],), (rd_,))

    P.act(LGT, LGT, AF.Exp, (R_LGT, R_c), (R_LGT,), scale=-1.0, bias=nbft[0:NH, :])
    P.act(LGT, LGT, AF.Ln, (R_LGT,), (R_LGT,), bias=1.0)
    cumsum(Gp, LGT, T, R_LGT, R_G)
    Gv = Gp.rearrange("h (m r t) -> h m r t", r=4, t=128)
    Gov = Go.rearrange("h (m t) -> h m t", t=128)
    P.ts("dve", Gov, Gv[:, :, 0, :], selt[0:NH, 0:1], None, ALU.mult, None, (R_G, R_c), (R_Go,))
    for rr in range(1, 4):
        P.stt("dve", Gov, Gv[:, :, rr, :], selt[0:NH, rr:rr + 1], Gov, ALU.mult, ALU.add, (R_G, R_c, R_Go), (R_Go,))
    P.ts("dve", Go, Go, -1.0, None, ALU.mult, None, (R_Go,), (R_Go,))
    split_rows(Gp, T, R_G, lambda j, pb: [(KAs[:, 3 + j, :], pb[:, 0:T], R_scr["KAs"])])
    split_rows(Go, TO, R_Go, lambda j, pb: [(QAs[:, j, :], pb[:, 0:TO], R_scr["QAs"])])
    ones_rows(T, [(KAs[:, j, :], R_scr["KAs"]) for j in range(3)] +
              [(QAs[:, 3 + j, :], R_scr["QAs"]) for j in range(3)])
    P.dma("sp", LGSv[:, :, 0:PAST], clfT.rearrange("h (s n) -> h s n", s=2), (R_in,), (R_LGS,))
    P.ts("dve", LGSv[:, :, 0:PAST], LGSv[:, :, 0:PAST], -1.0, None, ALU.mult, None, (R_LGS,), (R_LGS,))
    P.act(LGSv[:, :, PAST:PAST + DT], LGSv[:, :, PAST:PAST + DT], AF.Exp, (R_LGS, R_c), (R_LGS,), scale=-1.0,
          bias=nbft[0:NH, :])
    P.act(LGSv[:, :, PAST:PAST + DT], LGSv[:, :, PAST:PAST + DT], AF.Ln, (R_LGS,), (R_LGS,), bias=1.0)
    for sbi in range(2):
        cumsum(Gsv[:, sbi, :], LGSv[:, sbi, :], PAST + DT, R_LGS, R_Gs)
    P.ts("dve", Gqv, Gsv[:, :, PAST:PAST + DT], -1.0, None, ALU.mult, None, (R_Gs,), (R_Gq,))

    def kdst_s(j, pb):
        pv = pb[:, 0:2 * (PAST + DT)].rearrange("h (s n) -> h s n", s=2)
        out = []
        for sbi in range(2):
            out.append((KTss[sbi, :, 1, 67 + j, 0:DT], pv[:, sbi, PAST:PAST + DT], R_scr["KTss"]))
            out.append((KTss[sbi, :, 1, 67 + j, 128:128 + PAST], pv[:, sbi, 0:PAST], R_scr["KTss"]))
        return out

    def qdst_s(j, pb):
        pv = pb[:, 0:2 * DT].rearrange("h (s n) -> h s n", s=2)
        return [(QTss[sbi, :, 1, 64 + j, :], pv[:, sbi, :], R_scr["QTss"]) for sbi in range(2)]

    split_rows(Gs, 2 * (PAST + DT), R_Gs, kdst_s)
    split_rows(Gq, 2 * DT, R_Gq, qdst_s)
    ones_rows(SK, [(KTss[sbi, :, 1, 64 + j, :], R_scr["KTss"]) for sbi in range(2) for j in range(3)] +
              [(QTss[sbi, :, 1, 67 + j, :], R_scr["QTss"]) for sbi in range(2) for j in range(3)])
    P.barrier()
    AR.release(mL)

    mB = AR.mark()
    maskb = AR.alloc([128, 2, 16, 512], BF16, "maskb")
    smaskb = AR.alloc([128, 2, 64], BF16, "smaskb")
    R_mask = Res("mask")
    mv = masks.rearrange("p (s k q) -> p s k q", s=2, k=16)
    for s in range(2):
        for k4 in range(0, 16, 4):
            P.dma("pool", maskb[:, s, k4:k4 + 4, :], mv[:, s, k4:k4 + 4, :], (R_in,), (R_mask,))
    P.dma("pool", smaskb, smask.rearrange("p (s q) -> p s q", s=2), (R_in,), (R_mask,))
    NKB = T // 128
    KTt = [[AR.alloc([128, T], BF16, f"KTt{s}{i}") for i in range(2)] for s in range(2)]
    Vt = [[AR.alloc([128, NKB, 128], BF16, f"Vt{s}{i}") for i in range(2)] for s in range(2)]
    QTt = [[AR.alloc([128, TO], BF16, f"QTt{s}{i}") for i in range(2)] for s in range(2)]
    R_KT = [[Res(f"KT{s}{i}") for i in range(2)] for s in range(2)]
    R_V = [[Res(f"V{s}{i}") for i in range(2)] for s in range(2)]
    R_QT = [[Res(f"QT{s}{i}") for i in range(2)] for s in range(2)]
    for s in range(2):
        for i in range(2):
            P.memset("pool", Vt[s][i], 0.0, (R_V[s][i],))
            P.memset("pool", KTt[s][i], 0.0, (R_KT[s][i],))
            P.memset("pool", QTt[s][i], 0.0, (R_QT[s][i],))
        for i in range(2):
            if s == 1:
                P.memset("dve", Vt[s][i][:, :, 64:65], 1.0, (R_V[s][i],))
    SPt = [AR.alloc([128, 512], BF16, f"SPt{i}") for i in range(2)]
    Gt = [AR.alloc([128, 512], F32, f"Gt{i}") for i in range(2)]
    at = [AR.alloc([128, 512], BF16, f"at{i}") for i in range(2)]
    Pt = [AR.alloc([128, 512], BF16, f"Pt{i}") for i in range(2)]
    R_E = [Res("E0"), Res("E1")]; R_SP = [Res("SP0"), Res("SP1")]; R_G2 = [Res("G0"), Res("G1")]
    R_a = [Res("a0"), Res("a1")]; R_P = [Res("P0"), Res("P1")]
    osb_s2 = [AR.alloc([128, 512], F32, f"osb_s{i}") for i in range(2)]
    ofx_s2 = [AR.alloc([128, 512], F32, f"ofx_s{i}") for i in range(2)]
    R_os2 = [Res("os0"), Res("os1")]
    pending = []
    fin_cnt = [0]
    otok = AR.alloc([128, 4, 2, 64], F32, "otok")
    rec = AR.alloc([128, 4], F32, "rec")
    R_os, R_otok = Res("os"), Res("otok", stg=True)
    BZ, BS, BA, BOS, BOF, BTP = (0, 1, 7), (2, 3), 4, 5, 6, 0
    itc = [0]

    def sweep(KT, QT, Vtile, rK, rQ, rV, q0, Nq, blocks, odst_fn):
        A = bank(BA); Osb = bank(BOS); Ofx = bank(BOF)
        for (bk_, M, rb) in ((A, 128, RB[BA]), (Osb, 128, RB[BOS]), (Ofx, 128, RB[BOF])):
            P.mm(bk_[0:M, 0:Nq], zrow[0:1, 0:M], onesb[0:1, 0:Nq], True, True, (R_c,), (rb,))
        nb = len(blocks)

        def mm1(i):
            kcol, vkb, qlo, msb, mfx = blocks[i]
            b = (itc[0] + i) % 2
            b3 = (itc[0] + i) % 3
            Z = bank(BZ[b3]); S = bank(BS[b])
            P.mm(Z[:, qlo:Nq], KT[0][:, kcol:kcol + 128], QT[0][:, q0 + qlo:q0 + Nq], True, msb is None,
                 (rK[0], rQ[0]), (RB[BZ[b3]],))
            if msb is not None:
                P.mm(Z[:, qlo:Nq], identb, msb, False, True, (R_c, R_mask), (RB[BZ[b3]],))
            P.mm(S[:, qlo:Nq], KT[1][:, kcol:kcol + 128], QT[1][:, q0 + qlo:q0 + Nq], True, mfx is None,
                 (rK[1], rQ[1]), (RB[BS[b]],))
            if mfx is not None:
                P.mm(S[:, qlo:Nq], identb, mfx, False, True, (R_c, R_mask), (RB[BS[b]],))

        def actE(i):
            kcol, vkb, qlo, msb, mfx = blocks[i]
            b3 = (itc[0] + i) % 3
            Z = bank(BZ[b3])
            P.act(Z[:, qlo:Nq], Z[:, qlo:Nq], AF.Exp, (RB[BZ[b3]],), (RB[BZ[b3]],))

        for f_ in pending:
            f_()
        del pending[:]
        mm1(0)
        actE(0)
        for i in range(nb):
            kcol, vkb, qlo, msb, mfx = blocks[i]
            b = (itc[0] + i) % 2
            b3 = (itc[0] + i) % 3
            Z = bank(BZ[b3]); S = bank(BS[b])
            sl = slice(qlo, Nq)
            if i + 1 < nb:
                mm1(i + 1)
            P.act(SPt[b][:, sl], Z[:, sl], AF.Ln, (RB[BZ[b3]],), (R_SP[b],), bias=1.0)
            P.mm(A[:, sl], triIb, SPt[b][:, sl], False, True, (R_c, R_SP[b]), (RB[BA],))
            P.act(Pt[b][:, sl], S[:, sl], AF.Exp, (RB[BS[b]],), (R_P[b],))
            P.mm(Ofx[:, sl], Vtile[1][:, vkb, :], Pt[b][:, sl], False, True, (rV[1], R_P[b]), (RB[BOF],))
            if i + 1 < nb:
                actE(i + 1)
            if i > 0:
                pk, pv, pq, _, _ = blocks[i - 1]
                pb_ = (itc[0] + i - 1) % 2
                P.mm(Osb[:, pq:Nq], Vtile[0][:, pv, :], at[pb_][:, pq:Nq], False, True,
                     (rV[0], R_a[pb_]), (RB[BOS],))
            P.act(Gt[b][:, sl], A[:, sl], AF.Exp, (RB[BA],), (R_G2[b],))
            P.mm(A[:, sl], compb, SPt[b][:, sl], False, True, (R_c, R_SP[b]), (RB[BA],))
            P.tt("dve", at[b][:, sl], Z[:, sl], Gt[b][:, sl], ALU.mult, (RB[BZ[b3]], R_G2[b]), (R_a[b],))
        pk, pv, pq, _, _ = blocks[nb - 1]
        pb_ = (itc[0] + nb - 1) % 2
        P.mm(Osb[:, pq:Nq], Vtile[0][:, pv, :], at[pb_][:, pq:Nq], False, True, (rV[0], R_a[pb_]), (RB[BOS],))
        itc[0] += nb
        j = fin_cnt[0] % 2
        fin_cnt[0] += 1
        P.cp("dve", osb_s2[j][0:64, 0:Nq], Osb[0:64, 0:Nq], (RB[BOS],), (R_os2[j],))
        P.cp("dve", ofx_s2[j][0:65, 0:Nq], Ofx[0:65, 0:Nq], (RB[BOF],), (R_os2[j],))

        def fin(j=j, Nq=Nq, odst_fn=odst_fn):
            osb_s, ofx_s, R_os = osb_s2[j], ofx_s2[j], R_os2[j]
            nt = min(128, Nq)
            ng = max(1, Nq // 128)
            TP = bank(BTP)
            for i in range(ng):
                P.tr(TP[0:nt, i * 64:(i + 1) * 64], osb_s[0:64, i * 128:i * 128 + nt], identf[0:64, 0:64],
                     (R_os, R_c), (RB[BTP],))
            P.cp("dve", otok[0:nt, 0:ng, 0, :], TP[0:nt, 0:ng * 64].rearrange("p (g c) -> p g c", c=64),
                 (RB[BTP],), (R_otok,))
            for i in range(ng):
                P.tr(TP[0:nt, i * 65:(i + 1) * 65], ofx_s[0:65, i * 128:i * 128 + nt], identf[0:65, 0:65],
                     (R_os, R_c), (RB[BTP],))
            TFv = TP[0:nt, 0:ng * 65].rearrange("p (g c) -> p g c", c=65)
            P.op("dve", lambda e: e.reciprocal(out=rec[0:nt, 0:ng], in_=TFv[:, :, 64]), (RB[BTP],), (R_otok,))
            for i in range(ng):
                P.ts("dve", otok[0:nt, i, 1, :], TFv[:, i, 0:64], rec[0:nt, i:i + 1], None, ALU.mult, None,
                     (RB[BTP], R_otok), (R_otok,))
            for (d_, s_, rd_) in odst_fn(otok, nt, ng):
                P.dma("sp", d_, s_, (R_otok,), (rd_,))

        pending.append(fin)

    hp_count = [0]

    def load_pair(ktsrc, vsrc, qsrc, nkeys, nkb, nq, rk_scr, rv_scr, rq_scr):
        i = hp_count[0] % 2
        hp_count[0] += 1
        for s in range(2):
            for (r0, r1, src) in ktsrc(s):
                P.dma("sp", KTt[s][i][r0:r1, 0:nkeys], src, rk_scr, (R_KT[s][i],))
            for (r0, r1, src) in qsrc(s):
                P.dma("sp", QTt[s][i][r0:r1, 0:nq], src, rq_scr, (R_QT[s][i],))
            P.dma("sp", Vt[s][i][:, 0:nkb, 0:64], vsrc(s).rearrange("(kb k) d -> k kb d", k=128),
                  (rv_scr,), (R_V[s][i],))
        return ([KTt[0][i], KTt[1][i]], [QTt[0][i], QTt[1][i]], [Vt[0][i], Vt[1][i]],
                [R_KT[0][i], R_KT[1][i]], [R_QT[0][i], R_QT[1][i]], [R_V[0][i], R_V[1][i]])

    with nc.allow_non_contiguous_dma(reason="head-sliced V rows / O columns (128-256B segments)"):
        jobs = []
        for p in range(NH):
            def ld(p=p):
                hs = slice((p % 2) * 64, (p % 2) * 64 + 64)
                return load_pair(
                    lambda s: [(0, 64, KTs[p // 2, s, hs, :])] + ([(64, 70, KAs[p])] if s == 1 else []),
                    lambda s: Vs[s, :, p * 64:(p + 1) * 64],
                    lambda s: [(0, 64, QTs[p // 2, s, hs, :])] + ([(64, 70, QAs[p])] if s == 1 else []),
                    T, NKB, TO, (R_scr["KTs"], R_scr["KAs"]), R_scr["Vs"], (R_scr["QTs"], R_scr["QAs"]))

            def sw(ld_, p=p):
                KT, QT, Vl, rK, rQ, rV = ld_
                for z in range(NZ):
                    blocks = []
                    for kbz in range(15, -1, -1):
                        qlo = (kbz // 4) * 128
                        kb = 16 * z + kbz
                        blocks.append((kb * 128, kb, qlo, maskb[:, 0, kbz, qlo:512], maskb[:, 1, kbz, qlo:512]))
                    for kb in range(16 * z - 1, -1, -1):
                        blocks.append((kb * 128, kb, 0, None, None))

                    def odst(ot, nt, ng, z=z, p=p):
                        rows = Os[z * 512:(z + 1) * 512, :].rearrange("(g t) c -> t g c", t=128)
                        return [(rows[:, :, p * 64:(p + 1) * 64], ot[:, :, 0, :], R_scr["Os"]),
                                (rows[:, :, W + p * 64:W + (p + 1) * 64], ot[:, :, 1, :], R_scr["Os"])]

                    sweep(KT, QT, Vl, rK, rQ, rV, z * 512, 512, blocks, odst)
            jobs.append((ld, sw))
        for sbi in range(2):
            for p in range(NH):
                def ld(sbi=sbi, p=p):
                    return load_pair(
                        lambda s: [(0, 64 if s == 0 else 70, KTss[sbi, p, s, 0:(64 if s == 0 else 70), :])],
                        lambda s: Vss[sbi, s, :, p * 64:(p + 1) * 64],
                        lambda s: [(0, 64 if s == 0 else 70, QTss[sbi, p, s, 0:(64 if s == 0 else 70), :])],
                        SK, SKB, NQS, (R_scr["KTss"],), R_scr["Vss"], (R_scr["QTss"],))

                def sw(ld_, sbi=sbi, p=p):
                    KT, QT, Vl, rK, rQ, rV = ld_
                    blocks = [(0, 0, 0, smaskb[:, 0, :], smaskb[:, 1, :])]
                    for kb in range(PB - 1, -1, -1):
                        blocks.append((128 + kb * 128, 1 + kb, 0, None, None))

                    def odst_s(ot, nt, ng, sbi=sbi, p=p):
                        return [(Oss[sbi * 64:(sbi + 1) * 64, p * 64:(p + 1) * 64], ot[0:64, 0, 0, :], R_scr["Oss"]),
                                (Oss[sbi * 64:(sbi + 1) * 64, W + p * 64:W + (p + 1) * 64], ot[0:64, 0, 1, :],
                                 R_scr["Oss"])]

                    sweep(KT, QT, Vl, rK, rQ, rV, 0, NQS, blocks, odst_s)
                jobs.append((ld, sw))
        nxt = jobs[0][0]()
        for ji, (ld, sw) in enumerate(jobs):
            cur_ld = nxt
            if ji + 1 < len(jobs):
                nxt = jobs[ji + 1][0]()
            sw(cur_ld)
    for f_ in pending:
        f_()
    del pending[:]
    P.barrier()
    AR.release(mB)

    GC = 2
    wob = AR.alloc([128, MC, D], BF16, "wob")
    wgb = AR.alloc([128, KD, DFF], BF16, "wgb")
    wub = AR.alloc([128, KD, DFF], BF16, "wub")
    wdb = AR.alloc([128, FFC, D], BF16, "wdb")
    R_wo, R_wg, R_wu, R_wd = Res("wo"), Res("wg"), Res("wu"), Res("wd")
    for (dst, src, nk, rw_, rs_) in ((wob, wos, MC, R_wo, R_wsc["wo"]), (wgb, wgs, KD, R_wg, R_wsc["wg"]),
                                     (wub, wus, KD, R_wu, R_wsc["wu"]), (wdb, wds, FFC, R_wd, R_wsc["wd"])):
        sv = src.rearrange("(k p) c -> p k c", p=128)
        for k0 in range(0, nk, 4):
            k1 = min(nk, k0 + 4)
            P.dma("sp", dst[:, k0:k1, :], sv[:, k0:k1, :], (rs_,), (rw_,))
    gp2 = AR.alloc([128, D], F32, "gp2"); gp5 = AR.alloc([128, D], F32, "gp5")
    gfin = AR.alloc([128, D], F32, "gfin")
    R_g = Res("gates")
    P.dma("sp", gp2, adas[0, 2, :].partition_broadcast(128), (R_scr["adas"],), (R_g,))
    P.dma("sp", gp5, adas[0, 5, :].partition_broadcast(128), (R_scr["adas"],), (R_g,))
    P.dma("sp", gfin, g_final.partition_broadcast(128), (R_in,), (R_g,))
    xc = AR.alloc([128, GC, D], F32, "xc")
    ocy = AR.alloc([128, GC, max(D, MIX)], F32, "ocy")
    oc = ocy[:, :, 0:MIX]
    yc = ocy[:, :, 0:D]
    nb_ = AR.alloc([128, GC, max(D, MIX)], BF16, "nb_")
    fT = AR.alloc([128, max(KD, MC), GC * 128], BF16, "fT")
    actT = AR.alloc([128, FFC, GC * 128], BF16, "actT")
    tsq = AR.alloc([128, max(D, MIX)], BF16, "tsq")
    ssC = AR.alloc([128, 8], F32, "ssC")
    tmpf = AR.alloc([128, 512], F32, "tmpf")
    sg = AR.alloc([128, GC * 128], F32, "sg")
    R_xc, R_oc, R_nb, R_fT, R_actT, R_tmpf, R_sg = (Res("xc"), Res("oc", stg=True), Res("nb"), Res("fT"),
                                                   Res("actT"), Res("tmpf"), Res("sg"))
    R_yc = R_oc

    def fvec_ffn(cnd, k):
        return fffn[:, 0, cnd, k:k + 1], fffn[:, 1, cnd, k:k + 1]

    def post_group(xsrc, osrc, ydst, G, conds, g2, g5):
        N = G * 128
        P.dma("sp", xc[:, 0:G, :], xsrc.rearrange("(g p) d -> p g d", p=128), (R_in,), (R_xc,))
        P.dma("sp", oc[:, 0:G, :], osrc.rearrange("(g p) d -> p g d", p=128), (R_scr["Os"], R_scr["Oss"]), (R_oc,))
        for s in range(2):
            norm_to_fm(oc[:, :, s * W:(s + 1) * W], G, W, tsq, ssC, nb_[:, :, 0:W],
                       fT[:, s * WC:(s + 1) * WC, :], [(0, 128, 0)],
                       lambda cnd, k, s=s: (goTt[:, s * WC + k:s * WC + k + 1], None),
                       R_oc, R_nb, R_fT, WC, (0, 1))
        for g in range(G):
            for nb2 in range(D // 512):
                cs = slice(nb2 * 512, (nb2 + 1) * 512)
                bi = 2 + ((g * (D // 512) + nb2) % 2)
                bk = bank(bi)
                for c in range(MC):
                    P.mm(bk, fT[:, c, g * 128:(g + 1) * 128], wob[:, c, cs], c == 0, c == MC - 1, (R_fT, R_wo), (RB[bi],))
                P.tt("dve", tmpf, bk, g2[:, cs], ALU.mult, (RB[bi], R_g), (R_tmpf,))
                P.tt("dve", xc[:, g, cs], xc[:, g, cs], tmpf, ALU.add, (R_xc, R_tmpf), (R_xc,))
        norm_to_fm(xc, G, D, tsq, ssC, nb_[:, :, 0:D], fT[:, 0:KD, :], conds, fvec_ffn, R_xc, R_nb, R_fT, KD, (0, 1))
        for fc in range(FFC):
            bg, bu = 4 + (fc % 2), 6 + (fc % 2)
            for k in range(KD):
                P.mm(bank(bg)[:, 0:N], wgb[:, k, fc * 128:(fc + 1) * 128], fT[:, k, 0:N], k == 0, k == KD - 1,
                     (R_wg, R_fT), (RB[bg],))
            for k in range(KD):
                P.mm(bank(bu)[:, 0:N], wub[:, k, fc * 128:(fc + 1) * 128], fT[:, k, 0:N], k == 0, k == KD - 1,
                     (R_wu, R_fT), (RB[bu],))
            P.act(sg[:, 0:N], bank(bg)[:, 0:N], AF.Silu, (RB[bg],), (R_sg,))
            P.tt("dve", actT[:, fc, 0:N], sg[:, 0:N], bank(bu)[:, 0:N], ALU.mult, (R_sg, RB[bu]), (R_actT,))
        for g in range(G):
            for nb2 in range(D // 512):
                cs = slice(nb2 * 512, (nb2 + 1) * 512)
                bi = 2 + ((g * (D // 512) + nb2) % 2)
                bk = bank(bi)
                for fc in range(FFC):
                    P.mm(bk, actT[:, fc, g * 128:(g + 1) * 128], wdb[:, fc, cs], fc == 0, fc == FFC - 1,
                         (R_actT, R_wd), (RB[bi],))
                P.tt("dve", tmpf, bk, g5[:, cs], ALU.mult, (RB[bi], R_g), (R_tmpf,))
                P.tt("dve", xc[:, g, cs], xc[:, g, cs], tmpf, ALU.add, (R_xc, R_tmpf), (R_xc,))
        for g in range(G):
            P.memset("dve", ssC[:, g:g + 1], 0.0, (R_nb,))
            P.act(tsq[:, 0:D], xc[:, g, :], AF.Square, (R_xc, R_nb), (R_nb,), accum_out=ssC[:, g:g + 1])
        P.ts("dve", ssC[:, 0:G], ssC[:, 0:G], 1.0 / D, EPS, ALU.mult, ALU.add, (R_nb,), (R_nb,))
        P.act(ssC[:, 0:G], ssC[:, 0:G], AF.Ln, (R_nb,), (R_nb,))
        P.act(ssC[:, 0:G], ssC[:, 0:G], AF.Exp, (R_nb,), (R_nb,), scale=-0.5)
        for g in range(G):
            P.stt("dve", yc[:, g, :], xc[:, g, :], ssC[:, g:g + 1], gfin, ALU.mult, ALU.mult, (R_xc, R_nb, R_g), (R_yc,))
        P.dma("sp", ydst.rearrange("(g p) d -> p g d", p=128), yc[:, 0:G, :], (R_yc,), (R_out,))

    for gi in range(TO // (GC * 128)):
        rs = slice(gi * GC * 128, (gi + 1) * GC * 128)
        post_group(xown[rs, :], Os[rs, :], y_own[rs, :], GC, [(0, 128, 0)], gp2, gp5)
    for sbi in range(2):
        P.dma("sp", gp2[sbi * 64:(sbi + 1) * 64, :], adas[1 + sbi, 2, :].partition_broadcast(64),
              (R_scr["adas"],), (R_g,))
        P.dma("sp", gp5[sbi * 64:(sbi + 1) * 64, :], adas[1 + sbi, 5, :].partition_broadcast(64),
              (R_scr["adas"],), (R_g,))
    post_group(xsam, Oss, y_sam, 1, [(0, 64, 1), (64, 128, 2)], gp2, gp5)
    P.barrier()

    print("arena high-water", AR.hw, "of", ARENA_BYTES, "ops", {e: len(P.q[e]) for e in ENGS}, "sems", P.nsem, flush=True)
    blk = es.enter_context(nc.Block())
    P.emit(blk)
    es.close()
    return nc


def make_consts():
    c = np.zeros((128, 640), np.float32)
    s = np.arange(128)[:, None]
    j = np.arange(128)[None, :]
    c[:, 0:128] = (s == j)
    c[:, 128:256] = -(s >= j).astype(np.float32)
    c[:, 256:384] = -(s < j).astype(np.float32)
    return c


def make_masks(r):
    m = np.zeros((128, 2, 16, 512), np.float32)
    j = np.arange(128)[:, None]
    for kbz in range(16):
        for i in range(4):
            key = kbz * 128 + j
            q = (4 * i + r) * 128 + np.arange(128)[None, :]
            m[:, 0, kbz, i * 128:(i + 1) * 128] = np.where(key < q, 0.0, NEG)
            m[:, 1, kbz, i * 128:(i + 1) * 128] = np.where(key <= q, 0.0, NEG)
    sm = np.zeros((128, 2, 64), np.float32)
    q = np.arange(64)[None, :]
    sm[:, 0, :] = np.where((j < q) & (j < 64), 0.0, NEG)
    sm[:, 1, :] = np.where((j <= q) & (j < 64), 0.0, NEG)
    return m.reshape(128, -1), sm.reshape(128, -1)


_NC_CACHE = {}


def run(cfg, inputs):
    D, NH, T, DFF, PAST, DT = cfg["D"], cfg["NH"], cfg["T"], cfg["DFF"], cfg["PAST"], cfg["DT"]
    W = NH * 64
    KD = D // 128
    key = tuple(sorted(cfg.items()))
    nc = build(cfg)
    f = lambda a: np.ascontiguousarray(np.asarray(a, dtype=np.float32))
    xp, xs = f(inputs["x_prompt"]), f(inputs["x_sample"])
    cp, cs = f(inputs["c_prompt"]), f(inputs["c_sample"])
    NSUB = T // 128
    in_maps = []
    cst = make_consts()
    for c in range(8):
        b, r = c // 4, c % 4
        own = np.arange(r, NSUB, 4)
        xo = xp[b].reshape(NSUB, 128, D)[own].reshape(-1, D)
        crows = np.stack([cp[b], cs[2 * c], cs[2 * c + 1]], axis=1)
        cTm = crows.reshape(KD, 128, 3).transpose(1, 0, 2).reshape(128, KD * 3)
        mk, smk = make_masks(r)
        selv = np.zeros((128, 4), np.float32); selv[:, r] = 1.0
        go = np.concatenate([f(inputs["g_sb_out"])[0], f(inputs["g_fox_out"])[0]])
        goT = go.reshape(-1, 128).T
        m = {
            "xfull": xp[b], "xown": xo, "xsam": xs[2 * c:2 * c + 2].reshape(128, D), "cT": cTm,
            "csk": f(inputs["cache_sb_k"])[0, 2 * c:2 * c + 2].reshape(2, PAST, W),
            "csv": f(inputs["cache_sb_v"])[0, 2 * c:2 * c + 2].reshape(2, PAST, W),
            "cfk": f(inputs["cache_fox_k"])[0, 2 * c:2 * c + 2].reshape(2, PAST, W),
            "cfv": f(inputs["cache_fox_v"])[0, 2 * c:2 * c + 2].reshape(2, PAST, W),
            "clfT": f(inputs["cache_fox_logf"])[0, 2 * c:2 * c + 2].transpose(2, 0, 1).reshape(NH, 2 * PAST),
            "w_ada": f(inputs["w_ada"])[0], "b_ada": f(inputs["b_ada"])[0], "w_in": f(inputs["w_in"])[0],
            "w_o": f(inputs["w_o"])[0], "w_gate": f(inputs["w_gate"])[0], "w_up": f(inputs["w_up"])[0],
            "w_down": f(inputs["w_down"])[0], "g_mix": f(inputs["g_mix"])[0], "g_ffn": f(inputs["g_ffn"])[0],
            "g_final": f(inputs["g_final"]), "goT": goT, "b_f": f(inputs["b_f"])[0],
            "cst": cst, "masks": mk, "smask": smk, "sel": selv,
        }
        in_maps.append({k: np.ascontiguousarray(v, dtype=np.float32) for k, v in m.items()})
    if cfg.get("_prep_only"):
        return nc, in_maps
    res = run_bass_kernel_spmd(nc, in_maps, core_ids=list(range(8)))
    R = res.results
    yp = np.zeros((2, T, D), np.float32)
    outs_p = {k: np.zeros((1, 2, T, NH, 64), np.float32) for k in ("o_sbk", "o_sbv", "o_fxk", "o_fxv")}
    lfp = np.zeros((1, 2, T, NH), np.float32)
    ysm = np.zeros((16, DT, D), np.float32)
    outs_s = {k: np.zeros((1, 16, DT, NH, 64), np.float32) for k in ("s_sbk", "s_sbv", "s_fxk", "s_fxv")}
    lfs = np.zeros((1, 16, DT, NH), np.float32)
    for c in range(8):
        b, r = c // 4, c % 4
        own = np.arange(r, NSUB, 4)
        yp[b].reshape(NSUB, 128, D)[own] = R[c]["y_own"].reshape(-1, 128, D)
        for k in outs_p:
            outs_p[k][0, b].reshape(NSUB, 128, NH, 64)[own] = R[c][k].reshape(-1, 128, NH, 64)
        lfp[0, b].reshape(NSUB, 128, NH)[own] = R[c]["o_lf"].reshape(-1, 128, NH)
        ysm[2 * c:2 * c + 2] = R[c]["y_sam"].reshape(2, DT, D)
        for k in outs_s:
            outs_s[k][0, 2 * c:2 * c + 2] = R[c][k].reshape(2, DT, NH, 64)
        lfs[0, 2 * c:2 * c + 2] = R[c]["s_lf"].reshape(2, DT, NH)
    return (yp, ysm, outs_p["o_sbk"], outs_p["o_sbv"], outs_p["o_fxk"], outs_p["o_fxv"], lfp,
            outs_s["s_sbk"], outs_s["s_sbv"], outs_s["s_fxk"], outs_s["s_fxv"], lfs)


def kernel(**inputs):
    return run(CFG_FULL, inputs)
```
